# Optimizing a Trainium2 kernel written in Bass

```python
import math
import jax
import jax.numpy as jnp
from jax import lax
import numpy as np

D_MODEL = 1024
BATCH = 4
SEQ = 4096
DEPTH = 4

CTX_LEN = 256
GRID_W = 64
N_BRANCH = 4
BRANCH_DIM = D_MODEL // 2
POOL_GROUPS = 4
POOL_GROUP_DIM = BRANCH_DIM // POOL_GROUPS
POOL_WINDOWS = (2, 4, 8, 16)
FOUR_GROUPS = 4
FOUR_GROUP_DIM = BRANCH_DIM // FOUR_GROUPS
SSD_HEADDIM = 64
SSD_HEADS = BRANCH_DIM // SSD_HEADDIM
SSD_STATE = 128
SSD_GROUPS = 2
SSD_CHUNK = 128
SSD_XBC = BRANCH_DIM + 2 * SSD_GROUPS * SSD_STATE
GDN_DK = 128
GDN_DV = 128
GDN_HEADS = BRANCH_DIM // GDN_DK
GDN_CHUNK = 64
SHORT_CONV = 5
PEER_HEADS = 8
PEER_NKEYS = 128
PEER_EXPERTS = PEER_NKEYS * PEER_NKEYS
PEER_DQ = 256
PEER_TOPK = 16
PEER_BLOCK = 128
DEEPNORM_ALPHA = (2 * DEPTH) ** 0.25
DEEPNORM_BETA = (8 * DEPTH) ** -0.25
LN_EPS = 1e-6
SPLIT_WIDTHS = (BRANCH_DIM, BRANCH_DIM, SSD_XBC, BRANCH_DIM, 2 * SSD_HEADS, 3 * BRANCH_DIM, BRANCH_DIM, 2 * GDN_HEADS, 2 * GDN_HEADS, N_BRANCH * D_MODEL)
N_IN = sum(SPLIT_WIDTHS)

kernel_name = 'hybrid_pool_ssd_deltanet_fourier_peer_dit'


def _layer_norm(x, g=None, b=None):
    xf = x.astype(jnp.float32)
    xc = xf - jnp.mean(xf, -1, keepdims=True)
    y = xc * lax.rsqrt(jnp.mean(xc * xc, -1, keepdims=True) + LN_EPS)
    if g is not None:
        y = y * g.astype(jnp.float32) + b.astype(jnp.float32)
    return y.astype(x.dtype)


def _modulate(x, shift, scale):
    return _layer_norm(x) * (1.0 + scale[:, None]) + shift[:, None]


def _rms(x):
    xf = x.astype(jnp.float32)
    return xf * lax.rsqrt(jnp.mean(xf * xf, -1, keepdims=True) + LN_EPS)


def _l2norm(x):
    return x * lax.rsqrt(jnp.sum(x * x, -1, keepdims=True) + 1e-6)


def _flip(a):
    return jnp.flip(a, axis=1)


def _split_proj(proj):
    offsets, acc = [], 0
    for w in SPLIT_WIDTHS[:-1]:
        acc += w
        offsets.append(acc)
    return jnp.split(proj, offsets, axis=-1)


def _dw_conv(x, w, b=None):
    k = w.shape[0]
    y = lax.conv_general_dilated(x, w[:, None, :], window_strides=(1,), padding=[(k // 2, k - 1 - k // 2)],
                                 dimension_numbers=('NWC', 'WIO', 'NWC'), feature_group_count=x.shape[-1])
    if b is not None:
        y = y + b
    return y


def _box_sum(x, win, axis):
    n = x.shape[axis]
    lo = win // 2
    hi = win - 1 - lo
    pad = [(0, 0)] * x.ndim
    pad[axis] = (lo + 1, hi)
    cs = jnp.cumsum(jnp.pad(x, pad), axis=axis)
    s = lax.slice_in_dim(cs, win, win + n, axis=axis) - lax.slice_in_dim(cs, 0, n, axis=axis)
    t = jnp.arange(n)
    cnt = (jnp.minimum(t + hi, n - 1) - jnp.maximum(t - lo, 0) + 1).astype(jnp.float32)
    return s, cnt


def _window_mean(u, win, rows):
    bsz, seq, ch = u.shape
    if rows is None:
        s, cnt = _box_sum(u, win, 1)
        return s / cnt[None, :, None]
    g = u.reshape(bsz, rows, GRID_W, ch)
    s, cnt_c = _box_sum(g, win, 2)
    s, cnt_r = _box_sum(s, win, 1)
    return (s / (cnt_r[:, None] * cnt_c[None, :])[None, :, :, None]).reshape(bsz, seq, ch)


def pool_branch(u, w_pool, s_pool, rows):
    bsz, seq, _ = u.shape
    ug = u.astype(jnp.float32).reshape(bsz, seq, POOL_GROUPS, POOL_GROUP_DIM)
    outs = []
    for gi, win in enumerate(POOL_WINDOWS):
        x_g = ug[:, :, gi]
        outs.append((_window_mean(x_g, win, rows) - x_g).astype(u.dtype) @ w_pool[gi])
    return jnp.concatenate(outs, axis=-1) * s_pool


def fourier_branch(u):
    bsz, seq, _ = u.shape
    ug = u.astype(jnp.float32).reshape(bsz, seq, FOUR_GROUPS, FOUR_GROUP_DIM)
    return jnp.fft.fft2(ug, axes=(1, 3), norm='ortho').real.reshape(bsz, seq, BRANCH_DIM).astype(u.dtype)


def ssd_scan(x, dt, a, bm, cm, s0):
    bsz, seq, nh, hp = x.shape
    ns = bm.shape[-1]
    q = SSD_CHUNK
    nc = seq // q
    la = (dt * a).reshape(bsz, nc, q, nh)
    xdt = (x * dt[..., None]).reshape(bsz, nc, q, nh, hp)
    bm = bm.reshape(bsz, nc, q, nh, ns)
    cm = cm.reshape(bsz, nc, q, nh, ns)
    acs = jnp.cumsum(la, axis=2)
    lower = jnp.tril(jnp.ones((q, q), bool))[None, None, :, :, None]
    decay = jnp.exp(jnp.where(lower, acs[:, :, :, None, :] - acs[:, :, None, :, :], -jnp.inf))
    scores = jnp.einsum('bcihn,bcjhn->bcijh', cm, bm) * decay
    y_diag = jnp.einsum('bcijh,bcjhp->bcihp', scores, xdt)
    last = acs[:, :, -1]
    w_state = jnp.exp(last[:, :, None] - acs)
    chunk_states = jnp.einsum('bcjhn,bcjhp->bchpn', bm, xdt * w_state[..., None])

    def step(s, inp):
        st, dl = inp
        return s * jnp.exp(dl)[:, :, None, None] + st, s

    s_fin, s_in = lax.scan(step, s0, (jnp.moveaxis(chunk_states, 1, 0), jnp.moveaxis(last, 1, 0)))
    s_in = jnp.moveaxis(s_in, 0, 1)
    y_off = jnp.einsum('bcihn,bchpn->bcihp', cm, s_in) * jnp.exp(acs)[..., None]
    return (y_diag + y_off).reshape(bsz, seq, nh, hp), s_fin


def gdn_scan(q, k, v, g, beta, s0):
    bsz, seq, nh, _ = q.shape
    dv = v.shape[-1]
    cl = GDN_CHUNK
    nc = seq // cl

    def chunks(t):
        return jnp.swapaxes(t.reshape(bsz, nc, cl, nh, -1), 2, 3)

    q, k, v = chunks(q), chunks(k), chunks(v)
    g = jnp.swapaxes(g.reshape(bsz, nc, cl, nh), 2, 3)
    beta = jnp.swapaxes(beta.reshape(bsz, nc, cl, nh), 2, 3)
    gc = jnp.cumsum(g, axis=-1)
    lower = jnp.tril(jnp.ones((cl, cl), bool))
    strict = jnp.tril(jnp.ones((cl, cl), bool), -1)
    decay = jnp.exp(jnp.where(lower, gc[..., :, None] - gc[..., None, :], -jnp.inf))
    kk = jnp.einsum('bchid,bchjd->bchij', k, k)
    a_mat = jnp.where(strict, kk * decay * beta[..., :, None], 0.0)
    ia = a_mat + jnp.eye(cl, dtype=jnp.float32)
    u = lax.linalg.triangular_solve(ia, v * beta[..., None], left_side=True, lower=True, unit_diagonal=True)
    w = lax.linalg.triangular_solve(ia, k * (beta * jnp.exp(gc))[..., None], left_side=True, lower=True, unit_diagonal=True)
    attn = jnp.einsum('bchid,bchjd->bchij', q, k) * decay
    qg = q * jnp.exp(gc)[..., None]
    glast = gc[..., -1]
    kdec = k * jnp.exp(glast[..., None] - gc)[..., None]

    def step(s, inp):
        u_c, w_c, at_c, qg_c, kd_c, gl_c = inp
        v_new = u_c - jnp.einsum('bhtk,bhkv->bhtv', w_c, s)
        o = jnp.einsum('bhtk,bhkv->bhtv', qg_c, s) + jnp.einsum('bhts,bhsv->bhtv', at_c, v_new)
        s = s * jnp.exp(gl_c)[..., None, None] + jnp.einsum('bhtk,bhtv->bhkv', kd_c, v_new)
        return s, o

    xs = tuple(jnp.moveaxis(t, 1, 0) for t in (u, w, attn, qg, kdec, glast))
    s_fin, o = lax.scan(step, s0, xs)
    o = jnp.swapaxes(jnp.moveaxis(o, 0, 1), 2, 3).reshape(bsz, seq, nh, dv)
    return o, s_fin


def recurrent_branches(parts, lp, init):
    f32 = jnp.float32
    ssd_xbc, ssd_dt, gdn_qkv, gdn_beta, gdn_a = parts[2], parts[4], parts[5], parts[7], parts[8]
    bsz, seq, _ = ssd_xbc.shape
    xbc = jax.nn.silu(_dw_conv(ssd_xbc, lp['ssd_conv_w'], lp['ssd_conv_b'])).astype(f32)
    sx, sb, sc = jnp.split(xbc, [BRANCH_DIM, BRANCH_DIM + SSD_GROUPS * SSD_STATE], axis=-1)
    sx = sx.reshape(bsz, seq, SSD_HEADS, SSD_HEADDIM)
    rep = SSD_HEADS // SSD_GROUPS
    sb = jnp.repeat(sb.reshape(bsz, seq, SSD_GROUPS, SSD_STATE), rep, axis=2)
    sc = jnp.repeat(sc.reshape(bsz, seq, SSD_GROUPS, SSD_STATE), rep, axis=2)
    dt = jax.nn.softplus(ssd_dt.astype(f32).reshape(bsz, seq, 2, SSD_HEADS) + lp['ssd_dt_bias'].astype(f32))
    a = -jnp.exp(lp['ssd_a_log'].astype(f32))
    y_f, s_f = ssd_scan(sx, dt[:, :, 0], a[0], sb, sc, init[0])
    y_b, s_b = ssd_scan(_flip(sx), _flip(dt[:, :, 1]), a[1], _flip(sb), _flip(sc), init[1])
    ssd_y = y_f + _flip(y_b) + lp['ssd_d'].astype(f32)[:, None] * sx
    qkv = jax.nn.silu(_dw_conv(gdn_qkv, lp['gdn_conv_w'])).astype(f32)
    q, k, v = (t.reshape(bsz, seq, GDN_HEADS, -1) for t in jnp.split(qkv, 3, axis=-1))
    q = _l2norm(q) * GDN_DK ** -0.5
    k = _l2norm(k)
    beta = jax.nn.sigmoid(gdn_beta.astype(f32).reshape(bsz, seq, 2, GDN_HEADS))
    g = -jnp.exp(lp['gdn_a_log'].astype(f32)) * jax.nn.softplus(
        gdn_a.astype(f32).reshape(bsz, seq, 2, GDN_HEADS) + lp['gdn_dt_bias'].astype(f32))
    o_f, gs_f = gdn_scan(q, k, v, g[:, :, 0], beta[:, :, 0], init[2])
    o_b, gs_b = gdn_scan(_flip(q), _flip(k), _flip(v), _flip(g[:, :, 1]), _flip(beta[:, :, 1]), init[3])
    gdn_o = o_f + _flip(o_b)
    return ssd_y, gdn_o, (s_f, s_b, gs_f, gs_b)


def token_mixer(h, lp, init, rows):
    bsz, seq, _ = h.shape
    f32 = jnp.float32
    parts = _split_proj(h @ lp['w_in'])
    pool_in, four_in, ssd_z, gdn_z, gate_in = parts[0], parts[1], parts[3], parts[6], parts[9]
    ssd_y, gdn_o, states = recurrent_branches(parts, lp, init)
    pool_out = pool_branch(pool_in, lp['pool_w'], lp['pool_scale'], rows)
    four_out = fourier_branch(four_in)
    ssd_g = ssd_y.reshape(bsz, seq, BRANCH_DIM) * jax.nn.silu(ssd_z.astype(f32))
    ssd_out = (_rms(ssd_g.reshape(bsz, seq, SSD_GROUPS, -1)).reshape(bsz, seq, BRANCH_DIM)
               * lp['ssd_norm_w'].astype(f32)).astype(h.dtype)
    gdn_gate = jax.nn.silu(gdn_z.astype(f32).reshape(bsz, seq, GDN_HEADS, GDN_DV))
    gdn_out = (_rms(gdn_o) * lp['gdn_norm_w'].astype(f32) * gdn_gate).reshape(bsz, seq, BRANCH_DIM).astype(h.dtype)
    gates = jax.nn.sigmoid(gate_in.reshape(bsz, seq, N_BRANCH, D_MODEL))
    branches = (pool_out, four_out, ssd_out, gdn_out)
    merged = sum(gates[:, :, i] * (br @ lp['w_branch'][i]) for i, br in enumerate(branches))
    return merged @ lp['w_out'], states


def context_states(h, lp, init):
    parts = _split_proj(h @ lp['w_in'])
    return recurrent_branches(parts, lp, init)[2]


def peer_ffn(h, wq, sub_keys, u_tab, v_tab):
    bsz, seq, dm = h.shape
    blocks = h.reshape(-1, PEER_BLOCK, dm)

    def one_block(hb):
        t = hb.shape[0]
        q = (hb @ wq).reshape(t, PEER_HEADS, 2, PEER_DQ // 2)
        s = jnp.einsum('thsd,hskd->thsk', q, sub_keys).astype(jnp.float32)
        v1, i1 = lax.top_k(s[:, :, 0], PEER_TOPK)
        v2, i2 = lax.top_k(s[:, :, 1], PEER_TOPK)
        cand = (v1[..., :, None] + v2[..., None, :]).reshape(t, PEER_HEADS, PEER_TOPK * PEER_TOPK)
        cid = (i1[..., :, None] * PEER_NKEYS + i2[..., None, :]).reshape(t, PEER_HEADS, PEER_TOPK * PEER_TOPK)
        best, pos = lax.top_k(cand, PEER_TOPK)
        eid = jnp.take_along_axis(cid, pos, axis=-1)
        gate = jax.nn.softmax(best, axis=-1)
        act = jax.nn.gelu(jnp.einsum('td,thkd->thk', hb, u_tab[eid]).astype(jnp.float32), approximate=False)
        return jnp.einsum('thk,thkd->td', (gate * act).astype(hb.dtype), v_tab[eid])

    return lax.map(one_block, blocks).reshape(bsz, seq, dm)


def setup_inputs(seed: int = 0) -> dict:
    key = jax.random.key(seed)
    keys = jax.random.split(key, 32)
    f32 = jnp.float32
    dm = D_MODEL
    nl = DEPTH

    def nrm(i, shape, scale):
        return jax.random.normal(keys[i], shape, f32) * scale

    def dt_bias(i, shape):
        dt = jnp.exp(jax.random.uniform(keys[i], shape, f32, math.log(1e-3), math.log(1e-1)))
        return dt + jnp.log(-jnp.expm1(-dt))

    def a_log(i, shape):
        return jnp.log(jax.random.uniform(keys[i], shape, f32, 1.0, 16.0))

    return {
        'x': nrm(0, (BATCH, SEQ, dm), 1.0),
        'c': nrm(1, (BATCH, dm), 1.0),
        'ctx': nrm(2, (BATCH, CTX_LEN, dm), 1.0),
        'c_ctx': nrm(3, (dm,), 1.0),
        'w_mod': nrm(4, (nl, dm, 6 * dm), 0.5 * dm ** -0.5),
        'b_mod': nrm(5, (nl, 6 * dm), 0.02),
        'w_in': nrm(6, (nl, dm, N_IN), dm ** -0.5),
        'pool_w': nrm(7, (nl, POOL_GROUPS, POOL_GROUP_DIM, POOL_GROUP_DIM), POOL_GROUP_DIM ** -0.5),
        'pool_scale': 1.0 + nrm(8, (nl, BRANCH_DIM), 0.1),
        'ssd_conv_w': nrm(9, (nl, SHORT_CONV, SSD_XBC), SHORT_CONV ** -0.5),
        'ssd_conv_b': nrm(10, (nl, SSD_XBC), 0.02),
        'ssd_dt_bias': dt_bias(11, (nl, 2, SSD_HEADS)),
        'ssd_a_log': a_log(12, (nl, 2, SSD_HEADS)),
        'ssd_d': 1.0 + nrm(13, (nl, SSD_HEADS), 0.1),
        'ssd_norm_w': 1.0 + nrm(14, (nl, BRANCH_DIM), 0.05),
        'gdn_conv_w': nrm(15, (nl, SHORT_CONV, 3 * BRANCH_DIM), SHORT_CONV ** -0.5),
        'gdn_dt_bias': dt_bias(16, (nl, 2, GDN_HEADS)),
        'gdn_a_log': a_log(17, (nl, 2, GDN_HEADS)),
        'gdn_norm_w': 1.0 + nrm(18, (nl, GDN_DV), 0.05),
        'w_branch': nrm(19, (nl, N_BRANCH, BRANCH_DIM, dm), DEEPNORM_BETA * BRANCH_DIM ** -0.5),
        'w_out': nrm(20, (nl, dm, dm), DEEPNORM_BETA * dm ** -0.5),
        'ln1_g': 1.0 + nrm(21, (nl, dm), 0.05),
        'ln1_b': nrm(22, (nl, dm), 0.02),
        'peer_wq': nrm(23, (nl, dm, PEER_HEADS * PEER_DQ), dm ** -0.5),
        'peer_keys': nrm(24, (nl, PEER_HEADS, 2, PEER_NKEYS, PEER_DQ // 2), (PEER_DQ // 2) ** -0.5),
        'peer_u': nrm(25, (nl, PEER_EXPERTS, dm), dm ** -0.5),
        'peer_v': nrm(26, (nl, PEER_EXPERTS, dm), DEEPNORM_BETA),
        'ln2_g': 1.0 + nrm(27, (nl, dm), 0.05),
        'ln2_b': nrm(28, (nl, dm), 0.02),
    }


def reference(x, c, ctx, c_ctx, w_mod, b_mod, w_in, pool_w, pool_scale, ssd_conv_w, ssd_conv_b, ssd_dt_bias,
              ssd_a_log, ssd_d, ssd_norm_w, gdn_conv_w, gdn_dt_bias, gdn_a_log, gdn_norm_w, w_branch, w_out,
              ln1_g, ln1_b, peer_wq, peer_keys, peer_u, peer_v, ln2_g, ln2_b):
    bsz = x.shape[0]
    rows = x.shape[1] // GRID_W
    f32 = jnp.float32
    zero_states = ((jnp.zeros((bsz, SSD_HEADS, SSD_HEADDIM, SSD_STATE), f32),) * 2
                   + (jnp.zeros((bsz, GDN_HEADS, GDN_DK, GDN_DV), f32),) * 2)
    silu_c = jax.nn.silu(c)
    silu_cc = jax.nn.silu(c_ctx)[None]
    for l in range(DEPTH):
        lp = {
            'w_in': w_in[l], 'pool_w': pool_w[l], 'pool_scale': pool_scale[l],
            'ssd_conv_w': ssd_conv_w[l], 'ssd_conv_b': ssd_conv_b[l], 'ssd_dt_bias': ssd_dt_bias[l],
            'ssd_a_log': ssd_a_log[l], 'ssd_d': ssd_d[l], 'ssd_norm_w': ssd_norm_w[l],
            'gdn_conv_w': gdn_conv_w[l], 'gdn_dt_bias': gdn_dt_bias[l], 'gdn_a_log': gdn_a_log[l],
            'gdn_norm_w': gdn_norm_w[l], 'w_branch': w_branch[l], 'w_out': w_out[l],
        }
        mod_x = jnp.split(silu_c @ w_mod[l] + b_mod[l], 6, axis=-1)
        mod_c = jnp.split(silu_cc @ w_mod[l] + b_mod[l], 6, axis=-1)
        h_c = _modulate(ctx, mod_c[0], mod_c[1])
        if l == DEPTH - 1:
            states = context_states(h_c, lp, zero_states)
        else:
            mix_c, states = token_mixer(h_c, lp, zero_states, None)
            ctx = _layer_norm(DEEPNORM_ALPHA * ctx + mod_c[2][:, None] * mix_c, ln1_g[l], ln1_b[l])
            h_c = _modulate(ctx, mod_c[3], mod_c[4])
            ffn_c = peer_ffn(h_c, peer_wq[l], peer_keys[l], peer_u[l], peer_v[l])
            ctx = _layer_norm(DEEPNORM_ALPHA * ctx + mod_c[5][:, None] * ffn_c, ln2_g[l], ln2_b[l])
        h_x = _modulate(x, mod_x[0], mod_x[1])
        mix_x, _ = token_mixer(h_x, lp, states, rows)
        x = _layer_norm(DEEPNORM_ALPHA * x + mod_x[2][:, None] * mix_x, ln1_g[l], ln1_b[l])
        h_x = _modulate(x, mod_x[3], mod_x[4])
        ffn_x = peer_ffn(h_x, peer_wq[l], peer_keys[l], peer_u[l], peer_v[l])
        x = _layer_norm(DEEPNORM_ALPHA * x + mod_x[5][:, None] * ffn_x, ln2_g[l], ln2_b[l])
    return x
```

```python
import numpy as np
from contextlib import ExitStack
import concourse.bass as bass
import concourse.mybir as mybir
from concourse.bass_utils import run_bass_kernel_spmd

F32 = mybir.dt.float32
BF16 = mybir.dt.bfloat16
U32 = mybir.dt.uint32
AF = mybir.ActivationFunctionType
ALU = mybir.AluOpType
AX = mybir.AxisListType

NDS = 64
NHW = 16


class KB:
    def __init__(self, nc, es):
        self.nc, self.es = nc, es
        self.engs = {'pe': nc.tensor, 'act': nc.scalar, 'dve': nc.vector, 'pool': nc.gpsimd, 'sp': nc.sync}
        self.sem, self.cnt = {}, {}
        for e in ('pe', 'act', 'dve', 'pool'):
            self.sem[('e', e)] = es.enter_context(nc.semaphore('s_' + e))
            self.cnt[e] = 0
        self.dcnt = [0] * NDS
        for i in range(NDS):
            self.sem[('d', i)] = es.enter_context(nc.semaphore('d%d' % i))
        self.dnext = 0
        self.dnext_sw = 0
        self.seen = {e: {} for e in self.engs}
        self.lastw, self.readers = {}, {}
        self.nbuf = 0

    def sb(self, shape, dt, name=None):
        self.nbuf += 1
        return self.es.enter_context(self.nc.sbuf_tensor('sb%d_' % self.nbuf + (name or 'b'), list(shape), dt))

    def ps(self, shape, dt, name=None):
        self.nbuf += 1
        return self.es.enter_context(self.nc.psum_tensor('ps%d_' % self.nbuf + (name or 'p'), list(shape), dt))

    def op(self, eng, fn, r=(), w=(), dma=False):
        deps = {}

        def need(tok):
            if tok is not None and deps.get(tok[0], 0) < tok[1]:
                deps[tok[0]] = tok[1]
        for key in r:
            need(self.lastw.get(key))
        for key in w:
            need(self.lastw.get(key))
            for k, v in self.readers.get(key, {}).items():
                need((k, v))
        E = self.engs[eng]
        if dma:
            if eng == 'pool':
                i = NHW + self.dnext_sw
                self.dnext_sw = (self.dnext_sw + 1) % (NDS - NHW)
            else:
                i = self.dnext
                self.dnext = (i + 1) % NHW
            if self.dcnt[i] > 0:
                need((('d', i), self.dcnt[i]))
        for k, v in deps.items():
            if eng == 'pe' and k == ('e', 'pe'):
                continue
            if self.seen[eng].get(k, 0) >= v:
                continue
            E.wait_ge(self.sem[k], v)
            self.seen[eng][k] = v
        inst = fn(E)
        if dma:
            self.dcnt[i] += 16
            inst.then_inc(self.sem[('d', i)], 16)
            tok = (('d', i), self.dcnt[i])
        else:
            self.cnt[eng] += 1
            inst.then_inc(self.sem[('e', eng)], 1)
            tok = (('e', eng), self.cnt[eng])
        for key in r:
            d = self.readers.setdefault(key, {})
            if d.get(tok[0], 0) < tok[1]:
                d[tok[0]] = tok[1]
        for key in w:
            self.lastw[key] = tok
            self.readers[key] = {}
        return tok

    def dma(self, out, in_, r=(), w=(), eng='sp', **kw):
        return self.op(eng, lambda E: E.dma_start(out=out, in_=in_, **kw), r=r, w=w, dma=True)

    def finish(self):
        E = self.engs['sp']
        for i in range(NDS):
            if self.dcnt[i] > 0 and self.seen['sp'].get(('d', i), 0) < self.dcnt[i]:
                E.wait_ge(self.sem[('d', i)], self.dcnt[i])
        for e in ('pe', 'act', 'dve', 'pool'):
            if self.cnt[e] > 0:
                E.wait_ge(self.sem[('e', e)], self.cnt[e])

    def barrier(self):
        for en, E in self.engs.items():
            for i in range(NDS):
                if self.dcnt[i] > 0 and self.seen[en].get(('d', i), 0) < self.dcnt[i]:
                    E.wait_ge(self.sem[('d', i)], self.dcnt[i])
                    self.seen[en][('d', i)] = self.dcnt[i]
            for e in ('pe', 'act', 'dve', 'pool'):
                if e != en and self.cnt[e] > 0 and self.seen[en].get(('e', e), 0) < self.cnt[e]:
                    E.wait_ge(self.sem[('e', e)], self.cnt[e])
                    self.seen[en][('e', e)] = self.cnt[e]
        self.lastw, self.readers = {}, {}


    def collective(self, kind, ins, outs):
        E = self.engs['pool']
        if not hasattr(self, 'ccsem'):
            self.ccsem = self.es_top.enter_context(self.nc.semaphore('cc_sem'))
            self.sem[('c', 0)] = self.ccsem
            self.cccnt = 0
        self.barrier()
        for i_, o_ in zip(ins, outs):
            E.collective_compute(kind, ALU.bypass, replica_groups=RG_PAIRS, ins=[i_], outs=[o_]).then_inc(self.ccsem)
            self.cccnt += 1
        for en, EE in self.engs.items():
            EE.wait_ge(self.ccsem, self.cccnt)
            self.seen[en][('c', 0)] = self.cccnt


D = 1024
ALPHA = 8 ** 0.25
EPS = 1e-6
NTOK = 2176
NEG = -1.0e30


def ln_normalize(kb, xin, xkey, hn, hnkey, st, mv, rstd):
    kb.op('dve', lambda E: E.bn_stats(out=st[:, 0, :], in_=xin[:, 0:512]), r=[xkey], w=['st0'])
    kb.op('dve', lambda E: E.bn_stats(out=st[:, 1, :], in_=xin[:, 512:1024]), r=[xkey], w=['st1'])
    kb.op('dve', lambda E: E.bn_aggr(out=mv[:, :], in_=st[:, :, :]), r=['st0', 'st1'], w=['mv'])
    kb.op('dve', lambda E: E.tensor_scalar(out=rstd[:, :], in0=mv[:, 1:2], scalar1=EPS, scalar2=None,
                                           op0=ALU.add), r=['mv'], w=['rstd'])
    kb.op('act', lambda E: E.activation(out=rstd[:, :], in_=rstd[:, :], func=AF.Sqrt), r=['rstd'], w=['rstd'])
    kb.op('dve', lambda E: E.reciprocal(out=rstd[:, :], in_=rstd[:, :]), r=['rstd'], w=['rstd'])
    kb.op('dve', lambda E: E.tensor_scalar(out=hn, in0=xin, scalar1=mv[:, 0:1], scalar2=rstd[:, 0:1],
                                           op0=ALU.subtract, op1=ALU.mult), r=[xkey, 'mv', 'rstd'], w=[hnkey])


def emit_B(nc, kb, l, C, x_d, G_d, out_d):
    es = ExitStack()
    kb.es = es
    dt = lambda n, s, d=F32, kind="ExternalInput": nc.dram_tensor(n + "_%d" % l, list(s), d, kind=kind).ap()
    cvec_d = C['cvec']
    wmod_d = C['w_mod_%d' % l]
    bmod_d = C['b_mod_%d' % l]
    wg_d = dt("wg", [8, D, 512])
    wb_d = dt("wb", [2048, D])
    wo_d = dt("wo", [D, D])
    lnp_d = dt("lnp", [4, D])
    wq_d = dt("wq", [D, 2048])
    keysT_d = dt("keysT", [128, 16, 128])
    pub_d, pvb_d = C['pub'], C['pvb']
    ident_d, iota_d, x1_d, idx_d = C['ident'], C['iota16'], C['x1'], C['idxB']
    idxb = kb.sb([128, 17, 16], U32, "idxb")
    kb.dma(idxb[:, :, :], idx_d[:, :, :], w=['idxb'])

    ident_f = kb.sb([128, 128], F32, "ident_f")
    ident_b = kb.sb([128, 128], BF16, "ident_b")
    ones_b = kb.sb([128, 128], F32, "ones_f")
    iota16 = kb.sb([128, 16], F32, "iota16")
    rows = kb.sb([128, 2, 6, D], F32, "modrows")
    lnrow = kb.sb([128, 4, D], F32, "lnrow")
    es0 = ExitStack()
    kb.es = es0
    cs = kb.sb([128, 2, 8], F32, "cs")
    csrep = kb.sb([128, 2, 8, 128], BF16, "csrep")
    bmrow = kb.sb([128, 512], F32, "bmrow")
    wmb = [kb.sb([128, 8, 512], BF16, "wmb%d" % i) for i in range(2)]
    pmod = [kb.ps([128, 512], F32, "pmod%d" % i) for i in range(2)]

    kb.dma(ident_f[:, :], ident_d[:, :], w=['ident_f'])
    kb.dma(iota16[:, :], iota_d[:, :], w=['iota16'])
    kb.op('dve', lambda E: E.tensor_copy(out=ident_b[:, :], in_=ident_f[:, :]), r=['ident_f'], w=['ident_b'])
    kb.op('pool', lambda E: E.memset(ones_b[:, :], 1.0), w=['ones'])
    for v in range(2):
        kb.dma(cs[:, v, :], cvec_d[v, :, :], w=['cs%d' % v])
        kb.op('act', lambda E: E.activation(out=cs[:, v, :], in_=cs[:, v, :], func=AF.Silu), r=['cs%d' % v], w=['cs%d' % v])
        for k in range(8):
            kb.op('dve', lambda E: E.tensor_scalar(out=csrep[:, v, k, :], in0=ones_b[:, :], scalar1=cs[:, v, k:k + 1],
                                                   scalar2=None, op0=ALU.mult), r=['cs%d' % v, 'ones'], w=['csrep%d' % v])
    for j in range(4):
        kb.dma(lnrow[:, j, :], lnp_d[j, :].partition_broadcast(128), w=['lnrow'])
    ci = 0
    for j in range(6):
        for hf in range(2):
            c0 = j * D + hf * 512
            wbuf = wmb[ci % 2]
            wk = 'wmb%d' % (ci % 2)
            kb.dma(wbuf[:, :, :], wmod_d[:, c0:c0 + 512].rearrange("(k p) n -> p k n", p=128), w=[wk], eng='pool')
            kb.dma(bmrow[:, :], bmod_d[c0:c0 + 512].partition_broadcast(128), w=['bmrow'])
            for v in range(2):
                pk = 'pmod%d' % v
                for k in range(8):
                    kb.op('pe', lambda E: E.matmul(pmod[v][:, :], lhsT=csrep[:, v, k, :], rhs=wbuf[:, k, :],
                                                   start=(k == 0), stop=(k == 7)), r=['csrep%d' % v, wk], w=[pk])
                if j in (1, 4):
                    kb.op('dve', lambda E: E.scalar_tensor_tensor(out=rows[:, v, j, hf * 512:(hf + 1) * 512], in0=pmod[v][:, :],
                                                                  scalar=1.0, in1=bmrow[:, :], op0=ALU.add, op1=ALU.add),
                          r=[pk, 'bmrow'], w=['rows'])
                else:
                    kb.op('dve', lambda E: E.tensor_tensor(out=rows[:, v, j, hf * 512:(hf + 1) * 512], in0=pmod[v][:, :],
                                                           in1=bmrow[:, :], op=ALU.add), r=[pk, 'bmrow'], w=['rows'])
            ci += 1
    kb.barrier()
    es0.close()

    es1 = ExitStack()
    kb1 = kb
    kb.es = es1
    wb = kb.sb([128, 16, D], BF16, "wb")
    wo = kb.sb([128, 8, D], BF16, "wo")
    wg = [kb.sb([128, 8, 512], BF16, "wg%d" % i) for i in range(2)]
    xb = kb.sb([128, 4, D], F32, "xb")
    hn = kb.sb([128, D], F32, "hn")
    t1 = kb.sb([128, D], F32, "t1")
    hb = kb.sb([128, D], BF16, "hb")
    hT = kb.sb([128, 8, 512], BF16, "hT")
    brTb = kb.sb([128, 16, 512], BF16, "brTb")
    gs = kb.sb([128, 4, 512], BF16, "gs")
    tb = kb.sb([128, 4, 512], F32, "tb")
    mT = kb.sb([128, 8, 512], BF16, "mT")
    st = kb.sb([128, 2, 6], F32, "st")
    mv = kb.sb([128, 2], F32, "mv")
    rstd = kb.sb([128, 1], F32, "rstd")
    pT = kb.ps([128, 8, 128], BF16, "pT")
    pG = [kb.ps([128, 512], F32, "pG%d" % i) for i in range(2)]
    pP = [kb.ps([128, 512], F32, "pP%d" % i) for i in range(2)]
    pM = kb.ps([128, D], F32, "pM")

    kb.dma(wb[:, :, :], wb_d.rearrange("(k p) n -> p k n", p=128), w=['wb'], eng='pool')
    kb.dma(wo[:, :, :], wo_d.rearrange("(k p) n -> p k n", p=128), w=['wo'], eng='pool')

    blocks = [(0, 128, 0)] + [(128 + 512 * i, 512, 1) for i in range(4)]
    wgi = 0
    for (t0, NT, v) in blocks:
        nt = NT // 128
        for kc in range(16):
            for ti in range(nt):
                tau = t0 // 128 + ti
                kb.op('pool', lambda E: E.indirect_dma_start(out=brTb[:, kc, ti * 128:(ti + 1) * 128], out_offset=None, in_=G_d[:, :],
                                                             in_offset=bass.IndirectOffsetOnAxis(ap=idxb[:, tau, kc:kc + 1], axis=0)),
                      r=['idxb', 'G_d'], w=['brTb'], dma=True)
        for ti in range(nt):
            xk = 'xb%d' % ti
            kb.dma(xb[:, ti, :], x_d[t0 + ti * 128:t0 + (ti + 1) * 128, :], r=['x_in'], w=[xk])
            ln_normalize(kb, xb[:, ti, :], xk, hn[:, :], 'hn', st, mv, rstd)
            kb.op('pool', lambda E: E.tensor_tensor(out=t1[:, :], in0=hn[:, :], in1=rows[:, v, 1, :], op=ALU.mult), r=['hn'], w=['t1'])
            kb.op('pool', lambda E: E.tensor_tensor(out=hb[:, :], in0=t1[:, :], in1=rows[:, v, 0, :], op=ALU.add), r=['t1'], w=['hb'])
            for k in range(8):
                kb.op('pe', lambda E: E.transpose(out=pT[:, k, :], in_=hb[:, k * 128:(k + 1) * 128], identity=ident_b[:, :]), r=['hb'], w=['pT'])
            kb.op('act', lambda E: E.copy(out=hT[:, :, ti * 128:(ti + 1) * 128], in_=pT[:, :, :]), r=['pT'], w=['hT'])
        for oc in range(8):
            wgb = wg[wgi % 2]
            wgk = 'wg%d' % (wgi % 2)
            wgi += 1
            kb.dma(wgb[:, :, :], wg_d[oc].rearrange("(k p) n -> p k n", p=128), w=[wgk], eng='pool')
            for br in range(4):
                pg, pgk = pG[br % 2], 'pG%d' % (br % 2)
                pp, ppk = pP[br % 2], 'pP%d' % (br % 2)
                for k in range(8):
                    kb.op('pe', lambda E: E.matmul(pg[:, 0:NT], lhsT=wgb[:, k, br * 128:(br + 1) * 128], rhs=hT[:, k, 0:NT],
                                                   start=(k == 0), stop=(k == 7)), r=[wgk, 'hT'], w=[pgk])
                kb.op('act', lambda E: E.activation(out=gs[:, br, 0:NT], in_=pg[:, 0:NT], func=AF.Sigmoid), r=[pgk], w=['gs%d' % br])
                for k in range(4):
                    kb.op('pe', lambda E: E.matmul(pp[:, 0:NT], lhsT=wb[:, br * 4 + k, oc * 128:(oc + 1) * 128], rhs=brTb[:, br * 4 + k, 0:NT],
                                                   start=(k == 0), stop=(k == 3)), r=['wb', 'brTb'], w=[ppk])
                kb.op('dve', lambda E: E.tensor_tensor(out=tb[:, br, 0:NT], in0=pp[:, 0:NT], in1=gs[:, br, 0:NT], op=ALU.mult),
                      r=[ppk, 'gs%d' % br], w=['tb%d' % br])
            kb.op('pool', lambda E: E.tensor_tensor(out=tb[:, 0, 0:NT], in0=tb[:, 0, 0:NT], in1=tb[:, 1, 0:NT], op=ALU.add), r=['tb0', 'tb1'], w=['tb0'])
            kb.op('pool', lambda E: E.tensor_tensor(out=tb[:, 2, 0:NT], in0=tb[:, 2, 0:NT], in1=tb[:, 3, 0:NT], op=ALU.add), r=['tb2', 'tb3'], w=['tb2'])
            kb.op('pool', lambda E: E.tensor_tensor(out=mT[:, oc, 0:NT], in0=tb[:, 0, 0:NT], in1=tb[:, 2, 0:NT], op=ALU.add), r=['tb0', 'tb2'], w=['mT'])
        for ti in range(nt):
            xk = 'xb%d' % ti
            for hf in range(2):
                for k in range(8):
                    kb.op('pe', lambda E: E.matmul(pM[:, hf * 512:(hf + 1) * 512], lhsT=mT[:, k, ti * 128:(ti + 1) * 128],
                                                   rhs=wo[:, k, hf * 512:(hf + 1) * 512], start=(k == 0), stop=(k == 7)), r=['mT', 'wo'], w=['pM'])
            kb.op('dve', lambda E: E.tensor_tensor(out=t1[:, :], in0=pM[:, :], in1=rows[:, v, 2, :], op=ALU.mult), r=['pM'], w=['t1'])
            kb.op('dve', lambda E: E.scalar_tensor_tensor(out=t1[:, :], in0=xb[:, ti, :], scalar=ALPHA, in1=t1[:, :], op0=ALU.mult, op1=ALU.add),
                  r=[xk, 't1'], w=['t1'])
            ln_normalize(kb, t1[:, :], 't1', hn[:, :], 'hn', st, mv, rstd)
            kb.op('pool', lambda E: E.tensor_tensor(out=hn[:, :], in0=hn[:, :], in1=lnrow[:, 0, :], op=ALU.mult), r=['hn'], w=['hn'])
            kb.op('pool', lambda E: E.tensor_tensor(out=xb[:, ti, :], in0=hn[:, :], in1=lnrow[:, 1, :], op=ALU.add), r=['hn'], w=[xk])
            kb.dma(x1_d[t0 + ti * 128:t0 + (ti + 1) * 128, :], xb[:, ti, :], r=[xk], w=['x1d'])
    kb.barrier()
    es1.close()

    es2 = ExitStack()
    kb.es = es2
    wq = kb.sb([128, 8, 2048], BF16, "wq")
    keysT = kb.sb([128, 16, 128], BF16, "keysTb")
    xt = kb.sb([128, D], F32, "xt")
    hn = kb.sb([128, D], F32, "hn2")
    h2b = kb.sb([128, D], BF16, "h2b")
    h2T = kb.sb([128, 8, 128], BF16, "h2T")
    qT = kb.sb([128, 16, 128], BF16, "qT")
    sc = kb.sb([128, 16, 128], F32, "sc")
    sc2 = sc
    vv = kb.sb([128, 16, 16], F32, "vv")
    ix = kb.sb([128, 16, 16], U32, "ix")
    ixf = kb.sb([128, 16, 16], F32, "ixf")
    cand = kb.sb([128, 8, 256], F32, "cand")
    cand2 = cand
    best = kb.sb([128, 8, 16], F32, "best")
    pos = kb.sb([128, 8, 16], U32, "pos")
    pa = kb.sb([128, 8, 16], U32, "pa")
    pb_ = kb.sb([128, 8, 16], U32, "pb")
    paf = kb.sb([128, 2, 8, 16], F32, "paf")
    eq = kb.sb([128, 8, 16, 16], F32, "eq")
    sel = kb.sb([128, 2, 8, 16], F32, "sel")
    eidf = kb.sb([128, 128], F32, "eidf")
    eid = kb.sb([128, 128], U32, "eid")
    gate = kb.sb([128, 8, 16], F32, "gate")
    gsum = kb.sb([128, 8], F32, "gsum")
    act = kb.sb([128, 128], F32, "actv")
    wgt = kb.sb([128, 128], F32, "wgt")
    junk = kb.sb([128, D], BF16, "junk")
    Wd = [kb.sb([128, 16, 128], BF16, "Wd%d" % i) for i in range(2)]
    wgtb = kb.sb([128, 128], BF16, "wgtb")
    t1 = kb.sb([128, D], F32, "t1b")
    st = kb.sb([128, 2, 6], F32, "st2")
    mv = kb.sb([128, 2], F32, "mv2")
    rstd = kb.sb([128, 1], F32, "rstd2")
    NG = 20
    ug = [kb.sb([128, D], BF16, "ug%d" % i) for i in range(NG)]
    pT = kb.ps([128, 8, 128], BF16, "pT2")
    pQ = [kb.ps([128, 512], F32, "pQ0")]
    pS = kb.ps([128, 16, 128], F32, "pS")
    pV = kb.ps([128, D], F32, "pV")

    kb.dma(wq[:, :, :], wq_d.rearrange("(k p) n -> p k n", p=128), w=['wq'], eng='pool')
    kb.dma(keysT[:, :, :], keysT_d[:, :, :], w=['keysT'], eng='pool')
    gi = 0
    for ti in range(NTOK // 128):
        v = 0 if ti == 0 else 1
        r0 = ti * 128
        kb.dma(xt[:, :], x1_d[r0:r0 + 128, :], r=['x1d'], w=['xt'])
        ln_normalize(kb, xt[:, :], 'xt', hn[:, :], 'hn', st, mv, rstd)
        kb.op('pool', lambda E: E.tensor_tensor(out=t1[:, :], in0=hn[:, :], in1=rows[:, v, 4, :], op=ALU.mult), r=['hn'], w=['t1'])
        kb.op('pool', lambda E: E.tensor_tensor(out=h2b[:, :], in0=t1[:, :], in1=rows[:, v, 3, :], op=ALU.add), r=['t1'], w=['h2b'])
        for k in range(8):
            kb.op('pe', lambda E: E.transpose(out=pT[:, k, :], in_=h2b[:, k * 128:(k + 1) * 128], identity=ident_b[:, :]), r=['h2b'], w=['pT'])
        kb.op('act', lambda E: E.copy(out=h2T[:, :, :], in_=pT[:, :, :]), r=['pT'], w=['h2T'])
        for g4 in range(4):
            pq, pqk = pQ[0], 'pQ0'
            for j in range(4):
                hs = g4 * 4 + j
                for k in range(8):
                    kb.op('pe', lambda E: E.matmul(pq[:, j * 128:(j + 1) * 128], lhsT=wq[:, k, hs * 128:(hs + 1) * 128], rhs=h2T[:, k, :],
                                                   start=(k == 0), stop=(k == 7)), r=['wq', 'h2T'], w=[pqk])
            kb.op('act', lambda E: E.copy(out=qT[:, g4 * 4:(g4 + 1) * 4, :], in_=pq[:, :].rearrange("p (j t) -> p j t", j=4)), r=[pqk], w=['qT%d' % g4])
        for hs in range(16):
            kb.op('pe', lambda E: E.matmul(pS[:, hs, :], lhsT=qT[:, hs, :], rhs=keysT[:, hs, :], start=True, stop=True),
                  r=['qT%d' % (hs // 4), 'keysT'], w=['pS'])
        kb.op('act', lambda E: E.copy(out=sc[:, :, :], in_=pS[:, :, :]), r=['pS'], w=['sc%d' % i for i in range(16)])
        for hs in range(16):
            sk = 'sc%d' % hs
            kb.op('dve', lambda E: E.max(out=vv[:, hs, 0:8], in_=sc[:, hs, :]), r=[sk], w=['vva%d' % hs])
            kb.op('dve', lambda E: E.max_index(out=ix[:, hs, 0:8], in_max=vv[:, hs, 0:8], in_values=sc[:, hs, :]), r=[sk, 'vva%d' % hs], w=['ixa%d' % hs])
            kb.op('dve', lambda E: E.match_replace(out=sc2[:, hs, :], in_to_replace=vv[:, hs, 0:8], in_values=sc[:, hs, :], imm_value=NEG),
                  r=[sk, 'vva%d' % hs], w=[sk])
            kb.op('dve', lambda E: E.max(out=vv[:, hs, 8:16], in_=sc2[:, hs, :]), r=[sk], w=['vvb%d' % hs])
            kb.op('dve', lambda E: E.max_index(out=ix[:, hs, 8:16], in_max=vv[:, hs, 8:16], in_values=sc2[:, hs, :]), r=[sk, 'vvb%d' % hs], w=['ixb%d' % hs])
        allv = ['vva%d' % i for i in range(16)] + ['vvb%d' % i for i in range(16)]
        alli = ['ixa%d' % i for i in range(16)] + ['ixb%d' % i for i in range(16)]
        kb.op('dve', lambda E: E.tensor_copy(out=ixf[:, :, :], in_=ix[:, :, :]), r=alli, w=['ixf'])
        v4 = vv[:, :, :].rearrange("p (h s) a -> p h s a", s=2)
        for h in range(8):
            kb.op('dve', lambda E: E.tensor_tensor(out=cand[:, h, :].rearrange("p (a b) -> p a b", a=16),
                                                   in0=vv[:, 2 * h, :].unsqueeze(2).to_broadcast([128, 16, 16]),
                                                   in1=vv[:, 2 * h + 1, :].unsqueeze(1).to_broadcast([128, 16, 16]), op=ALU.add), r=allv, w=['cand%d' % h])
            ck = 'cand%d' % h
            kb.op('dve', lambda E: E.max(out=best[:, h, 0:8], in_=cand[:, h, :]), r=[ck], w=['besta%d' % h])
            kb.op('dve', lambda E: E.max_index(out=pos[:, h, 0:8], in_max=best[:, h, 0:8], in_values=cand[:, h, :]), r=[ck, 'besta%d' % h], w=['posa%d' % h])
            kb.op('dve', lambda E: E.match_replace(out=cand2[:, h, :], in_to_replace=best[:, h, 0:8], in_values=cand[:, h, :], imm_value=NEG),
                  r=[ck, 'besta%d' % h], w=[ck])
            kb.op('dve', lambda E: E.max(out=best[:, h, 8:16], in_=cand2[:, h, :]), r=[ck], w=['bestb%d' % h])
            kb.op('dve', lambda E: E.max_index(out=pos[:, h, 8:16], in_max=best[:, h, 8:16], in_values=cand2[:, h, :]), r=[ck, 'bestb%d' % h], w=['posb%d' % h])
        allb = ['besta%d' % i for i in range(8)] + ['bestb%d' % i for i in range(8)]
        allp = ['posa%d' % i for i in range(8)] + ['posb%d' % i for i in range(8)]
        kb.op('dve', lambda E: E.tensor_scalar(out=pa[:, :, :], in0=pos[:, :, :], scalar1=4, scalar2=None, op0=ALU.logical_shift_right), r=allp, w=['pa'])
        kb.op('dve', lambda E: E.tensor_scalar(out=pb_[:, :, :], in0=pos[:, :, :], scalar1=15, scalar2=None, op0=ALU.bitwise_and), r=allp, w=['pb'])
        kb.op('dve', lambda E: E.tensor_copy(out=paf[:, 0, :, :], in_=pa[:, :, :]), r=['pa'], w=['paf0'])
        kb.op('dve', lambda E: E.tensor_copy(out=paf[:, 1, :, :], in_=pb_[:, :, :]), r=['pb'], w=['paf1'])
        for s_ in range(2):
            for h in range(8):
                kb.op('dve', lambda E: E.tensor_tensor(out=eq[:, h, :, :], in0=iota16[:, :].unsqueeze(1).to_broadcast([128, 16, 16]),
                                                       in1=paf[:, s_, h, :].unsqueeze(2).to_broadcast([128, 16, 16]), op=ALU.is_equal),
                      r=['paf%d' % s_, 'iota16'], w=['eq%d' % h])
                kb.op('dve', lambda E: E.tensor_tensor(out=eq[:, h, :, :], in0=eq[:, h, :, :],
                                                       in1=ixf[:, 2 * h + s_, :].unsqueeze(1).to_broadcast([128, 16, 16]), op=ALU.mult),
                      r=['eq%d' % h, 'ixf'], w=['eq%d' % h])
            kb.op('dve', lambda E: E.tensor_reduce(out=sel[:, s_, :, :], in_=eq[:, :, :, :], axis=AX.X, op=ALU.add), r=['eq%d' % h for h in range(8)], w=['sel%d' % s_])
        kb.op('dve', lambda E: E.scalar_tensor_tensor(out=eidf[:, :], in0=sel[:, 0, :, :].rearrange("p h k -> p (h k)"), scalar=128.0,
                                                      in1=sel[:, 1, :, :].rearrange("p h k -> p (h k)"), op0=ALU.mult, op1=ALU.add), r=['sel0', 'sel1'], w=['eidf'])
        kb.op('dve', lambda E: E.tensor_copy(out=eid[:, :], in_=eidf[:, :]), r=['eidf'], w=['eid'])
        kb.op('dve', lambda E: E.tensor_tensor(out=gate[:, :, :], in0=best[:, :, :], in1=best[:, :, 0:1].to_broadcast([128, 8, 16]), op=ALU.subtract), r=allb, w=['gate'])
        kb.op('act', lambda E: E.activation(out=gate[:, :, :], in_=gate[:, :, :], func=AF.Exp), r=['gate'], w=['gate'])
        kb.op('dve', lambda E: E.tensor_reduce(out=gsum[:, :], in_=gate[:, :, :], axis=AX.X, op=ALU.add), r=['gate'], w=['gsum'])
        kb.op('dve', lambda E: E.reciprocal(out=gsum[:, :], in_=gsum[:, :]), r=['gsum'], w=['gsum'])
        kb.op('dve', lambda E: E.tensor_tensor(out=gate[:, :, :], in0=gate[:, :, :], in1=gsum[:, :].unsqueeze(2).to_broadcast([128, 8, 16]), op=ALU.mult), r=['gate', 'gsum'], w=['gate'])
        for r_ in range(128):
            gb, gk = ug[gi % NG], 'ug%d' % (gi % NG)
            gi += 1
            kb.op('pool', lambda E: E.indirect_dma_start(out=gb[:, :], out_offset=None, in_=pub_d[:, :],
                                                         in_offset=bass.IndirectOffsetOnAxis(ap=eid[:, r_:r_ + 1], axis=0)),
                  r=['eid'], w=[gk], dma=True)
            kb.op('dve', lambda E: E.scalar_tensor_tensor(out=junk[:, :], in0=gb[:, :], scalar=1.0, in1=h2b[:, :], op0=ALU.mult, op1=ALU.mult,
                                                          accum_out=act[:, r_:r_ + 1]), r=[gk, 'h2b'], w=['junk', 'act'])
        kb.op('act', lambda E: E.activation(out=act[:, :], in_=act[:, :], func=AF.Gelu), r=['act'], w=['act'])
        kb.op('dve', lambda E: E.tensor_tensor(out=wgtb[:, :], in0=act[:, :], in1=gate[:, :, :].rearrange("p h k -> p (h k)"), op=ALU.mult), r=['act', 'gate'], w=['wgtb'])
        for r_ in range(128):
            gb, gk = ug[gi % NG], 'ug%d' % (gi % NG)
            gi += 1
            if r_ % 16 == 0:
                wdb, wdk = Wd[(r_ // 16) % 2], 'Wd%d' % ((r_ // 16) % 2)
                kb.op('dve', lambda E: E.tensor_tensor(out=wdb[:, :, :], in0=ident_b[:, :].unsqueeze(1).to_broadcast([128, 16, 128]),
                                                       in1=wgtb[:, r_:r_ + 16].unsqueeze(2).to_broadcast([128, 16, 128]), op=ALU.mult),
                      r=['ident_b', 'wgtb'], w=[wdk])
            kb.op('pool', lambda E: E.indirect_dma_start(out=gb[:, :], out_offset=None, in_=pvb_d[:, :],
                                                         in_offset=bass.IndirectOffsetOnAxis(ap=eid[:, r_:r_ + 1], axis=0)),
                  r=['eid'], w=[gk], dma=True)
            for hf in range(2):
                kb.op('pe', lambda E: E.matmul(pV[:, hf * 512:(hf + 1) * 512], lhsT=wdb[:, r_ % 16, :], rhs=gb[:, hf * 512:(hf + 1) * 512],
                                               start=(r_ == 0), stop=(r_ == 127)), r=[wdk, gk], w=['pV'])
        kb.op('dve', lambda E: E.tensor_tensor(out=t1[:, :], in0=pV[:, :], in1=rows[:, v, 5, :], op=ALU.mult), r=['pV'], w=['t1'])
        kb.op('dve', lambda E: E.scalar_tensor_tensor(out=t1[:, :], in0=xt[:, :], scalar=ALPHA, in1=t1[:, :], op0=ALU.mult, op1=ALU.add), r=['xt', 't1'], w=['t1'])
        ln_normalize(kb, t1[:, :], 't1', hn[:, :], 'hn', st, mv, rstd)
        kb.op('pool', lambda E: E.tensor_tensor(out=hn[:, :], in0=hn[:, :], in1=lnrow[:, 2, :], op=ALU.mult), r=['hn'], w=['hn'])
        kb.op('pool', lambda E: E.tensor_tensor(out=t1[:, :], in0=hn[:, :], in1=lnrow[:, 3, :], op=ALU.add), r=['hn'], w=['t1'])
        kb.dma(out_d[r0:r0 + 128, :], t1[:, :], r=['t1'], w=['outd'])
    kb.barrier()
    es2.close()
    kb.barrier()
    es.close()


GATE0 = 8736 - 4096
_CONST = {}


def consts():
    if not _CONST:
        _CONST['ident'] = np.eye(128, dtype=np.float32)
        _CONST['iota16'] = np.tile(np.arange(16, dtype=np.float32)[None, :], (128, 1))
    return _CONST


def layer_weights_B(inp, l):
    w_in = inp['w_in'][l]
    wg = np.ascontiguousarray(w_in[:, GATE0:].reshape(D, 4, 8, 128).transpose(2, 0, 1, 3).reshape(8, D, 512))
    keysT = np.ascontiguousarray(inp['peer_keys'][l].reshape(16, 128, 128).transpose(2, 0, 1))
    c = consts()
    return {
        'w_mod': inp['w_mod'][l], 'b_mod': inp['b_mod'][l], 'wg': wg,
        'wb': np.ascontiguousarray(inp['w_branch'][l].reshape(2048, D)), 'wo': inp['w_out'][l],
        'lnp': np.stack([inp['ln1_g'][l], inp['ln1_b'][l], inp['ln2_g'][l], inp['ln2_b'][l]], 0),
        'wq': inp['peer_wq'][l], 'keysT': keysT, 'peer_u': inp['peer_u'][l], 'peer_v': inp['peer_v'][l],
        'ident': c['ident'], 'iota16': c['iota16'],
    }


def cvec_layout(c_ctx, c_b):
    return np.ascontiguousarray(np.stack([c_ctx, c_b], 0).reshape(2, 8, 128).transpose(0, 2, 1))


TA = 4352
NTA = 34
POOL_WINDOWS = (2, 4, 8, 16)


def emit_A(nc, kb, l, C, xa_d, O_d):
    es = ExitStack()
    kb.es = es
    dt = lambda n, s, d=F32, kind="ExternalInput": nc.dram_tensor(n + "_%d" % l, list(s), d, kind=kind).ap()
    cvec_d = C['cvec']
    wmod_d = C['w_mod_%d' % l]
    bmod_d = C['b_mod_%d' % l]
    wfm_d = dt("wa_fm", [D, 1792])
    wtm_d = dt("wa_tm", [D, 784])
    cw_d = dt("convw", [128, 10, 6])
    poolw_d = dt("pool_w", [4, 128, 128])
    pscale_d = dt("pool_scale", [128, 4])
    invc_d, invcc_d = C['invcnt'], C['invcnt_c']
    rowp_d = dt("rowp", [5, 256])
    dftx_d, dftc_d, dfch_d, tri_d, ident_d, tm_d, fm_d = C['dft_x'], C['dft_c'], C['dft_ch'], C['tri'], C['ident'], C['tm'], C['fm']
    Ov = O_d.rearrange("(t r) c -> r t c", r=1280)
    xrow = (lambda q: q * 128) if l == 0 else (lambda q: _gathered_row(X_CH, *((0, 0) if q == 0 else ((1, 0) if q == 1 else ((0, 128 + (q - 2) * 128) if q < 18 else (1, 128 + (q - 18) * 128))))))

    ident_f = kb.sb([128, 128], F32, "ident_f")
    ident_b = kb.sb([128, 128], BF16, "ident_b")
    ones_f = kb.sb([128, 128], F32, "ones_f")
    kb.dma(ident_f[:, :], ident_d[:, :], w=['ident_f'])
    kb.op('dve', lambda E: E.tensor_copy(out=ident_b[:, :], in_=ident_f[:, :]), r=['ident_f'], w=['ident_b'])
    kb.op('pool', lambda E: E.memset(ones_f[:, :], 1.0), w=['ones'])

    es0 = ExitStack()
    kb.es = es0
    rows = kb.sb([128, 2, 2, D], F32, "modrows")
    cs = kb.sb([128, 2, 8], F32, "cs")
    csrep = kb.sb([128, 2, 8, 128], BF16, "csrep")
    bmrow = kb.sb([128, 512], F32, "bmrow")
    wmb = [kb.sb([128, 8, 512], BF16, "wmb%d" % i) for i in range(2)]
    pmod = [kb.ps([128, 512], F32, "pmod%d" % i) for i in range(2)]
    for v in range(2):
        kb.dma(cs[:, v, :], cvec_d[v, :, :], w=['cs%d' % v])
        kb.op('act', lambda E: E.activation(out=cs[:, v, :], in_=cs[:, v, :], func=AF.Silu), r=['cs%d' % v], w=['cs%d' % v])
        for k in range(8):
            kb.op('dve', lambda E: E.tensor_scalar(out=csrep[:, v, k, :], in0=ones_f[:, :], scalar1=cs[:, v, k:k + 1],
                                                   scalar2=None, op0=ALU.mult), r=['cs%d' % v, 'ones'], w=['csrep%d' % v])
    ci = 0
    for j in range(2):
        for hf in range(2):
            c0 = j * D + hf * 512
            wbuf, wk = wmb[ci % 2], 'wmb%d' % (ci % 2)
            kb.dma(wbuf[:, :, :], wmod_d[:, c0:c0 + 512].rearrange("(k p) n -> p k n", p=128), w=[wk], eng='pool')
            kb.dma(bmrow[:, :], bmod_d[c0:c0 + 512].partition_broadcast(128), w=['bmrow'])
            for v in range(2):
                pk = 'pmod%d' % v
                for k in range(8):
                    kb.op('pe', lambda E: E.matmul(pmod[v][:, :], lhsT=csrep[:, v, k, :], rhs=wbuf[:, k, :],
                                                   start=(k == 0), stop=(k == 7)), r=['csrep%d' % v, wk], w=[pk])
                kb.op('dve', lambda E: E.scalar_tensor_tensor(out=rows[:, v, j, hf * 512:(hf + 1) * 512], in0=pmod[v][:, :],
                                                              scalar=float(j), in1=bmrow[:, :], op0=ALU.add, op1=ALU.add),
                      r=[pk, 'bmrow'], w=['rows'])
            ci += 1
    wfm = kb.sb([128, 8, 1792], BF16, "wfm")
    wtm = kb.sb([128, 8, 784], BF16, "wtm")
    kb.dma(wfm[:, :, :], wfm_d.rearrange("(k p) n -> p k n", p=128), w=['wfm'], eng='pool')
    kb.dma(wtm[:, :, :], wtm_d.rearrange("(k p) n -> p k n", p=128), w=['wtm'], eng='pool')
    xt = kb.sb([128, D], F32, "xt")
    hn = kb.sb([128, D], F32, "hn")
    t1 = kb.sb([128, D], F32, "t1")
    hb = kb.sb([128, D], BF16, "hb")
    hT = kb.sb([128, 8, 512], BF16, "hT")
    tmo = kb.sb([128, 784], F32, "tmo")
    fmo = kb.sb([128, 512], F32, "fmo")
    st = kb.sb([128, 2, 6], F32, "st")
    mv = kb.sb([128, 2], F32, "mv")
    rstd = kb.sb([128, 1], F32, "rstd")
    pT = kb.ps([128, 8, 128], BF16, "pT")
    pTM = kb.ps([128, 1024], F32, "pTM")
    pFM = [kb.ps([128, 512], F32, "pFM%d" % i) for i in range(2)]
    blocks = [(0, 256, 0)] + [(256 + 512 * i, 512, 1) for i in range(8)]
    for (t0, NT, v) in blocks:
        nt = NT // 128
        for ti in range(nt):
            r0 = t0 + ti * 128
            kb.dma(xt[:, :], xa_d[xrow(r0 // 128):xrow(r0 // 128) + 128, :], r=['xa_d'], w=['xt'])
            ln_normalize(kb, xt[:, :], 'xt', hn[:, :], 'hn', st, mv, rstd)
            kb.op('pool', lambda E: E.tensor_tensor(out=t1[:, :], in0=hn[:, :], in1=rows[:, v, 1, :], op=ALU.mult), r=['hn'], w=['t1'])
            kb.op('pool', lambda E: E.tensor_tensor(out=hb[:, :], in0=t1[:, :], in1=rows[:, v, 0, :], op=ALU.add), r=['t1'], w=['hb'])
            for k in range(8):
                kb.op('pe', lambda E: E.transpose(out=pT[:, k, :], in_=hb[:, k * 128:(k + 1) * 128], identity=ident_b[:, :]), r=['hb'], w=['pT'])
            kb.op('act', lambda E: E.copy(out=hT[:, :, ti * 128:(ti + 1) * 128], in_=pT[:, :, :]), r=['pT'], w=['hT'])
            for (c0, c1) in ((0, 512), (512, 784)):
                for k in range(8):
                    kb.op('pe', lambda E: E.matmul(pTM[:, c0:c1], lhsT=hT[:, k, ti * 128:(ti + 1) * 128], rhs=wtm[:, k, c0:c1],
                                                   start=(k == 0), stop=(k == 7)), r=['hT', 'wtm'], w=['pTM'])
            kb.op('act', lambda E: E.copy(out=tmo[:, :], in_=pTM[:, 0:784]), r=['pTM'], w=['tmo'])
            kb.dma(tm_d[r0:r0 + 128, :], tmo[:, :], r=['tmo'], w=['tm_d'])
        for c in range(14):
            pf, pfk = pFM[c % 2], 'pFM%d' % (c % 2)
            for k in range(8):
                kb.op('pe', lambda E: E.matmul(pf[:, 0:NT], lhsT=wfm[:, k, c * 128:(c + 1) * 128], rhs=hT[:, k, 0:NT],
                                               start=(k == 0), stop=(k == 7)), r=['hT', 'wfm'], w=[pfk])
            kb.op('dve', lambda E: E.tensor_copy(out=fmo[:, 0:NT], in_=pf[:, 0:NT]), r=[pfk], w=['fmo'])
            kb.dma(fm_d[c * 128:(c + 1) * 128, t0:t0 + NT], fmo[:, 0:NT], r=['fmo'], w=['fm_d'])
    kb.barrier()
    es0.close()

    es2 = ExitStack()
    kb.es = es2
    pin = kb.sb([128, TA], F32, "pin")
    bufs = [kb.sb([128, 80, 80], F32, "pbA"), kb.sb([128, 80, 80], F32, "pbB")]
    cbufs = [kb.sb([128, 272], F32, "pcA"), kb.sb([128, 272], F32, "pcB")]
    invc = kb.sb([128, 4096], F32, "invc")
    invcc = kb.sb([128, 256], F32, "invcc")
    ptmp = kb.sb([128, 4096], F32, "ptmp")
    dB = kb.sb([128, TA], BF16, "dB")
    pwf = kb.sb([128, 4, 128], F32, "pwf")
    pw = kb.sb([128, 4, 128], BF16, "pw")
    psc = kb.sb([128, 4], F32, "psc")
    pob = kb.sb([128, 512], BF16, "pob")
    pPo = [kb.ps([128, 512], F32, "pPo%d" % i) for i in range(2)]
    kb.dma(pwf[:, :, :], poolw_d.rearrange("g c n -> c g n"), w=['pwf'])
    kb.op('dve', lambda E: E.tensor_copy(out=pw[:, :, :], in_=pwf[:, :, :]), r=['pwf'], w=['pw'])
    kb.dma(psc[:, :], pscale_d[:, :], w=['psc'])
    for g in range(4):
        w_ = POOL_WINDOWS[g]
        lo = w_ // 2
        kb.dma(pin[:, :], fm_d[g * 128:(g + 1) * 128, :], r=['fm_d'], w=['pin'])
        kb.dma(invc[:, :], invc_d[g, :].partition_broadcast(128), w=['invc'])
        kb.dma(invcc[:, :], invcc_d[g, :].partition_broadcast(128), w=['invcc'])
        for i in range(2):
            kb.op('pool', lambda E: E.memset(bufs[i][:, :, :], 0.0), w=['pb%d' % i])
            kb.op('pool', lambda E: E.memset(cbufs[i][:, :], 0.0), w=['pc%d' % i])
        kb.op('act', lambda E: E.copy(out=bufs[0][:, 8:72, 8:72], in_=pin[:, 256:TA].rearrange("p (r c) -> p r c", r=64)), r=['pin'], w=['pb0'])
        kb.op('act', lambda E: E.copy(out=cbufs[0][:, 8:264], in_=pin[:, 0:256]), r=['pin'], w=['pc0'])
        cur = 0
        step = 1
        while step < w_:
            L = 80 - 2 * step + 1
            kb.op('dve', lambda E: E.tensor_tensor(out=bufs[1 - cur][:, :, 0:L], in0=bufs[cur][:, :, 0:L], in1=bufs[cur][:, :, step:step + L], op=ALU.add),
                  r=['pb%d' % cur], w=['pb%d' % (1 - cur)])
            Lc = 272 - 2 * step + 1
            kb.op('pool', lambda E: E.tensor_tensor(out=cbufs[1 - cur][:, 0:Lc], in0=cbufs[cur][:, 0:Lc], in1=cbufs[cur][:, step:step + Lc], op=ALU.add),
                  r=['pc%d' % cur], w=['pc%d' % (1 - cur)])
            cur = 1 - cur
            step *= 2
        ccur = cur
        step = 1
        while step < w_:
            L = 80 - 2 * step + 1
            kb.op('dve', lambda E: E.tensor_tensor(out=bufs[1 - cur][:, 0:L, :], in0=bufs[cur][:, 0:L, :], in1=bufs[cur][:, step:step + L, :], op=ALU.add),
                  r=['pb%d' % cur], w=['pb%d' % (1 - cur)])
            cur = 1 - cur
            step *= 2
        a0 = 8 - lo
        kb.op('dve', lambda E: E.tensor_tensor(out=ptmp[:, :].rearrange("p (r c) -> p r c", r=64), in0=bufs[cur][:, a0:a0 + 64, a0:a0 + 64],
                                               in1=invc[:, :].rearrange("p (r c) -> p r c", r=64), op=ALU.mult), r=['pb%d' % cur, 'invc'], w=['ptmp'])
        kb.op('dve', lambda E: E.tensor_tensor(out=dB[:, 256:TA], in0=ptmp[:, :], in1=pin[:, 256:TA], op=ALU.subtract), r=['ptmp', 'pin'], w=['dB'])
        kb.op('pool', lambda E: E.tensor_tensor(out=cbufs[1 - ccur][:, 0:256], in0=cbufs[ccur][:, a0:a0 + 256], in1=invcc[:, :], op=ALU.mult),
              r=['pc%d' % ccur, 'invcc'], w=['pc%d' % (1 - ccur)])
        kb.op('pool', lambda E: E.tensor_tensor(out=dB[:, 0:256], in0=cbufs[1 - ccur][:, 0:256], in1=pin[:, 0:256], op=ALU.subtract), r=['pc%d' % (1 - ccur), 'pin'], w=['dB'])
        for bi, (t0, NT, v) in enumerate(blocks):
            pp, ppk = pPo[bi % 2], 'pPo%d' % (bi % 2)
            kb.op('pe', lambda E: E.matmul(pp[:, 0:NT], lhsT=pw[:, g, :], rhs=dB[:, t0:t0 + NT], start=True, stop=True), r=['pw', 'dB'], w=[ppk])
            kb.op('act', lambda E: E.activation(out=pob[:, 0:NT], in_=pp[:, 0:NT], func=AF.Copy, scale=psc[:, g:g + 1]), r=[ppk, 'psc'], w=['pob'])
            kb.dma(Ov[g * 128:(g + 1) * 128, t0 // 128:(t0 + NT) // 128, :], pob[:, 0:NT].rearrange("p (t c) -> p t c", c=128), r=['pob'], w=['brT_d'])
    kb.barrier()
    es2.close()

    es3 = ExitStack()
    kb.es = es3
    xf = kb.sb([128, NTA, 256], BF16, "xf")
    dfb = [kb.sb([128, 32, 512], BF16, "dfb%d" % i) for i in range(2)]
    dfc = kb.sb([128, 2, 2, 256], BF16, "dfc")
    dchf = kb.sb([128, 2, 128], F32, "dchf")
    dch = kb.sb([128, 2, 128], BF16, "dch")
    ysb = kb.sb([128, 2, 2, 512], BF16, "ysb")
    fob = kb.sb([128, 512], BF16, "fob")
    pY = [[kb.ps([128, 512], F32, "pY%d%d" % (g, t)) for t in range(2)] for g in range(2)]
    pFo = [kb.ps([128, 512], F32, "pFo%d" % i) for i in range(2)]
    kb.dma(xf[:, :, :], tm_d[:, 0:256].rearrange("(t p) c -> p t c", p=128), r=['tm_d'], w=['xf'], eng='pool')
    kb.dma(dchf[:, :, :], dfch_d.rearrange("t c n -> c t n"), w=['dchf'])
    kb.op('dve', lambda E: E.tensor_copy(out=dch[:, :, :], in_=dchf[:, :, :]), r=['dchf'], w=['dch'])
    kb.dma(dfc[:, :, :, :], dftc_d.rearrange("t (k p) n -> p t k n", p=128), w=['dfc'])
    dbi = 0
    fo_i = 0
    for n in range(9):
        if n == 0:
            NT, nk, tcol = 256, 2, 0
        else:
            NT, nk, tcol = 512, 32, 256 + (n - 1) * 512
        for trig in range(2):
            if n == 0:
                rhs_of = lambda k: dfc[:, trig, k, :]
                rk = 'dfc'
            else:
                db, rk = dfb[dbi % 2], 'dfb%d' % (dbi % 2)
                dbi += 1
                kb.dma(db[:, :, :], dftx_d[trig, :, (n - 1) * 512:n * 512].rearrange("(k p) n -> p k n", p=128), w=[rk])
                rhs_of = lambda k: db[:, k, :]
            for g in range(2):
                for k in range(nk):
                    tile_i = k if n == 0 else 2 + k
                    kb.op('pe', lambda E: E.matmul(pY[g][trig][:, 0:NT], lhsT=xf[:, tile_i, g * 128:(g + 1) * 128], rhs=rhs_of(k),
                                                   start=(k == 0), stop=(k == nk - 1)), r=['xf', rk], w=['pY%d%d' % (g, trig)])
                kb.op('act', lambda E: E.copy(out=ysb[:, g, trig, 0:NT], in_=pY[g][trig][:, 0:NT]), r=['pY%d%d' % (g, trig)], w=['ysb%d%d' % (g, trig)])
        for g in range(2):
            pf, pfk = pFo[fo_i % 2], 'pFo%d' % (fo_i % 2)
            fo_i += 1
            for trig in range(2):
                kb.op('pe', lambda E: E.matmul(pf[:, 0:NT], lhsT=dch[:, trig, :], rhs=ysb[:, g, trig, 0:NT], start=(trig == 0), stop=(trig == 1)),
                      r=['dch', 'ysb%d%d' % (g, trig)], w=[pfk])
            kb.op('dve', lambda E: E.tensor_copy(out=fob[:, 0:NT], in_=pf[:, 0:NT]), r=[pfk], w=['fob'])
            kb.dma(Ov[512 + g * 128:512 + (g + 1) * 128, tcol // 128:(tcol + NT) // 128, :], fob[:, 0:NT].rearrange("p (t c) -> p t c", c=128), r=['fob'], w=['brT_d'])
    kb.barrier()
    es3.close()
    kb.es = es

    tri = kb.sb([128, 4, 128], F32, "tri")
    kb.dma(tri[:, :, :], tri_d.rearrange("m j i -> j m i"), w=['tri'])
    r16 = kb.sb([128, NTA, 16], F32, "r16")
    sp16 = kb.sb([128, NTA, 16], F32, "sp16")
    brow = kb.sb([128, 32], F32, "brow")
    la = kb.sb([128, NTA, 12], F32, "la")
    acs = kb.sb([128, NTA, 12], F32, "acs")
    eacs = kb.sb([128, NTA, 12], F32, "eacs")
    neacs = kb.sb([128, NTA, 12], F32, "neacs")
    wst = kb.sb([128, NTA, 12], F32, "wst")
    etot = kb.sb([128, NTA, 12], F32, "etot")
    beta = kb.sb([128, NTA, 4], F32, "beta")
    rowp = kb.sb([128, 3, 256], F32, "rowp")
    with nc.allow_non_contiguous_dma(reason="small strided loads"):
        kb.dma(r16[:, :, :], tm_d[:, 768:784].rearrange("(t p) c -> p t c", p=128), r=['tm_d'], w=['r16'])
    kb.dma(brow[:, :], rowp_d[0, 0:32].partition_broadcast(128), w=['brow'])
    for j in range(3):
        kb.dma(rowp[:, j, :], rowp_d[1 + j, :].partition_broadcast(128), w=['rowp'])
    es4 = ExitStack()
    kb.es = es4
    pc1 = kb.ps([128, 512], F32, "pc1")
    pc2 = kb.ps([128, 512], F32, "pc2")
    pc3 = kb.ps([128, 512], F32, "pc3")
    V_ = lambda fn, r, w: kb.op('dve', fn, r=r, w=w)
    A_ = lambda fn, r, w: kb.op('act', fn, r=r, w=w)
    G_ = lambda fn, r, w: kb.op('pool', fn, r=r, w=w)
    P_ = lambda fn, r, w: kb.op('pe', fn, r=r, w=w)
    A_(lambda E: E.activation(out=brow[:, 16:32], in_=brow[:, 16:32], func=AF.Exp), ['brow'], ['brow'])
    V_(lambda E: E.tensor_scalar(out=brow[:, 16:32], in0=brow[:, 16:32], scalar1=-1.0, scalar2=None, op0=ALU.mult), ['brow'], ['brow'])
    V_(lambda E: E.tensor_tensor(out=sp16[:, :, :], in0=r16[:, :, :], in1=brow[:, 0:16].unsqueeze(1).to_broadcast([128, NTA, 16]), op=ALU.add), ['r16', 'brow'], ['sp16'])
    A_(lambda E: E.activation(out=sp16[:, :, :], in_=sp16[:, :, :], func=AF.Exp), ['sp16'], ['sp16'])
    V_(lambda E: E.tensor_scalar(out=sp16[:, :, :], in0=sp16[:, :, :], scalar1=1.0, scalar2=None, op0=ALU.add), ['sp16'], ['sp16'])
    A_(lambda E: E.activation(out=sp16[:, :, :], in_=sp16[:, :, :], func=AF.Ln), ['sp16'], ['sp16'])
    A_(lambda E: E.activation(out=beta[:, :, :], in_=r16[:, :, 8:12], func=AF.Sigmoid), ['r16'], ['beta'])
    V_(lambda E: E.tensor_tensor(out=la[:, :, 0:8], in0=sp16[:, :, 0:8], in1=brow[:, 16:24].unsqueeze(1).to_broadcast([128, NTA, 8]), op=ALU.mult), ['sp16', 'brow'], ['la'])
    V_(lambda E: E.tensor_tensor(out=la[:, :, 8:12], in0=sp16[:, :, 12:16], in1=brow[:, 28:32].unsqueeze(1).to_broadcast([128, NTA, 4]), op=ALU.mult), ['sp16', 'brow', 'la'], ['la'])
    laf = la[:, :, :].rearrange("p t c -> p (t c)")
    P_(lambda E: E.matmul(pc1[:, 0:408], lhsT=tri[:, 0, :], rhs=laf, start=True, stop=True), ['tri', 'la'], ['pc1'])
    P_(lambda E: E.matmul(pc2[:, 0:408], lhsT=tri[:, 1, :], rhs=laf, start=True, stop=True), ['tri', 'la'], ['pc2'])
    P_(lambda E: E.matmul(pc3[:, 0:408], lhsT=ones_f[:, :], rhs=laf, start=True, stop=True), ['ones', 'la'], ['pc3'])
    p1v = pc1[:, 0:408].rearrange("p (t c) -> p t c", c=12)
    p2v = pc2[:, 0:408].rearrange("p (t c) -> p t c", c=12)
    p3v = pc3[:, 0:408].rearrange("p (t c) -> p t c", c=12)
    for (c0, c1, pv, pk) in ((0, 4, p1v, 'pc1'), (4, 8, p2v, 'pc2'), (8, 10, p1v, 'pc1'), (10, 12, p2v, 'pc2')):
        V_(lambda E: E.tensor_copy(out=acs[:, :, c0:c1], in_=pv[:, :, c0:c1]), [pk, 'acs'], ['acs'])
    A_(lambda E: E.activation(out=eacs[:, :, :], in_=acs[:, :, :], func=AF.Exp), ['acs'], ['eacs'])
    V_(lambda E: E.tensor_scalar(out=neacs[:, :, :], in0=eacs[:, :, :], scalar1=-1.0, scalar2=None, op0=ALU.mult), ['eacs'], ['neacs'])
    V_(lambda E: E.tensor_tensor(out=wst[:, :, :], in0=p3v, in1=acs[:, :, :], op=ALU.subtract), ['pc3', 'acs'], ['wst'])
    A_(lambda E: E.activation(out=wst[:, :, :], in_=wst[:, :, :], func=AF.Exp), ['wst'], ['wst'])
    A_(lambda E: E.activation(out=etot[:, :, :], in_=p3v, func=AF.Exp), ['pc3'], ['etot'])

    kb.barrier()
    es4.close()
    kb.es = es
    cwt = kb.sb([128, 10, 6], F32, "cwt")
    kb.dma(cwt[:, :, :], cw_d[:, :, :], w=['cwt'])
    PSH = {}

    def open_conv_psum():
        esc = ExitStack()
        kb.es = esc
        PSH['pTt'] = kb.ps([128, 8, 128], BF16, "pTt")
        PSH['pSS'] = kb.ps([128, 512], F32, "pSS")
        return esc
    segs = ((0, 256), (256, TA))

    def conv_chunk(ci, cin, cout):
        kb.dma(cin[:, :], fm_d[(4 + ci) * 128:(5 + ci) * 128, :], r=['fm_d'], w=['cin'])
        V_(lambda E: E.tensor_scalar(out=cout[:, :], in0=cin[:, :], scalar1=cwt[:, ci, 2:3], scalar2=cwt[:, ci, 5:6], op0=ALU.mult, op1=ALU.add), ['cin', 'cwt'], ['cout'])
        for k in (0, 1, 3, 4):
            dd = k - 2
            for (s0, s1) in segs:
                a, b = s0 + max(0, -dd), s1 - max(0, dd)
                V_(lambda E: E.scalar_tensor_tensor(out=cout[:, a:b], in0=cin[:, a + dd:b + dd], scalar=cwt[:, ci, k:k + 1], in1=cout[:, a:b],
                                                    op0=ALU.mult, op1=ALU.add), ['cin', 'cwt', 'cout'], ['cout'])
        A_(lambda E: E.activation(out=cout[:, :], in_=cout[:, :], func=AF.Silu), ['cout'], ['cout'])

    def to_tok(src_bf, skey, dst, dkey, c0):
        for t0 in range(0, NTA, 8):
            n = min(8, NTA - t0)
            for j in range(n):
                P_(lambda E: E.transpose(out=PSH['pTt'][:, j, :], in_=src_bf[:, (t0 + j) * 128:(t0 + j + 1) * 128], identity=ident_b[:, :]), [skey], ['pTt'])
            A_(lambda E: E.copy(out=dst[:, t0:t0 + n, c0:c0 + 128], in_=PSH['pTt'][:, 0:n, :]), ['pTt'], [dkey])

    def l2norm_to(cout, cin, dst, dkey, scale):
        A_(lambda E: E.activation(out=cin[:, :], in_=cout[:, :], func=AF.Square), ['cout'], ['cin'])
        for (t0, NT, v) in blocks:
            P_(lambda E: E.matmul(PSH['pSS'][:, 0:NT], lhsT=ones_f[:, :], rhs=cin[:, t0:t0 + NT], start=True, stop=True), ['ones', 'cin'], ['pSS'])
            V_(lambda E: E.tensor_scalar(out=cin[:, t0:t0 + NT], in0=PSH['pSS'][:, 0:NT], scalar1=1e-6, scalar2=None, op0=ALU.add), ['pSS', 'cin'], ['cin'])
        A_(lambda E: E.activation(out=cin[:, :], in_=cin[:, :], func=AF.Sqrt), ['cin'], ['cin'])
        V_(lambda E: E.reciprocal(out=cin[:, :], in_=cin[:, :]), ['cin'], ['cin'])
        V_(lambda E: E.scalar_tensor_tensor(out=dst, in0=cout[:, :], scalar=scale, in1=cin[:, :], op0=ALU.mult, op1=ALU.mult), ['cout', 'cin'], [dkey])

    order_f = list(range(NTA))
    order_b = [1, 0] + list(range(NTA - 1, 1, -1))

    def scan(units, p, yacc, es_):
        kb.es = es_
        lanes = []
        for ui, u in enumerate(units):
            for d in range(2):
                L = dict(u=u, d=d, id='%d_%d' % (ui, d))
                L['S'] = kb.sb([128, p], F32, "S" + L['id'])
                L['Sb'] = kb.sb([128, p], BF16, "Sb" + L['id'])
                L['labc'] = kb.sb([128, 128], F32, "labc" + L['id'])
                L['Dm'] = kb.sb([128, 128], F32, "Dm" + L['id'])
                L['PT'] = kb.sb([128, 128], BF16, "PT" + L['id'])
                L['Vb'] = kb.sb([128, p], BF16, "Vb" + L['id'])
                L['Vw'] = kb.sb([128, p], BF16, "Vw" + L['id'])
                L['yt'] = kb.sb([128, p], F32, "yt" + L['id'])
                if u['gdn']:
                    for nm in ('Es', 'N0', 'N1', 'A0', 'A1', 'P', 'r'):
                        L[nm] = kb.sb([128, 128], F32, nm + L['id'])
                V_(lambda E: E.memset(L['S'][:, :], 0.0), [], ['S' + L['id']])
                V_(lambda E: E.memset(L['Sb'][:, :], 0.0), [], ['Sb' + L['id']])
                lanes.append(L)
        pA = kb.ps([128, 512], F32, "pA")
        pKQ = kb.ps([128, 512], F32, "pKQ")
        pY = kb.ps([128, 512], F32, "pYs")
        pO = kb.ps([128, 512], F32, "pO")
        pS_ = kb.ps([128, 512], F32, "pSd")
        pG = [kb.ps([128, 512], F32, "pG%d" % i) for i in range(3)] if units[0]['gdn'] else None
        for step in range(NTA):
            for L in lanes:
                u, d, lid = L['u'], L['d'], L['id']
                t = (order_f if d == 0 else order_b)[step]
                col = u['col'](d)
                QT = u['QT'][:, t * 128:(t + 1) * 128]
                KT = u['KT'][:, t * 128:(t + 1) * 128]
                Ktok = u['Ktok'][:, t, :]
                k = lambda nm: nm + lid
                V_(lambda E: E.tensor_scalar(out=L['labc'][:, :], in0=ones_f[:, :], scalar1=la[:, t, col:col + 1], scalar2=None, op0=ALU.mult), ['ones', 'la'], [k('labc')])
                P_(lambda E: E.matmul(pA[:, 0:128], lhsT=L['labc'][:, :], rhs=tri[:, d, :], start=True, stop=True), [k('labc'), 'tri'], ['pA'])
                V_(lambda E: E.scalar_tensor_tensor(out=L['Dm'][:, :], in0=pA[:, 0:128], scalar=acs[:, t, col:col + 1], in1=tri[:, 2 + d, :],
                                                    op0=ALU.subtract, op1=ALU.add), ['pA', 'acs', 'tri'], [k('Dm')])
                A_(lambda E: E.activation(out=L['Dm'][:, :], in_=L['Dm'][:, :], func=AF.Exp), [k('Dm')], [k('Dm')])
                P_(lambda E: E.matmul(pKQ[:, 0:128], lhsT=KT, rhs=QT, start=True, stop=True), [u['KTk'], u['QTk']], ['pKQ'])
                V_(lambda E: E.tensor_tensor(out=L['PT'][:, :], in0=pKQ[:, 0:128], in1=L['Dm'][:, :], op=ALU.mult), ['pKQ', k('Dm')], [k('PT')])
                if u['gdn']:
                    bc = u['bcol'](d)
                    G_(lambda E: E.tensor_tensor(out=L['Es'][:, :], in0=L['Dm'][:, :], in1=ident_f[:, :], op=ALU.subtract), [k('Dm'), 'ident_f'], [k('Es')])
                    P_(lambda E: E.matmul(pG[0][:, 0:128], lhsT=KT, rhs=KT, start=True, stop=True), [u['KTk']], ['pG0'])
                    N, A, Nn, An = L['N0'], L['A0'], L['N1'], L['A1']
                    nk = [k('N0'), k('A0'), k('N1'), k('A1')]
                    V_(lambda E: E.scalar_tensor_tensor(out=N[:, :], in0=pG[0][:, 0:128], scalar=beta[:, t, bc:bc + 1], in1=L['Es'][:, :],
                                                        op0=ALU.mult, op1=ALU.mult), ['pG0', 'beta', k('Es')], [nk[0]])
                    P_(lambda E: E.transpose(out=pG[1][:, 0:128], in_=N[:, :], identity=ident_f[:, :]), [nk[0], 'ident_f'], ['pG1'])
                    A_(lambda E: E.copy(out=A[:, :], in_=pG[1][:, 0:128]), ['pG1'], [nk[1]])
                    G_(lambda E: E.tensor_tensor(out=L['P'][:, :], in0=ident_f[:, :], in1=N[:, :], op=ALU.subtract), [nk[0], 'ident_f'], [k('P')])
                    for it in range(1, 7):
                        P_(lambda E: E.matmul(pG[1][:, 0:128], lhsT=N[:, :], rhs=A[:, :], start=True, stop=True), [nk[0], nk[1]], ['pG1'])
                        A_(lambda E: E.copy(out=An[:, :], in_=pG[1][:, 0:128]), ['pG1'], [nk[3]])
                        if it < 6:
                            P_(lambda E: E.matmul(pG[0][:, 0:128], lhsT=A[:, :], rhs=N[:, :], start=True, stop=True), [nk[0], nk[1]], ['pG0'])
                            V_(lambda E: E.tensor_copy(out=Nn[:, :], in_=pG[0][:, 0:128]), ['pG0'], [nk[2]])
                        P_(lambda E: E.matmul(pG[2][:, 0:128], lhsT=An[:, :], rhs=L['P'][:, :], start=True, stop=True), [nk[3], k('P')], ['pG2'])
                        V_(lambda E: E.tensor_tensor(out=L['P'][:, :], in0=L['P'][:, :], in1=pG[2][:, 0:128], op=ALU.add), ['pG2', k('P')], [k('P')])
                        N, A, Nn, An = Nn, An, N, A
                        nk = [nk[2], nk[3], nk[0], nk[1]]
                    P_(lambda E: E.matmul(pO[:, 0:p], lhsT=KT, rhs=L['Sb'][:, :], start=True, stop=True), [u['KTk'], k('Sb')], ['pO'])
                    V_(lambda E: E.scalar_tensor_tensor(out=L['r'][:, :], in0=pO[:, 0:p], scalar=neacs[:, t, col:col + 1], in1=u['vtok'][:, t, :],
                                                        op0=ALU.mult, op1=ALU.add), ['pO', 'neacs', u['vk']], [k('r')])
                    P_(lambda E: E.matmul(pS_[:, 0:p], lhsT=L['P'][:, :], rhs=L['r'][:, :], start=True, stop=True), [k('P'), k('r')], ['pSd'])
                    V_(lambda E: E.tensor_scalar(out=L['Vb'][:, :], in0=pS_[:, 0:p], scalar1=beta[:, t, bc:bc + 1], scalar2=None, op0=ALU.mult), ['pSd', 'beta'], [k('Vb')])
                else:
                    u['vfn'](d, t, L['Vb'], k('Vb'))
                P_(lambda E: E.matmul(pY[:, 0:p], lhsT=L['PT'][:, :], rhs=L['Vb'][:, :], start=True, stop=True), [k('PT'), k('Vb')], ['pY'])
                P_(lambda E: E.matmul(pO[:, 0:p], lhsT=QT, rhs=L['Sb'][:, :], start=True, stop=True), [u['QTk'], k('Sb')], ['pO'])
                ys = yacc[:, t, u['y0']:u['y0'] + p]
                V_(lambda E: E.scalar_tensor_tensor(out=L['yt'][:, :], in0=pO[:, 0:p], scalar=eacs[:, t, col:col + 1], in1=ys, op0=ALU.mult, op1=ALU.add),
                   ['pO', 'eacs', 'yacc%d' % t], [k('yt')])
                V_(lambda E: E.tensor_tensor(out=ys, in0=L['yt'][:, :], in1=pY[:, 0:p], op=ALU.add), [k('yt'), 'pY'], ['yacc%d' % t])
                V_(lambda E: E.tensor_scalar(out=L['Vw'][:, :], in0=L['Vb'][:, :], scalar1=wst[:, t, col:col + 1], scalar2=None, op0=ALU.mult), [k('Vb'), 'wst'], [k('Vw')])
                P_(lambda E: E.matmul(pS_[:, 0:p], lhsT=Ktok, rhs=L['Vw'][:, :], start=True, stop=True), [u['Ktokk'], k('Vw')], ['pSd'])
                V_(lambda E: E.scalar_tensor_tensor(out=L['S'][:, :], in0=L['S'][:, :], scalar=etot[:, t, col:col + 1], in1=pS_[:, 0:p], op0=ALU.mult, op1=ALU.add),
                   ['pSd', 'etot', k('S')], [k('S')])
                A_(lambda E: E.copy(out=L['Sb'][:, :], in_=L['S'][:, :]), [k('S')], [k('Sb')])

    def post_store(gsrc, gkey, row0, t, ob, pTo):
        for j in range(2):
            P_(lambda E: E.transpose(out=pTo[:, j, :], in_=gsrc[:, j * 128:(j + 1) * 128], identity=ident_b[:, :]), [gkey], ['pTo'])
        A_(lambda E: E.copy(out=ob[:, :, :], in_=pTo[:, :, :]), ['pTo'], ['ob'])
        for j in range(2):
            kb.dma(Ov[row0 + j * 128:row0 + (j + 1) * 128, t, :], ob[:, j, :], r=['ob'], w=['brT_d'])

    es5 = ExitStack()
    kb.es = es5
    cin = kb.sb([128, TA], F32, "cin")
    cout = kb.sb([128, TA], F32, "cout")
    cbf = kb.sb([128, TA], BF16, "cbf")
    BT = kb.sb([128, TA], BF16, "BT")
    CT = kb.sb([128, TA], BF16, "CT")
    xtok = kb.sb([128, NTA, 256], BF16, "xtok")
    Btok = kb.sb([128, NTA, 128], BF16, "Btok")
    yacc = kb.sb([128, NTA, 256], F32, "yacc")
    G_(lambda E: E.memset(yacc[:, :, :], 0.0), [], ['yacc%d' % t for t in range(NTA)])
    esc = open_conv_psum()
    for ci in range(2):
        conv_chunk(ci, cin, cout)
        A_(lambda E: E.copy(out=cbf[:, :], in_=cout[:, :]), ['cout'], ['cbf'])
        to_tok(cbf, 'cbf', xtok, 'xtok', ci * 128)
    conv_chunk(2, cin, cout)
    A_(lambda E: E.copy(out=BT[:, :], in_=cout[:, :]), ['cout'], ['BT'])
    to_tok(BT, 'BT', Btok, 'Btok', 0)
    conv_chunk(3, cin, cout)
    A_(lambda E: E.copy(out=CT[:, :], in_=cout[:, :]), ['cout'], ['CT'])

    def ssd_unit(h):
        def vfn(d, t, Vb, vkey):
            V_(lambda E: E.tensor_scalar(out=Vb[:, :], in0=xtok[:, t, h * 64:(h + 1) * 64], scalar1=sp16[:, t, d * 4 + h:d * 4 + h + 1], scalar2=None, op0=ALU.mult),
               ['xtok', 'sp16'], [vkey])
        return dict(QT=CT, QTk='CT', KT=BT, KTk='BT', Ktok=Btok, Ktokk='Btok', col=lambda d: d * 4 + h, vfn=vfn, gdn=False, y0=h * 64)
    kb.barrier()
    esc.close()
    pu_l = nc.dram_tensor("peer_u_%d" % l, [16384, D], F32, kind="ExternalInput").ap()
    pv_l = nc.dram_tensor("peer_v_%d" % l, [16384, D], F32, kind="ExternalInput").ap()
    C['pu_%d' % l], C['pv_%d' % l] = pu_l, pv_l
    for i in range(16):
        kb.dma(C['pub'][i * 1024:(i + 1) * 1024, :], pu_l[i * 1024:(i + 1) * 1024, :], w=['pub%d' % i], eng='pool')
        kb.dma(C['pvb'][i * 1024:(i + 1) * 1024, :], pv_l[i * 1024:(i + 1) * 1024, :], w=['pvb%d' % i], eng='pool')
    es5s = ExitStack()
    scan([ssd_unit(h) for h in range(4)], 64, yacc, es5s)
    kb.barrier()
    es5s.close()
    kb.es = es5
    zt = kb.sb([128, 256], F32, "zt")
    g1 = kb.sb([128, 256], F32, "g1")
    g2 = kb.sb([128, 256], F32, "g2")
    gb = kb.sb([128, 256], BF16, "gbf")
    ss = kb.sb([128, 2], F32, "ss")
    ob = kb.sb([128, 2, 128], BF16, "ob")
    pTo = kb.ps([128, 2, 128], BF16, "pTo")
    for t in range(NTA):
        kb.dma(zt[:, :], tm_d[t * 128:(t + 1) * 128, 256:512], r=['tm_d'], w=['zt'])
        A_(lambda E: E.activation(out=zt[:, :], in_=zt[:, :], func=AF.Silu), ['zt'], ['zt'])
        V_(lambda E: E.tensor_tensor(out=g1[:, :], in0=xtok[:, t, :], in1=rowp[:, 0, :], op=ALU.mult), ['xtok', 'rowp'], ['g1'])
        V_(lambda E: E.tensor_tensor(out=g1[:, :], in0=g1[:, :], in1=yacc[:, t, :], op=ALU.add), ['g1', 'yacc%d' % t], ['g1'])
        V_(lambda E: E.tensor_tensor(out=g1[:, :], in0=g1[:, :], in1=zt[:, :], op=ALU.mult), ['g1', 'zt'], ['g1'])
        V_(lambda E: E.scalar_tensor_tensor(out=g2[:, :], in0=g1[:, :], scalar=1.0 / 256.0, in1=g1[:, :], op0=ALU.mult, op1=ALU.mult, accum_out=ss[:, 0:1]), ['g1'], ['g2', 'ss'])
        V_(lambda E: E.tensor_scalar(out=ss[:, 0:1], in0=ss[:, 0:1], scalar1=EPS, scalar2=None, op0=ALU.add), ['ss'], ['ss'])
        A_(lambda E: E.activation(out=ss[:, 0:1], in_=ss[:, 0:1], func=AF.Sqrt), ['ss'], ['ss'])
        V_(lambda E: E.reciprocal(out=ss[:, 0:1], in_=ss[:, 0:1]), ['ss'], ['ss'])
        V_(lambda E: E.scalar_tensor_tensor(out=gb[:, :], in0=g1[:, :], scalar=ss[:, 0:1], in1=rowp[:, 1, :], op0=ALU.mult, op1=ALU.mult), ['g1', 'ss', 'rowp'], ['gb'])
        post_store(gb, 'gb', 768, t, ob, pTo)
    kb.barrier()
    es5.close()

    es6 = ExitStack()
    kb.es = es6
    cin = kb.sb([128, TA], F32, "cin6")
    cout = kb.sb([128, TA], F32, "cout6")
    cbf = kb.sb([128, TA], BF16, "cbf6")
    qT = [kb.sb([128, TA], BF16, "qT%d" % h) for h in range(2)]
    kT = [kb.sb([128, TA], BF16, "kT%d" % h) for h in range(2)]
    ktok = [kb.sb([128, NTA, 128], BF16, "ktok%d" % h) for h in range(2)]
    vtok = [kb.sb([128, NTA, 128], BF16, "vtok%d" % h) for h in range(2)]
    oacc = kb.sb([128, NTA, 256], F32, "oacc")
    G_(lambda E: E.memset(oacc[:, :, :], 0.0), [], ['yacc%d' % t for t in range(NTA)])
    esc = open_conv_psum()
    for h in range(2):
        conv_chunk(4 + h, cin, cout)
        l2norm_to(cout, cin, qT[h][:, :], 'qT%d' % h, 128 ** -0.5)
        conv_chunk(6 + h, cin, cout)
        l2norm_to(cout, cin, kT[h][:, :], 'kT%d' % h, 1.0)
        to_tok(kT[h], 'kT%d' % h, ktok[h], 'ktok%d' % h, 0)
        conv_chunk(8 + h, cin, cout)
        A_(lambda E: E.copy(out=cbf[:, :], in_=cout[:, :]), ['cout'], ['cbf'])
        to_tok(cbf, 'cbf', vtok[h], 'vtok%d' % h, 0)

    def gdn_unit(h):
        return dict(QT=qT[h], QTk='qT%d' % h, KT=kT[h], KTk='kT%d' % h, Ktok=ktok[h], Ktokk='ktok%d' % h, col=lambda d: 8 + d * 2 + h,
                    bcol=lambda d: d * 2 + h, vtok=vtok[h], vk='vtok%d' % h, gdn=True, y0=h * 128)
    kb.barrier()
    esc.close()
    es6s = ExitStack()
    scan([gdn_unit(h) for h in range(2)], 128, oacc, es6s)
    kb.barrier()
    es6s.close()
    kb.es = es6
    zt = kb.sb([128, 256], F32, "zt6")
    g1 = kb.sb([128, 256], F32, "g16")
    g2 = kb.sb([128, 256], F32, "g26")
    gb = kb.sb([128, 256], BF16, "gbf6")
    ss = kb.sb([128, 2], F32, "ss6")
    ob = kb.sb([128, 2, 128], BF16, "ob6")
    pTo = kb.ps([128, 2, 128], BF16, "pTo6")
    for t in range(NTA):
        kb.dma(zt[:, :], tm_d[t * 128:(t + 1) * 128, 512:768], r=['tm_d'], w=['zt'])
        A_(lambda E: E.activation(out=zt[:, :], in_=zt[:, :], func=AF.Silu), ['zt'], ['zt'])
        for h in range(2):
            hs = slice(h * 128, (h + 1) * 128)
            V_(lambda E: E.scalar_tensor_tensor(out=g2[:, hs], in0=oacc[:, t, hs], scalar=1.0 / 128.0, in1=oacc[:, t, hs], op0=ALU.mult, op1=ALU.mult,
                                                accum_out=ss[:, h:h + 1]), ['yacc%d' % t], ['g2', 'ss'])
        V_(lambda E: E.tensor_scalar(out=ss[:, :], in0=ss[:, :], scalar1=EPS, scalar2=None, op0=ALU.add), ['ss'], ['ss'])
        A_(lambda E: E.activation(out=ss[:, :], in_=ss[:, :], func=AF.Sqrt), ['ss'], ['ss'])
        V_(lambda E: E.reciprocal(out=ss[:, :], in_=ss[:, :]), ['ss'], ['ss'])
        for h in range(2):
            hs = slice(h * 128, (h + 1) * 128)
            V_(lambda E: E.scalar_tensor_tensor(out=g1[:, hs], in0=oacc[:, t, hs], scalar=ss[:, h:h + 1], in1=rowp[:, 2, hs], op0=ALU.mult, op1=ALU.mult),
               ['yacc%d' % t, 'ss', 'rowp'], ['g1'])
        V_(lambda E: E.tensor_tensor(out=gb[:, :], in0=g1[:, :], in1=zt[:, :], op=ALU.mult), ['g1', 'zt'], ['gb'])
        post_store(gb, 'gb', 1024, t, ob, pTo)
    kb.barrier()
    es6.close()
    kb.barrier()
    es.close()


def _cnt(n, win):
    lo = win // 2
    hi = win - 1 - lo
    t = np.arange(n)
    return (np.minimum(t + hi, n - 1) - np.maximum(t - lo, 0) + 1).astype(np.float64)


def consts_A():
    c = consts()
    if 'dft_x' not in c:
        import ml_dtypes
        bf = ml_dtypes.bfloat16
        t = np.arange(4096, dtype=np.int64)
        m = (t[:, None] * t[None, :]) % 4096
        ang = 2.0 * np.pi * m.astype(np.float64) / 4096.0
        c['dft_x'] = np.stack([(np.cos(ang) / 64.0).astype(np.float32).astype(bf), (np.sin(ang) / 64.0).astype(np.float32).astype(bf)], 0)
        t = np.arange(256, dtype=np.int64)
        ang = 2.0 * np.pi * ((t[:, None] * t[None, :]) % 256).astype(np.float64) / 256.0
        c['dft_c'] = np.stack([(np.cos(ang) / 16.0).astype(np.float32).astype(bf), (np.sin(ang) / 16.0).astype(np.float32).astype(bf)], 0)
        t = np.arange(128, dtype=np.int64)
        ang = 2.0 * np.pi * ((t[:, None] * t[None, :]) % 128).astype(np.float64) / 128.0
        c['dft_ch'] = np.stack([np.cos(ang) / np.sqrt(128.0), -np.sin(ang) / np.sqrt(128.0)], 0).astype(np.float32)
        j = np.arange(128)[:, None]
        i = np.arange(128)[None, :]
        c['tri'] = np.stack([(j <= i).astype(np.float32), (j >= i).astype(np.float32),
                             np.where(i >= j, 0.0, NEG).astype(np.float32), np.where(i <= j, 0.0, NEG).astype(np.float32)], 0)
        c['invcnt'] = np.stack([(1.0 / (_cnt(64, w)[:, None] * _cnt(64, w)[None, :])).reshape(-1) for w in POOL_WINDOWS], 0).astype(np.float32)
        c['invcnt_c'] = np.stack([1.0 / _cnt(256, w) for w in POOL_WINDOWS], 0).astype(np.float32)
    return c


def layer_weights_A(inp, l, s):
    w_in = inp['w_in'][l]
    cat = np.concatenate
    fm_cols = cat([np.arange(0, 512), 1024 + s * 256 + np.arange(256), 1536 + s * 128 + np.arange(128), 1792 + s * 128 + np.arange(128),
                   2576 + s * 256 + np.arange(256), 3088 + s * 256 + np.arange(256), 3600 + s * 256 + np.arange(256)])
    tm_cols = cat([512 + s * 256 + np.arange(256), 2048 + s * 256 + np.arange(256), 4112 + s * 256 + np.arange(256),
                   2560 + 4 * s + np.arange(4), 2568 + 4 * s + np.arange(4), 4624 + 2 * s + np.arange(2), 4628 + 2 * s + np.arange(2),
                   4632 + 2 * s + np.arange(2), 4636 + 2 * s + np.arange(2)])
    scw, scb, gcw = inp['ssd_conv_w'][l], inp['ssd_conv_b'][l], inp['gdn_conv_w'][l]
    ch_ssd = cat([s * 256 + np.arange(256), 512 + s * 128 + np.arange(128), 768 + s * 128 + np.arange(128)])
    ch_gdn = cat([s * 256 + np.arange(256), 512 + s * 256 + np.arange(256), 1024 + s * 256 + np.arange(256)])
    cw = np.zeros((10, 128, 6), np.float32)
    cw[0:4, :, 0:5] = scw[:, ch_ssd].T.reshape(4, 128, 5)
    cw[0:4, :, 5] = scb[ch_ssd].reshape(4, 128)
    cw[4:10, :, 0:5] = gcw[:, ch_gdn].T.reshape(6, 128, 5)
    rowp = np.zeros((5, 256), np.float32)
    rowp[0, 0:4] = inp['ssd_dt_bias'][l][0, 4 * s:4 * s + 4]
    rowp[0, 4:8] = inp['ssd_dt_bias'][l][1, 4 * s:4 * s + 4]
    rowp[0, 12:14] = inp['gdn_dt_bias'][l][0, 2 * s:2 * s + 2]
    rowp[0, 14:16] = inp['gdn_dt_bias'][l][1, 2 * s:2 * s + 2]
    rowp[0, 16:20] = inp['ssd_a_log'][l][0, 4 * s:4 * s + 4]
    rowp[0, 20:24] = inp['ssd_a_log'][l][1, 4 * s:4 * s + 4]
    rowp[0, 28:30] = inp['gdn_a_log'][l][0, 2 * s:2 * s + 2]
    rowp[0, 30:32] = inp['gdn_a_log'][l][1, 2 * s:2 * s + 2]
    rowp[1] = np.repeat(inp['ssd_d'][l][4 * s:4 * s + 4], 64)
    rowp[2] = inp['ssd_norm_w'][l][s * 256:(s + 1) * 256]
    rowp[3] = np.tile(inp['gdn_norm_w'][l], 2)
    c = consts_A()
    return {
        'w_mod': np.ascontiguousarray(inp['w_mod'][l][:, :2 * D]), 'b_mod': np.ascontiguousarray(inp['b_mod'][l][:2 * D]),
        'wa_fm': np.ascontiguousarray(w_in[:, fm_cols]), 'wa_tm': np.ascontiguousarray(w_in[:, tm_cols]),
        'convw': np.ascontiguousarray(cw.transpose(1, 0, 2)), 'pool_w': inp['pool_w'][l],
        'pool_scale': np.ascontiguousarray(inp['pool_scale'][l].reshape(4, 128).T), 'invcnt': c['invcnt'], 'invcnt_c': c['invcnt_c'],
        'rowp': rowp, 'dft_x': c['dft_x'], 'dft_c': c['dft_c'], 'dft_ch': c['dft_ch'], 'tri': c['tri'], 'ident': c['ident'],
    }


RG_PAIRS = [[0, 1], [2, 3], [4, 5], [6, 7]]
DEPTH = 4
O_CH = [(i * 6400, min((i + 1) * 6400, NTA * 1280)) for i in range(7)]
X_CH = [(i * 512, min((i + 1) * 512, NTOK)) for i in range(5)]


def _gathered_row(chunks, rank, j):
    for (a, b) in chunks:
        if a <= j < b:
            return 2 * a + rank * (b - a) + (j - a)
    raise ValueError(j)


def build_fused(depth=DEPTH):
    es = ExitStack()
    nc = bass.Bass("TRN2", target_bir_lowering=False)
    kb = KB(nc, es)
    kb.es_top = es
    di = lambda n, s, d=F32: nc.dram_tensor(n, list(s), d, kind="ExternalInput").ap()
    dint = lambda n, s, d=F32: nc.dram_tensor(n, list(s), d, addr_space="Local", kind="Internal").ap()
    C = {
        'cvec': di("cvec", [2, 128, 8]), 'ident': di("ident", [128, 128]), 'iota16': di("iota16", [128, 16]),
        'idxB': di("idxB", [128, 17, 16], U32), 'invcnt': di("invcnt", [4, 4096]), 'invcnt_c': di("invcnt_c", [4, 256]),
        'dft_x': di("dft_x", [2, 4096, 4096], BF16), 'dft_c': di("dft_c", [2, 256, 256], BF16), 'dft_ch': di("dft_ch", [2, 128, 128]),
        'tri': di("tri", [4, 128, 128]),
        'pub': dint("pub_scr", [16384, D], BF16), 'pvb': dint("pvb_scr", [16384, D], BF16),
        'x1': dint("x1_scr", [NTOK, D]), 'tm': dint("tm_scr", [TA, 784]), 'fm': dint("fm_scr", [1792, TA]),
    }
    for l in range(depth):
        C['w_mod_%d' % l] = di("w_mod_%d" % l, [D, 6 * D])
        C['b_mod_%d' % l] = di("b_mod_%d" % l, [6 * D])
    xa0 = di("xa0", [TA, D])
    xmy0 = di("xmy0", [NTOK, D])
    out_d = nc.dram_tensor("out", [NTOK, D], F32, kind="ExternalOutput").ap()
    O_d = dint("O_loc", [NTA * 1280, 128], BF16)
    G_d = dint("G_all", [2 * NTA * 1280, 128], BF16)
    xg = [xa0] + [dint("xg_%d" % l, [TA, D]) for l in range(1, depth)]
    xmy = [xmy0] + [dint("xmy_%d" % l, [NTOK, D]) for l in range(1, depth)]
    for l in range(depth):
        emit_A(nc, kb, l, C, xg[l], O_d)
        kb.collective("AllGather", [O_d[a:b, :] for (a, b) in O_CH], [G_d[2 * a:2 * b, :] for (a, b) in O_CH])
        o = out_d if l == depth - 1 else xmy[l + 1]
        emit_B(nc, kb, l, C, xmy[l], G_d, o)
        if l < depth - 1:
            kb.collective("AllGather", [xmy[l + 1][a:b, :] for (a, b) in X_CH], [xg[l + 1][2 * a:2 * b, :] for (a, b) in X_CH])
    kb.finish()
    return nc, es


def idx_table(s):
    idx = np.zeros((128, 17, 16), np.uint32)
    p = np.arange(128)
    for tau in range(17):
        T = s if tau == 0 else 2 + 16 * s + (tau - 1)
        for kc in range(16):
            br, j = kc // 4, kc % 4
            if br == 0:
                rs, row0 = 0, j * 128
            else:
                rs, row0 = j // 2, 512 + (br - 1) * 256 + (j % 2) * 128
            idx[:, tau, kc] = _gathered_row(O_CH, rs, T * 1280 + row0) + p
    return idx


_PROG = {}


def kernel(**inp):
    inp = {k: np.asarray(v) for k, v in inp.items()}
    depth = inp['w_in'].shape[0]
    if 'nc' not in _PROG:
        _PROG['nc'] = build_fused(depth)
    nc, _ = _PROG['nc']
    cA = consts_A()
    shared = {'ident': cA['ident'], 'iota16': cA['iota16'], 'invcnt': cA['invcnt'], 'invcnt_c': cA['invcnt_c'],
              'dft_x': cA['dft_x'], 'dft_c': cA['dft_c'], 'dft_ch': cA['dft_ch'], 'tri': cA['tri']}
    skipA = ('w_mod', 'b_mod', 'invcnt', 'invcnt_c', 'dft_x', 'dft_c', 'dft_ch', 'tri', 'ident')
    skipB = ('ident', 'iota16')
    per_half = []
    for s in range(2):
        m = {}
        for l in range(depth):
            for k, v in layer_weights_A(inp, l, s).items():
                if k not in skipA:
                    m['%s_%d' % (k, l)] = v
        m['idxB'] = idx_table(s)
        per_half.append(m)
    common = dict(shared)
    for l in range(depth):
        for k, v in layer_weights_B(inp, l).items():
            if k not in skipB:
                common['%s_%d' % (k, l)] = v
    x, ctx = inp['x'], inp['ctx']
    in_maps = []
    for core in range(8):
        b, s = core // 2, core % 2
        halves = [np.concatenate([ctx[b, r * 128:(r + 1) * 128], x[b, r * 2048:(r + 1) * 2048]], 0) for r in range(2)]
        m = dict(common)
        m.update(per_half[s])
        m['xa0'] = np.ascontiguousarray(np.concatenate([ctx[b], x[b]], 0))
        m['xmy0'] = np.ascontiguousarray(halves[s])
        m['cvec'] = cvec_layout(inp['c_ctx'], inp['c'][b])
        in_maps.append(m)
    res = run_bass_kernel_spmd(nc, in_maps, core_ids=list(range(8))).results
    out = np.empty_like(x)
    for core in range(8):
        b, s = core // 2, core % 2
        out[b, s * 2048:(s + 1) * 2048] = np.asarray(res[core]['out'])[128:]
    return out
```

```python
import numpy as np
from contextlib import ExitStack
import concourse.bass as bass
import concourse.mybir as mybir
from concourse.bass_utils import run_bass_kernel_spmd

F32 = mybir.dt.float32
BF16 = mybir.dt.bfloat16
U32 = mybir.dt.uint32
AF = mybir.ActivationFunctionType
ALU = mybir.AluOpType
AX = mybir.AxisListType

NDS = 64
NHW = 16


class KB:
    def __init__(self, nc, es):
        self.nc, self.es = nc, es
        self.engs = {'pe': nc.tensor, 'act': nc.scalar, 'dve': nc.vector, 'pool': nc.gpsimd, 'sp': nc.sync}
        self.sem, self.cnt = {}, {}
        for e in ('pe', 'act', 'dve', 'pool'):
            self.sem[('e', e)] = es.enter_context(nc.semaphore('s_' + e))
            self.cnt[e] = 0
        self.dcnt = [0] * NDS
        for i in range(NDS):
            self.sem[('d', i)] = es.enter_context(nc.semaphore('d%d' % i))
        self.dnext = 0
        self.dnext_sw = 0
        self.seen = {e: {} for e in self.engs}
        self.lastw, self.readers = {}, {}
        self.nbuf = 0

    def sb(self, shape, dt, name=None):
        self.nbuf += 1
        return self.es.enter_context(self.nc.sbuf_tensor('sb%d_' % self.nbuf + (name or 'b'), list(shape), dt))

    def ps(self, shape, dt, name=None):
        self.nbuf += 1
        return self.es.enter_context(self.nc.psum_tensor('ps%d_' % self.nbuf + (name or 'p'), list(shape), dt))

    def op(self, eng, fn, r=(), w=(), dma=False):
        deps = {}

        def need(tok):
            if tok is not None and deps.get(tok[0], 0) < tok[1]:
                deps[tok[0]] = tok[1]
        for key in r:
            need(self.lastw.get(key))
        for key in w:
            need(self.lastw.get(key))
            for k, v in self.readers.get(key, {}).items():
                need((k, v))
        E = self.engs[eng]
        if dma:
            if eng == 'pool':
                i = NHW + self.dnext_sw
                self.dnext_sw = (self.dnext_sw + 1) % (NDS - NHW)
            else:
                i = self.dnext
                self.dnext = (i + 1) % NHW
            if self.dcnt[i] > 0:
                need((('d', i), self.dcnt[i]))
        for k, v in deps.items():
            if eng == 'pe' and k == ('e', 'pe'):
                continue
            if self.seen[eng].get(k, 0) >= v:
                continue
            E.wait_ge(self.sem[k], v)
            self.seen[eng][k] = v
        inst = fn(E)
        if dma:
            self.dcnt[i] += 16
            inst.then_inc(self.sem[('d', i)], 16)
            tok = (('d', i), self.dcnt[i])
        else:
            self.cnt[eng] += 1
            inst.then_inc(self.sem[('e', eng)], 1)
            tok = (('e', eng), self.cnt[eng])
        for key in r:
            d = self.readers.setdefault(key, {})
            if d.get(tok[0], 0) < tok[1]:
                d[tok[0]] = tok[1]
        for key in w:
            self.lastw[key] = tok
            self.readers[key] = {}
        return tok

    def dma(self, out, in_, r=(), w=(), eng='sp', **kw):
        return self.op(eng, lambda E: E.dma_start(out=out, in_=in_, **kw), r=r, w=w, dma=True)

    def finish(self):
        E = self.engs['sp']
        for i in range(NDS):
            if self.dcnt[i] > 0 and self.seen['sp'].get(('d', i), 0) < self.dcnt[i]:
                E.wait_ge(self.sem[('d', i)], self.dcnt[i])
        for e in ('pe', 'act', 'dve', 'pool'):
            if self.cnt[e] > 0:
                E.wait_ge(self.sem[('e', e)], self.cnt[e])

    def barrier(self):
        for en, E in self.engs.items():
            for i in range(NDS):
                if self.dcnt[i] > 0 and self.seen[en].get(('d', i), 0) < self.dcnt[i]:
                    E.wait_ge(self.sem[('d', i)], self.dcnt[i])
                    self.seen[en][('d', i)] = self.dcnt[i]
            for e in ('pe', 'act', 'dve', 'pool'):
                if e != en and self.cnt[e] > 0 and self.seen[en].get(('e', e), 0) < self.cnt[e]:
                    E.wait_ge(self.sem[('e', e)], self.cnt[e])
                    self.seen[en][('e', e)] = self.cnt[e]
        self.lastw, self.readers = {}, {}


    def collective(self, kind, ins, outs):
        E = self.engs['pool']
        if not hasattr(self, 'ccsem'):
            self.ccsem = self.es_top.enter_context(self.nc.semaphore('cc_sem'))
            self.sem[('c', 0)] = self.ccsem
            self.cccnt = 0
        self.barrier()
        for i_, o_ in zip(ins, outs):
            E.collective_compute(kind, ALU.bypass, replica_groups=RG_PAIRS, ins=[i_], outs=[o_]).then_inc(self.ccsem)
            self.cccnt += 1
        for en, EE in self.engs.items():
            EE.wait_ge(self.ccsem, self.cccnt)
            self.seen[en][('c', 0)] = self.cccnt


D = 1024
ALPHA = 8 ** 0.25
EPS = 1e-6
NTOK = 2176
NEG = -1.0e30


def ln_normalize(kb, xin, xkey, hn, hnkey, st, mv, rstd):
    kb.op('dve', lambda E: E.bn_stats(out=st[:, 0, :], in_=xin[:, 0:512]), r=[xkey], w=['st0'])
    kb.op('dve', lambda E: E.bn_stats(out=st[:, 1, :], in_=xin[:, 512:1024]), r=[xkey], w=['st1'])
    kb.op('dve', lambda E: E.bn_aggr(out=mv[:, :], in_=st[:, :, :]), r=['st0', 'st1'], w=['mv'])
    kb.op('dve', lambda E: E.tensor_scalar(out=rstd[:, :], in0=mv[:, 1:2], scalar1=EPS, scalar2=None,
                                           op0=ALU.add), r=['mv'], w=['rstd'])
    kb.op('act', lambda E: E.activation(out=rstd[:, :], in_=rstd[:, :], func=AF.Sqrt), r=['rstd'], w=['rstd'])
    kb.op('dve', lambda E: E.reciprocal(out=rstd[:, :], in_=rstd[:, :]), r=['rstd'], w=['rstd'])
    kb.op('dve', lambda E: E.tensor_scalar(out=hn, in0=xin, scalar1=mv[:, 0:1], scalar2=rstd[:, 0:1],
                                           op0=ALU.subtract, op1=ALU.mult), r=[xkey, 'mv', 'rstd'], w=[hnkey])


def emit_B(nc, kb, l, C, x_d, G_d, out_d):
    es = ExitStack()
    kb.es = es
    dt = lambda n, s, d=F32, kind="ExternalInput": nc.dram_tensor(n + "_%d" % l, list(s), d, kind=kind).ap()
    cvec_d = C['cvec']
    wmod_d = C['w_mod_%d' % l]
    bmod_d = C['b_mod_%d' % l]
    wg_d = dt("wg", [8, D, 512])
    wb_d = dt("wb", [2048, D])
    wo_d = dt("wo", [D, D])
    lnp_d = dt("lnp", [4, D])
    wq_d = dt("wq", [D, 2048])
    keysT_d = dt("keysT", [128, 16, 128])
    pub_d, pvb_d = C['pub'], C['pvb']
    ident_d, iota_d, x1_d, idx_d = C['ident'], C['iota16'], C['x1'], C['idxB']
    idxb = kb.sb([128, 17, 16], U32, "idxb")
    kb.dma(idxb[:, :, :], idx_d[:, :, :], w=['idxb'])

    ident_f = kb.sb([128, 128], F32, "ident_f")
    ident_b = kb.sb([128, 128], BF16, "ident_b")
    ones_b = kb.sb([128, 128], F32, "ones_f")
    iota16 = kb.sb([128, 16], F32, "iota16")
    rows = kb.sb([128, 2, 6, D], F32, "modrows")
    lnrow = kb.sb([128, 4, D], F32, "lnrow")
    es0 = ExitStack()
    kb.es = es0
    cs = kb.sb([128, 2, 8], F32, "cs")
    csrep = kb.sb([128, 2, 8, 128], BF16, "csrep")
    bmrow = kb.sb([128, 512], F32, "bmrow")
    wmb = [kb.sb([128, 8, 512], BF16, "wmb%d" % i) for i in range(2)]
    pmod = [kb.ps([128, 512], F32, "pmod%d" % i) for i in range(2)]

    kb.dma(ident_f[:, :], ident_d[:, :], w=['ident_f'])
    kb.dma(iota16[:, :], iota_d[:, :], w=['iota16'])
    kb.op('dve', lambda E: E.tensor_copy(out=ident_b[:, :], in_=ident_f[:, :]), r=['ident_f'], w=['ident_b'])
    kb.op('pool', lambda E: E.memset(ones_b[:, :], 1.0), w=['ones'])
    for v in range(2):
        kb.dma(cs[:, v, :], cvec_d[v, :, :], w=['cs%d' % v])
        kb.op('act', lambda E: E.activation(out=cs[:, v, :], in_=cs[:, v, :], func=AF.Silu), r=['cs%d' % v], w=['cs%d' % v])
        for k in range(8):
            kb.op('dve', lambda E: E.tensor_scalar(out=csrep[:, v, k, :], in0=ones_b[:, :], scalar1=cs[:, v, k:k + 1],
                                                   scalar2=None, op0=ALU.mult), r=['cs%d' % v, 'ones'], w=['csrep%d' % v])
    for j in range(4):
        kb.dma(lnrow[:, j, :], lnp_d[j, :].partition_broadcast(128), w=['lnrow'])
    ci = 0
    for j in range(6):
        for hf in range(2):
            c0 = j * D + hf * 512
            wbuf = wmb[ci % 2]
            wk = 'wmb%d' % (ci % 2)
            kb.dma(wbuf[:, :, :], wmod_d[:, c0:c0 + 512].rearrange("(k p) n -> p k n", p=128), w=[wk], eng='pool')
            kb.dma(bmrow[:, :], bmod_d[c0:c0 + 512].partition_broadcast(128), w=['bmrow'])
            for v in range(2):
                pk = 'pmod%d' % v
                for k in range(8):
                    kb.op('pe', lambda E: E.matmul(pmod[v][:, :], lhsT=csrep[:, v, k, :], rhs=wbuf[:, k, :],
                                                   start=(k == 0), stop=(k == 7)), r=['csrep%d' % v, wk], w=[pk])
                if j in (1, 4):
                    kb.op('dve', lambda E: E.scalar_tensor_tensor(out=rows[:, v, j, hf * 512:(hf + 1) * 512], in0=pmod[v][:, :],
                                                                  scalar=1.0, in1=bmrow[:, :], op0=ALU.add, op1=ALU.add),
                          r=[pk, 'bmrow'], w=['rows'])
                else:
                    kb.op('dve', lambda E: E.tensor_tensor(out=rows[:, v, j, hf * 512:(hf + 1) * 512], in0=pmod[v][:, :],
                                                           in1=bmrow[:, :], op=ALU.add), r=[pk, 'bmrow'], w=['rows'])
            ci += 1
    kb.barrier()
    es0.close()

    es1 = ExitStack()
    kb1 = kb
    kb.es = es1
    wb = kb.sb([128, 16, D], BF16, "wb")
    wo = kb.sb([128, 8, D], BF16, "wo")
    wg = [kb.sb([128, 8, 512], BF16, "wg%d" % i) for i in range(2)]
    xb = kb.sb([128, 4, D], F32, "xb")
    hn = kb.sb([128, D], F32, "hn")
    t1 = kb.sb([128, D], F32, "t1")
    hb = kb.sb([128, D], BF16, "hb")
    hT = kb.sb([128, 8, 512], BF16, "hT")
    brTb = kb.sb([128, 16, 512], BF16, "brTb")
    gs = kb.sb([128, 4, 512], BF16, "gs")
    tb = kb.sb([128, 4, 512], F32, "tb")
    mT = kb.sb([128, 8, 512], BF16, "mT")
    st = kb.sb([128, 2, 6], F32, "st")
    mv = kb.sb([128, 2], F32, "mv")
    rstd = kb.sb([128, 1], F32, "rstd")
    pT = kb.ps([128, 8, 128], BF16, "pT")
    pG = [kb.ps([128, 512], F32, "pG%d" % i) for i in range(2)]
    pP = [kb.ps([128, 512], F32, "pP%d" % i) for i in range(2)]
    pM = kb.ps([128, D], F32, "pM")

    kb.dma(wb[:, :, :], wb_d.rearrange("(k p) n -> p k n", p=128), w=['wb'], eng='pool')
    kb.dma(wo[:, :, :], wo_d.rearrange("(k p) n -> p k n", p=128), w=['wo'], eng='pool')

    blocks = [(0, 128, 0)] + [(128 + 512 * i, 512, 1) for i in range(4)]
    wgi = 0
    for (t0, NT, v) in blocks:
        nt = NT // 128
        for kc in range(16):
            for ti in range(nt):
                tau = t0 // 128 + ti
                kb.op('pool', lambda E: E.indirect_dma_start(out=brTb[:, kc, ti * 128:(ti + 1) * 128], out_offset=None, in_=G_d[:, :],
                                                             in_offset=bass.IndirectOffsetOnAxis(ap=idxb[:, tau, kc:kc + 1], axis=0)),
                      r=['idxb', 'G_d'], w=['brTb'], dma=True)
        for ti in range(nt):
            xk = 'xb%d' % ti
            kb.dma(xb[:, ti, :], x_d[t0 + ti * 128:t0 + (ti + 1) * 128, :], r=['x_in'], w=[xk])
            ln_normalize(kb, xb[:, ti, :], xk, hn[:, :], 'hn', st, mv, rstd)
            kb.op('pool', lambda E: E.tensor_tensor(out=t1[:, :], in0=hn[:, :], in1=rows[:, v, 1, :], op=ALU.mult), r=['hn'], w=['t1'])
            kb.op('pool', lambda E: E.tensor_tensor(out=hb[:, :], in0=t1[:, :], in1=rows[:, v, 0, :], op=ALU.add), r=['t1'], w=['hb'])
            for k in range(8):
                kb.op('pe', lambda E: E.transpose(out=pT[:, k, :], in_=hb[:, k * 128:(k + 1) * 128], identity=ident_b[:, :]), r=['hb'], w=['pT'])
            kb.op('act', lambda E: E.copy(out=hT[:, :, ti * 128:(ti + 1) * 128], in_=pT[:, :, :]), r=['pT'], w=['hT'])
        for oc in range(8):
            wgb = wg[wgi % 2]
            wgk = 'wg%d' % (wgi % 2)
            wgi += 1
            kb.dma(wgb[:, :, :], wg_d[oc].rearrange("(k p) n -> p k n", p=128), w=[wgk], eng='pool')
            for br in range(4):
                pg, pgk = pG[br % 2], 'pG%d' % (br % 2)
                pp, ppk = pP[br % 2], 'pP%d' % (br % 2)
                for k in range(8):
                    kb.op('pe', lambda E: E.matmul(pg[:, 0:NT], lhsT=wgb[:, k, br * 128:(br + 1) * 128], rhs=hT[:, k, 0:NT],
                                                   start=(k == 0), stop=(k == 7)), r=[wgk, 'hT'], w=[pgk])
                kb.op('act', lambda E: E.activation(out=gs[:, br, 0:NT], in_=pg[:, 0:NT], func=AF.Sigmoid), r=[pgk], w=['gs%d' % br])
                for k in range(4):
                    kb.op('pe', lambda E: E.matmul(pp[:, 0:NT], lhsT=wb[:, br * 4 + k, oc * 128:(oc + 1) * 128], rhs=brTb[:, br * 4 + k, 0:NT],
                                                   start=(k == 0), stop=(k == 3)), r=['wb', 'brTb'], w=[ppk])
                kb.op('dve', lambda E: E.tensor_tensor(out=tb[:, br, 0:NT], in0=pp[:, 0:NT], in1=gs[:, br, 0:NT], op=ALU.mult),
                      r=[ppk, 'gs%d' % br], w=['tb%d' % br])
            kb.op('pool', lambda E: E.tensor_tensor(out=tb[:, 0, 0:NT], in0=tb[:, 0, 0:NT], in1=tb[:, 1, 0:NT], op=ALU.add), r=['tb0', 'tb1'], w=['tb0'])
            kb.op('pool', lambda E: E.tensor_tensor(out=tb[:, 2, 0:NT], in0=tb[:, 2, 0:NT], in1=tb[:, 3, 0:NT], op=ALU.add), r=['tb2', 'tb3'], w=['tb2'])
            kb.op('pool', lambda E: E.tensor_tensor(out=mT[:, oc, 0:NT], in0=tb[:, 0, 0:NT], in1=tb[:, 2, 0:NT], op=ALU.add), r=['tb0', 'tb2'], w=['mT'])
        for ti in range(nt):
            xk = 'xb%d' % ti
            for hf in range(2):
                for k in range(8):
                    kb.op('pe', lambda E: E.matmul(pM[:, hf * 512:(hf + 1) * 512], lhsT=mT[:, k, ti * 128:(ti + 1) * 128],
                                                   rhs=wo[:, k, hf * 512:(hf + 1) * 512], start=(k == 0), stop=(k == 7)), r=['mT', 'wo'], w=['pM'])
            kb.op('dve', lambda E: E.tensor_tensor(out=t1[:, :], in0=pM[:, :], in1=rows[:, v, 2, :], op=ALU.mult), r=['pM'], w=['t1'])
            kb.op('dve', lambda E: E.scalar_tensor_tensor(out=t1[:, :], in0=xb[:, ti, :], scalar=ALPHA, in1=t1[:, :], op0=ALU.mult, op1=ALU.add),
                  r=[xk, 't1'], w=['t1'])
            ln_normalize(kb, t1[:, :], 't1', hn[:, :], 'hn', st, mv, rstd)
            kb.op('pool', lambda E: E.tensor_tensor(out=hn[:, :], in0=hn[:, :], in1=lnrow[:, 0, :], op=ALU.mult), r=['hn'], w=['hn'])
            kb.op('pool', lambda E: E.tensor_tensor(out=xb[:, ti, :], in0=hn[:, :], in1=lnrow[:, 1, :], op=ALU.add), r=['hn'], w=[xk])
            kb.dma(x1_d[t0 + ti * 128:t0 + (ti + 1) * 128, :], xb[:, ti, :], r=[xk], w=['x1d'])
    kb.barrier()
    es1.close()

    es2 = ExitStack()
    kb.es = es2
    wq = kb.sb([128, 8, 2048], BF16, "wq")
    keysT = kb.sb([128, 16, 128], BF16, "keysTb")
    xt = kb.sb([128, D], F32, "xt")
    hn = kb.sb([128, D], F32, "hn2")
    h2b = kb.sb([128, D], BF16, "h2b")
    h2T = kb.sb([128, 8, 128], BF16, "h2T")
    qT = kb.sb([128, 16, 128], BF16, "qT")
    sc = kb.sb([128, 16, 128], F32, "sc")
    sc2 = sc
    vv = kb.sb([128, 16, 16], F32, "vv")
    ix = kb.sb([128, 16, 16], U32, "ix")
    ixf = kb.sb([128, 16, 16], F32, "ixf")
    cand = kb.sb([128, 8, 256], F32, "cand")
    cand2 = cand
    best = kb.sb([128, 8, 16], F32, "best")
    pos = kb.sb([128, 8, 16], U32, "pos")
    pa = kb.sb([128, 8, 16], U32, "pa")
    pb_ = kb.sb([128, 8, 16], U32, "pb")
    paf = kb.sb([128, 2, 8, 16], F32, "paf")
    eq = kb.sb([128, 8, 16, 16], F32, "eq")
    sel = kb.sb([128, 2, 8, 16], F32, "sel")
    eidf = kb.sb([128, 128], F32, "eidf")
    eid = kb.sb([128, 128], U32, "eid")
    gate = kb.sb([128, 8, 16], F32, "gate")
    gsum = kb.sb([128, 8], F32, "gsum")
    act = kb.sb([128, 128], F32, "actv")
    wgt = kb.sb([128, 128], F32, "wgt")
    junk = kb.sb([128, D], BF16, "junk")
    Wd = [kb.sb([128, 16, 128], BF16, "Wd%d" % i) for i in range(2)]
    wgtb = kb.sb([128, 128], BF16, "wgtb")
    t1 = kb.sb([128, D], F32, "t1b")
    st = kb.sb([128, 2, 6], F32, "st2")
    mv = kb.sb([128, 2], F32, "mv2")
    rstd = kb.sb([128, 1], F32, "rstd2")
    NG = 16
    ug = [kb.sb([128, D], BF16, "ug%d" % i) for i in range(NG)]
    pT = kb.ps([128, 8, 128], BF16, "pT2")
    pQ = [kb.ps([128, 512], F32, "pQ0")]
    pS = kb.ps([128, 16, 128], F32, "pS")
    pV = kb.ps([128, D], F32, "pV")

    kb.dma(wq[:, :, :], wq_d.rearrange("(k p) n -> p k n", p=128), w=['wq'], eng='pool')
    kb.dma(keysT[:, :, :], keysT_d[:, :, :], w=['keysT'], eng='pool')
    xt2 = [xt, kb.sb([128, D], F32, 'xt_b')]
    h2b2 = [h2b, kb.sb([128, D], BF16, 'h2b_b')]
    eid2 = [eid, kb.sb([128, 128], U32, 'eid_b')]
    gate2 = [gate, kb.sb([128, 8, 16], F32, 'gate_b')]
    gstate = {'gi': 0}

    def topk(ti):
        v = 0 if ti == 0 else 1
        r0 = ti * 128
        q = ti % 2
        xtq, h2bq, eidq, gateq = xt2[q], h2b2[q], eid2[q], gate2[q]
        xk, hk, ek, gtk = 'xt%d' % q, 'h2b%d' % q, 'eid%d' % q, 'gate%d' % q
        kb.dma(xtq[:, :], x1_d[r0:r0 + 128, :], r=['x1d'], w=[xk])
        ln_normalize(kb, xtq[:, :], xk, hn[:, :], 'hn', st, mv, rstd)
        kb.op('pool', lambda E: E.tensor_tensor(out=t1[:, :], in0=hn[:, :], in1=rows[:, v, 4, :], op=ALU.mult), r=['hn'], w=['t1'])
        kb.op('pool', lambda E: E.tensor_tensor(out=h2bq[:, :], in0=t1[:, :], in1=rows[:, v, 3, :], op=ALU.add), r=['t1'], w=[hk])
        for k in range(8):
            kb.op('pe', lambda E: E.transpose(out=pT[:, k, :], in_=h2bq[:, k * 128:(k + 1) * 128], identity=ident_b[:, :]), r=[hk], w=['pT'])
        kb.op('act', lambda E: E.copy(out=h2T[:, :, :], in_=pT[:, :, :]), r=['pT'], w=['h2T'])
        yield
        for g4 in range(4):
            pq, pqk = pQ[0], 'pQ0'
            for j in range(4):
                hs = g4 * 4 + j
                for k in range(8):
                    kb.op('pe', lambda E: E.matmul(pq[:, j * 128:(j + 1) * 128], lhsT=wq[:, k, hs * 128:(hs + 1) * 128], rhs=h2T[:, k, :],
                                                   start=(k == 0), stop=(k == 7)), r=['wq', 'h2T'], w=[pqk])
            kb.op('act', lambda E: E.copy(out=qT[:, g4 * 4:(g4 + 1) * 4, :], in_=pq[:, :].rearrange("p (j t) -> p j t", j=4)), r=[pqk], w=['qT%d' % g4])
        yield
        for hs in range(16):
            kb.op('pe', lambda E: E.matmul(pS[:, hs, :], lhsT=qT[:, hs, :], rhs=keysT[:, hs, :], start=True, stop=True),
                  r=['qT%d' % (hs // 4), 'keysT'], w=['pS'])
        kb.op('act', lambda E: E.copy(out=sc[:, :, :], in_=pS[:, :, :]), r=['pS'], w=['sc%d' % i for i in range(16)])
        yield
        for hs in range(16):
            sk = 'sc%d' % hs
            kb.op('dve', lambda E: E.max(out=vv[:, hs, 0:8], in_=sc[:, hs, :]), r=[sk], w=['vva%d' % hs])
            kb.op('dve', lambda E: E.max_index(out=ix[:, hs, 0:8], in_max=vv[:, hs, 0:8], in_values=sc[:, hs, :]), r=[sk, 'vva%d' % hs], w=['ixa%d' % hs])
            kb.op('dve', lambda E: E.match_replace(out=sc2[:, hs, :], in_to_replace=vv[:, hs, 0:8], in_values=sc[:, hs, :], imm_value=NEG),
                  r=[sk, 'vva%d' % hs], w=[sk])
            kb.op('dve', lambda E: E.max(out=vv[:, hs, 8:16], in_=sc2[:, hs, :]), r=[sk], w=['vvb%d' % hs])
            kb.op('dve', lambda E: E.max_index(out=ix[:, hs, 8:16], in_max=vv[:, hs, 8:16], in_values=sc2[:, hs, :]), r=[sk, 'vvb%d' % hs], w=['ixb%d' % hs])
        yield
        allv = ['vva%d' % i for i in range(16)] + ['vvb%d' % i for i in range(16)]
        alli = ['ixa%d' % i for i in range(16)] + ['ixb%d' % i for i in range(16)]
        kb.op('dve', lambda E: E.tensor_copy(out=ixf[:, :, :], in_=ix[:, :, :]), r=alli, w=['ixf'])
        v4 = vv[:, :, :].rearrange("p (h s) a -> p h s a", s=2)
        yield
        for h in range(8):
            kb.op('dve', lambda E: E.tensor_tensor(out=cand[:, h, :].rearrange("p (a b) -> p a b", a=16),
                                                   in0=vv[:, 2 * h, :].unsqueeze(2).to_broadcast([128, 16, 16]),
                                                   in1=vv[:, 2 * h + 1, :].unsqueeze(1).to_broadcast([128, 16, 16]), op=ALU.add), r=allv, w=['cand%d' % h])
            ck = 'cand%d' % h
            kb.op('dve', lambda E: E.max(out=best[:, h, 0:8], in_=cand[:, h, :]), r=[ck], w=['besta%d' % h])
            kb.op('dve', lambda E: E.max_index(out=pos[:, h, 0:8], in_max=best[:, h, 0:8], in_values=cand[:, h, :]), r=[ck, 'besta%d' % h], w=['posa%d' % h])
            kb.op('dve', lambda E: E.match_replace(out=cand2[:, h, :], in_to_replace=best[:, h, 0:8], in_values=cand[:, h, :], imm_value=NEG),
                  r=[ck, 'besta%d' % h], w=[ck])
            kb.op('dve', lambda E: E.max(out=best[:, h, 8:16], in_=cand2[:, h, :]), r=[ck], w=['bestb%d' % h])
            kb.op('dve', lambda E: E.max_index(out=pos[:, h, 8:16], in_max=best[:, h, 8:16], in_values=cand2[:, h, :]), r=[ck, 'bestb%d' % h], w=['posb%d' % h])
        yield
        allb = ['besta%d' % i for i in range(8)] + ['bestb%d' % i for i in range(8)]
        allp = ['posa%d' % i for i in range(8)] + ['posb%d' % i for i in range(8)]
        kb.op('dve', lambda E: E.tensor_scalar(out=pa[:, :, :], in0=pos[:, :, :], scalar1=4, scalar2=None, op0=ALU.logical_shift_right), r=allp, w=['pa'])
        kb.op('dve', lambda E: E.tensor_scalar(out=pb_[:, :, :], in0=pos[:, :, :], scalar1=15, scalar2=None, op0=ALU.bitwise_and), r=allp, w=['pb'])
        kb.op('dve', lambda E: E.tensor_copy(out=paf[:, 0, :, :], in_=pa[:, :, :]), r=['pa'], w=['paf0'])
        kb.op('dve', lambda E: E.tensor_copy(out=paf[:, 1, :, :], in_=pb_[:, :, :]), r=['pb'], w=['paf1'])
        yield
        for s_ in range(2):
            for h in range(8):
                kb.op('dve', lambda E: E.tensor_tensor(out=eq[:, h, :, :], in0=iota16[:, :].unsqueeze(1).to_broadcast([128, 16, 16]),
                                                       in1=paf[:, s_, h, :].unsqueeze(2).to_broadcast([128, 16, 16]), op=ALU.is_equal),
                      r=['paf%d' % s_, 'iota16'], w=['eq%d' % h])
                kb.op('dve', lambda E: E.tensor_tensor(out=eq[:, h, :, :], in0=eq[:, h, :, :],
                                                       in1=ixf[:, 2 * h + s_, :].unsqueeze(1).to_broadcast([128, 16, 16]), op=ALU.mult),
                      r=['eq%d' % h, 'ixf'], w=['eq%d' % h])
            kb.op('dve', lambda E: E.tensor_reduce(out=sel[:, s_, :, :], in_=eq[:, :, :, :], axis=AX.X, op=ALU.add), r=['eq%d' % h for h in range(8)], w=['sel%d' % s_])
        kb.op('dve', lambda E: E.scalar_tensor_tensor(out=eidf[:, :], in0=sel[:, 0, :, :].rearrange("p h k -> p (h k)"), scalar=128.0,
                                                      in1=sel[:, 1, :, :].rearrange("p h k -> p (h k)"), op0=ALU.mult, op1=ALU.add), r=['sel0', 'sel1'], w=['eidf'])
        kb.op('dve', lambda E: E.tensor_copy(out=eidq[:, :], in_=eidf[:, :]), r=['eidf'], w=[ek])
        yield
        kb.op('dve', lambda E: E.tensor_tensor(out=gateq[:, :, :], in0=best[:, :, :], in1=best[:, :, 0:1].to_broadcast([128, 8, 16]), op=ALU.subtract), r=allb, w=[gtk])
        kb.op('act', lambda E: E.activation(out=gateq[:, :, :], in_=gateq[:, :, :], func=AF.Exp), r=[gtk], w=[gtk])
        kb.op('dve', lambda E: E.tensor_reduce(out=gsum[:, :], in_=gateq[:, :, :], axis=AX.X, op=ALU.add), r=[gtk], w=['gsum'])
        kb.op('dve', lambda E: E.reciprocal(out=gsum[:, :], in_=gsum[:, :]), r=['gsum'], w=['gsum'])
        kb.op('dve', lambda E: E.tensor_tensor(out=gateq[:, :, :], in0=gateq[:, :, :], in1=gsum[:, :].unsqueeze(2).to_broadcast([128, 8, 16]), op=ALU.mult), r=[gtk, 'gsum'], w=[gtk])
        yield

    def gphase(ti, nxt):
        v = 0 if ti == 0 else 1
        r0 = ti * 128
        q = ti % 2
        xtq, h2bq, eidq, gateq = xt2[q], h2b2[q], eid2[q], gate2[q]
        xk, hk, ek, gtk = 'xt%d' % q, 'h2b%d' % q, 'eid%d' % q, 'gate%d' % q
        gi = gstate['gi']
        for r_ in range(128):
            gb, gk = ug[gi % NG], 'ug%d' % (gi % NG)
            gi += 1
            if r_ % 16 == 15 and nxt is not None:
                next(nxt, None)
            kb.op('pool', lambda E: E.indirect_dma_start(out=gb[:, :], out_offset=None, in_=pub_d[:, :],
                                                         in_offset=bass.IndirectOffsetOnAxis(ap=eidq[:, r_:r_ + 1], axis=0)),
                  r=[ek], w=[gk], dma=True)
            kb.op('dve', lambda E: E.scalar_tensor_tensor(out=junk[:, :], in0=gb[:, :], scalar=1.0, in1=h2bq[:, :], op0=ALU.mult, op1=ALU.mult,
                                                          accum_out=act[:, r_:r_ + 1]), r=[gk, hk], w=['junk', 'act'])
        kb.op('act', lambda E: E.activation(out=act[:, :], in_=act[:, :], func=AF.Gelu), r=['act'], w=['act'])
        kb.op('dve', lambda E: E.tensor_tensor(out=wgtb[:, :], in0=act[:, :], in1=gateq[:, :, :].rearrange("p h k -> p (h k)"), op=ALU.mult), r=['act', gtk], w=['wgtb'])
        for r_ in range(128):
            gb, gk = ug[gi % NG], 'ug%d' % (gi % NG)
            gi += 1
            if r_ % 16 == 15 and nxt is not None:
                next(nxt, None)
            if r_ % 16 == 0:
                wdb, wdk = Wd[(r_ // 16) % 2], 'Wd%d' % ((r_ // 16) % 2)
                kb.op('dve', lambda E: E.tensor_tensor(out=wdb[:, :, :], in0=ident_b[:, :].unsqueeze(1).to_broadcast([128, 16, 128]),
                                                       in1=wgtb[:, r_:r_ + 16].unsqueeze(2).to_broadcast([128, 16, 128]), op=ALU.mult),
                      r=['ident_b', 'wgtb'], w=[wdk])
            kb.op('pool', lambda E: E.indirect_dma_start(out=gb[:, :], out_offset=None, in_=pvb_d[:, :],
                                                         in_offset=bass.IndirectOffsetOnAxis(ap=eidq[:, r_:r_ + 1], axis=0)),
                  r=[ek], w=[gk], dma=True)
            for hf in range(2):
                kb.op('pe', lambda E: E.matmul(pV[:, hf * 512:(hf + 1) * 512], lhsT=wdb[:, r_ % 16, :], rhs=gb[:, hf * 512:(hf + 1) * 512],
                                               start=(r_ == 0), stop=(r_ == 127)), r=[wdk, gk], w=['pV'])
        kb.op('dve', lambda E: E.tensor_tensor(out=t1[:, :], in0=pV[:, :], in1=rows[:, v, 5, :], op=ALU.mult), r=['pV'], w=['t1'])
        kb.op('dve', lambda E: E.scalar_tensor_tensor(out=t1[:, :], in0=xtq[:, :], scalar=ALPHA, in1=t1[:, :], op0=ALU.mult, op1=ALU.add), r=[xk, 't1'], w=['t1'])
        ln_normalize(kb, t1[:, :], 't1', hn[:, :], 'hn', st, mv, rstd)
        kb.op('pool', lambda E: E.tensor_tensor(out=hn[:, :], in0=hn[:, :], in1=lnrow[:, 2, :], op=ALU.mult), r=['hn'], w=['hn'])
        kb.op('pool', lambda E: E.tensor_tensor(out=t1[:, :], in0=hn[:, :], in1=lnrow[:, 3, :], op=ALU.add), r=['hn'], w=['t1'])
        kb.dma(out_d[r0:r0 + 128, :], t1[:, :], r=['t1'], w=['outd'])
        gstate['gi'] = gi

    NTI = NTOK // 128
    for _ in topk(0):
        pass
    for ti in range(NTI):
        nxt = topk(ti + 1) if ti + 1 < NTI else None
        gphase(ti, nxt)
        if nxt is not None:
            for _ in nxt:
                pass
    kb.barrier()
    es2.close()
    kb.barrier()
    es.close()


GATE0 = 8736 - 4096
_CONST = {}


def consts():
    if not _CONST:
        _CONST['ident'] = np.eye(128, dtype=np.float32)
        _CONST['iota16'] = np.tile(np.arange(16, dtype=np.float32)[None, :], (128, 1))
    return _CONST


def layer_weights_B(inp, l):
    w_in = inp['w_in'][l]
    wg = np.ascontiguousarray(w_in[:, GATE0:].reshape(D, 4, 8, 128).transpose(2, 0, 1, 3).reshape(8, D, 512))
    keysT = np.ascontiguousarray(inp['peer_keys'][l].reshape(16, 128, 128).transpose(2, 0, 1))
    c = consts()
    return {
        'w_mod': inp['w_mod'][l], 'b_mod': inp['b_mod'][l], 'wg': wg,
        'wb': np.ascontiguousarray(inp['w_branch'][l].reshape(2048, D)), 'wo': inp['w_out'][l],
        'lnp': np.stack([inp['ln1_g'][l], inp['ln1_b'][l], inp['ln2_g'][l], inp['ln2_b'][l]], 0),
        'wq': inp['peer_wq'][l], 'keysT': keysT, 'peer_u': inp['peer_u'][l], 'peer_v': inp['peer_v'][l],
        'ident': c['ident'], 'iota16': c['iota16'],
    }


def cvec_layout(c_ctx, c_b):
    return np.ascontiguousarray(np.stack([c_ctx, c_b], 0).reshape(2, 8, 128).transpose(0, 2, 1))


TA = 4352
NTA = 34
POOL_WINDOWS = (2, 4, 8, 16)


def emit_A(nc, kb, l, C, xa_d, O_d):
    es = ExitStack()
    kb.es = es
    dt = lambda n, s, d=F32, kind="ExternalInput": nc.dram_tensor(n + "_%d" % l, list(s), d, kind=kind).ap()
    cvec_d = C['cvec']
    wmod_d = C['w_mod_%d' % l]
    bmod_d = C['b_mod_%d' % l]
    wfm_d = dt("wa_fm", [D, 1792])
    wtm_d = dt("wa_tm", [D, 784])
    cw_d = dt("convw", [128, 10, 6])
    poolw_d = dt("pool_w", [4, 128, 128])
    pscale_d = dt("pool_scale", [128, 4])
    invc_d, invcc_d = C['invcnt'], C['invcnt_c']
    rowp_d = dt("rowp", [5, 256])
    dftx_d, dftc_d, dfch_d, tri_d, ident_d, tm_d, fm_d = C['dft_x'], C['dft_c'], C['dft_ch'], C['tri'], C['ident'], C['tm'], C['fm']
    Ov = O_d.rearrange("(t r) c -> r t c", r=1280)
    xrow = (lambda q: q * 128) if l == 0 else (lambda q: _gathered_row(X_CH, *((0, 0) if q == 0 else ((1, 0) if q == 1 else ((0, 128 + (q - 2) * 128) if q < 18 else (1, 128 + (q - 18) * 128))))))

    ident_f = kb.sb([128, 128], F32, "ident_f")
    ident_b = kb.sb([128, 128], BF16, "ident_b")
    ones_f = kb.sb([128, 128], F32, "ones_f")
    kb.dma(ident_f[:, :], ident_d[:, :], w=['ident_f'])
    kb.op('dve', lambda E: E.tensor_copy(out=ident_b[:, :], in_=ident_f[:, :]), r=['ident_f'], w=['ident_b'])
    kb.op('pool', lambda E: E.memset(ones_f[:, :], 1.0), w=['ones'])

    es0 = ExitStack()
    kb.es = es0
    rows = kb.sb([128, 2, 2, D], F32, "modrows")
    cs = kb.sb([128, 2, 8], F32, "cs")
    csrep = kb.sb([128, 2, 8, 128], BF16, "csrep")
    bmrow = kb.sb([128, 512], F32, "bmrow")
    wmb = [kb.sb([128, 8, 512], BF16, "wmb%d" % i) for i in range(2)]
    pmod = [kb.ps([128, 512], F32, "pmod%d" % i) for i in range(2)]
    for v in range(2):
        kb.dma(cs[:, v, :], cvec_d[v, :, :], w=['cs%d' % v])
        kb.op('act', lambda E: E.activation(out=cs[:, v, :], in_=cs[:, v, :], func=AF.Silu), r=['cs%d' % v], w=['cs%d' % v])
        for k in range(8):
            kb.op('dve', lambda E: E.tensor_scalar(out=csrep[:, v, k, :], in0=ones_f[:, :], scalar1=cs[:, v, k:k + 1],
                                                   scalar2=None, op0=ALU.mult), r=['cs%d' % v, 'ones'], w=['csrep%d' % v])
    ci = 0
    for j in range(2):
        for hf in range(2):
            c0 = j * D + hf * 512
            wbuf, wk = wmb[ci % 2], 'wmb%d' % (ci % 2)
            kb.dma(wbuf[:, :, :], wmod_d[:, c0:c0 + 512].rearrange("(k p) n -> p k n", p=128), w=[wk], eng='pool')
            kb.dma(bmrow[:, :], bmod_d[c0:c0 + 512].partition_broadcast(128), w=['bmrow'])
            for v in range(2):
                pk = 'pmod%d' % v
                for k in range(8):
                    kb.op('pe', lambda E: E.matmul(pmod[v][:, :], lhsT=csrep[:, v, k, :], rhs=wbuf[:, k, :],
                                                   start=(k == 0), stop=(k == 7)), r=['csrep%d' % v, wk], w=[pk])
                kb.op('dve', lambda E: E.scalar_tensor_tensor(out=rows[:, v, j, hf * 512:(hf + 1) * 512], in0=pmod[v][:, :],
                                                              scalar=float(j), in1=bmrow[:, :], op0=ALU.add, op1=ALU.add),
                      r=[pk, 'bmrow'], w=['rows'])
            ci += 1
    wfm = kb.sb([128, 8, 1792], BF16, "wfm")
    wtm = kb.sb([128, 8, 784], BF16, "wtm")
    kb.dma(wfm[:, :, :], wfm_d.rearrange("(k p) n -> p k n", p=128), w=['wfm'], eng='pool')
    kb.dma(wtm[:, :, :], wtm_d.rearrange("(k p) n -> p k n", p=128), w=['wtm'], eng='pool')
    xt = kb.sb([128, D], F32, "xt")
    hn = kb.sb([128, D], F32, "hn")
    t1 = kb.sb([128, D], F32, "t1")
    hb = kb.sb([128, D], BF16, "hb")
    hT = kb.sb([128, 8, 512], BF16, "hT")
    tmo = kb.sb([128, 784], F32, "tmo")
    fmo = kb.sb([128, 512], F32, "fmo")
    st = kb.sb([128, 2, 6], F32, "st")
    mv = kb.sb([128, 2], F32, "mv")
    rstd = kb.sb([128, 1], F32, "rstd")
    pT = kb.ps([128, 8, 128], BF16, "pT")
    pTM = kb.ps([128, 1024], F32, "pTM")
    pFM = [kb.ps([128, 512], F32, "pFM%d" % i) for i in range(2)]
    blocks = [(0, 256, 0)] + [(256 + 512 * i, 512, 1) for i in range(8)]
    for (t0, NT, v) in blocks:
        nt = NT // 128
        for ti in range(nt):
            r0 = t0 + ti * 128
            kb.dma(xt[:, :], xa_d[xrow(r0 // 128):xrow(r0 // 128) + 128, :], r=['xa_d'], w=['xt'])
            ln_normalize(kb, xt[:, :], 'xt', hn[:, :], 'hn', st, mv, rstd)
            kb.op('pool', lambda E: E.tensor_tensor(out=t1[:, :], in0=hn[:, :], in1=rows[:, v, 1, :], op=ALU.mult), r=['hn'], w=['t1'])
            kb.op('pool', lambda E: E.tensor_tensor(out=hb[:, :], in0=t1[:, :], in1=rows[:, v, 0, :], op=ALU.add), r=['t1'], w=['hb'])
            for k in range(8):
                kb.op('pe', lambda E: E.transpose(out=pT[:, k, :], in_=hb[:, k * 128:(k + 1) * 128], identity=ident_b[:, :]), r=['hb'], w=['pT'])
            kb.op('act', lambda E: E.copy(out=hT[:, :, ti * 128:(ti + 1) * 128], in_=pT[:, :, :]), r=['pT'], w=['hT'])
            for (c0, c1) in ((0, 512), (512, 784)):
                for k in range(8):
                    kb.op('pe', lambda E: E.matmul(pTM[:, c0:c1], lhsT=hT[:, k, ti * 128:(ti + 1) * 128], rhs=wtm[:, k, c0:c1],
                                                   start=(k == 0), stop=(k == 7)), r=['hT', 'wtm'], w=['pTM'])
            kb.op('act', lambda E: E.copy(out=tmo[:, :], in_=pTM[:, 0:784]), r=['pTM'], w=['tmo'])
            kb.dma(tm_d[r0:r0 + 128, :], tmo[:, :], r=['tmo'], w=['tm_d'])
        for c in range(14):
            pf, pfk = pFM[c % 2], 'pFM%d' % (c % 2)
            for k in range(8):
                kb.op('pe', lambda E: E.matmul(pf[:, 0:NT], lhsT=wfm[:, k, c * 128:(c + 1) * 128], rhs=hT[:, k, 0:NT],
                                               start=(k == 0), stop=(k == 7)), r=['hT', 'wfm'], w=[pfk])
            kb.op('dve', lambda E: E.tensor_copy(out=fmo[:, 0:NT], in_=pf[:, 0:NT]), r=[pfk], w=['fmo'])
            kb.dma(fm_d[c * 128:(c + 1) * 128, t0:t0 + NT], fmo[:, 0:NT], r=['fmo'], w=['fm_d'])
    kb.barrier()
    es0.close()

    es2 = ExitStack()
    kb.es = es2
    pin = kb.sb([128, TA], F32, "pin")
    bufs = [kb.sb([128, 80, 80], F32, "pbA"), kb.sb([128, 80, 80], F32, "pbB")]
    cbufs = [kb.sb([128, 272], F32, "pcA"), kb.sb([128, 272], F32, "pcB")]
    invc = kb.sb([128, 4096], F32, "invc")
    invcc = kb.sb([128, 256], F32, "invcc")
    ptmp = kb.sb([128, 4096], F32, "ptmp")
    dB = kb.sb([128, TA], BF16, "dB")
    pwf = kb.sb([128, 4, 128], F32, "pwf")
    pw = kb.sb([128, 4, 128], BF16, "pw")
    psc = kb.sb([128, 4], F32, "psc")
    pob = kb.sb([128, 512], BF16, "pob")
    pPo = [kb.ps([128, 512], F32, "pPo%d" % i) for i in range(2)]
    kb.dma(pwf[:, :, :], poolw_d.rearrange("g c n -> c g n"), w=['pwf'])
    kb.op('dve', lambda E: E.tensor_copy(out=pw[:, :, :], in_=pwf[:, :, :]), r=['pwf'], w=['pw'])
    kb.dma(psc[:, :], pscale_d[:, :], w=['psc'])
    for g in range(4):
        w_ = POOL_WINDOWS[g]
        lo = w_ // 2
        kb.dma(pin[:, :], fm_d[g * 128:(g + 1) * 128, :], r=['fm_d'], w=['pin'])
        kb.dma(invc[:, :], invc_d[g, :].partition_broadcast(128), w=['invc'])
        kb.dma(invcc[:, :], invcc_d[g, :].partition_broadcast(128), w=['invcc'])
        for i in range(2):
            kb.op('pool', lambda E: E.memset(bufs[i][:, :, :], 0.0), w=['pb%d' % i])
            kb.op('pool', lambda E: E.memset(cbufs[i][:, :], 0.0), w=['pc%d' % i])
        kb.op('act', lambda E: E.copy(out=bufs[0][:, 8:72, 8:72], in_=pin[:, 256:TA].rearrange("p (r c) -> p r c", r=64)), r=['pin'], w=['pb0'])
        kb.op('act', lambda E: E.copy(out=cbufs[0][:, 8:264], in_=pin[:, 0:256]), r=['pin'], w=['pc0'])
        cur = 0
        step = 1
        while step < w_:
            L = 80 - 2 * step + 1
            kb.op('dve', lambda E: E.tensor_tensor(out=bufs[1 - cur][:, :, 0:L], in0=bufs[cur][:, :, 0:L], in1=bufs[cur][:, :, step:step + L], op=ALU.add),
                  r=['pb%d' % cur], w=['pb%d' % (1 - cur)])
            Lc = 272 - 2 * step + 1
            kb.op('pool', lambda E: E.tensor_tensor(out=cbufs[1 - cur][:, 0:Lc], in0=cbufs[cur][:, 0:Lc], in1=cbufs[cur][:, step:step + Lc], op=ALU.add),
                  r=['pc%d' % cur], w=['pc%d' % (1 - cur)])
            cur = 1 - cur
            step *= 2
        ccur = cur
        step = 1
        while step < w_:
            L = 80 - 2 * step + 1
            kb.op('dve', lambda E: E.tensor_tensor(out=bufs[1 - cur][:, 0:L, :], in0=bufs[cur][:, 0:L, :], in1=bufs[cur][:, step:step + L, :], op=ALU.add),
                  r=['pb%d' % cur], w=['pb%d' % (1 - cur)])
            cur = 1 - cur
            step *= 2
        a0 = 8 - lo
        kb.op('dve', lambda E: E.tensor_tensor(out=ptmp[:, :].rearrange("p (r c) -> p r c", r=64), in0=bufs[cur][:, a0:a0 + 64, a0:a0 + 64],
                                               in1=invc[:, :].rearrange("p (r c) -> p r c", r=64), op=ALU.mult), r=['pb%d' % cur, 'invc'], w=['ptmp'])
        kb.op('dve', lambda E: E.tensor_tensor(out=dB[:, 256:TA], in0=ptmp[:, :], in1=pin[:, 256:TA], op=ALU.subtract), r=['ptmp', 'pin'], w=['dB'])
        kb.op('pool', lambda E: E.tensor_tensor(out=cbufs[1 - ccur][:, 0:256], in0=cbufs[ccur][:, a0:a0 + 256], in1=invcc[:, :], op=ALU.mult),
              r=['pc%d' % ccur, 'invcc'], w=['pc%d' % (1 - ccur)])
        kb.op('pool', lambda E: E.tensor_tensor(out=dB[:, 0:256], in0=cbufs[1 - ccur][:, 0:256], in1=pin[:, 0:256], op=ALU.subtract), r=['pc%d' % (1 - ccur), 'pin'], w=['dB'])
        for bi, (t0, NT, v) in enumerate(blocks):
            pp, ppk = pPo[bi % 2], 'pPo%d' % (bi % 2)
            kb.op('pe', lambda E: E.matmul(pp[:, 0:NT], lhsT=pw[:, g, :], rhs=dB[:, t0:t0 + NT], start=True, stop=True), r=['pw', 'dB'], w=[ppk])
            kb.op('act', lambda E: E.activation(out=pob[:, 0:NT], in_=pp[:, 0:NT], func=AF.Copy, scale=psc[:, g:g + 1]), r=[ppk, 'psc'], w=['pob'])
            kb.dma(Ov[g * 128:(g + 1) * 128, t0 // 128:(t0 + NT) // 128, :], pob[:, 0:NT].rearrange("p (t c) -> p t c", c=128), r=['pob'], w=['brT_d'])
    kb.barrier()
    es2.close()

    es3 = ExitStack()
    kb.es = es3
    xf = kb.sb([128, NTA, 256], BF16, "xf")
    dfb = [kb.sb([128, 32, 512], BF16, "dfb%d" % i) for i in range(2)]
    dfc = kb.sb([128, 2, 2, 256], BF16, "dfc")
    dchf = kb.sb([128, 2, 128], F32, "dchf")
    dch = kb.sb([128, 2, 128], BF16, "dch")
    ysb = kb.sb([128, 2, 2, 512], BF16, "ysb")
    fob = kb.sb([128, 512], BF16, "fob")
    pY = [[kb.ps([128, 512], F32, "pY%d%d" % (g, t)) for t in range(2)] for g in range(2)]
    pFo = [kb.ps([128, 512], F32, "pFo%d" % i) for i in range(2)]
    kb.dma(xf[:, :, :], tm_d[:, 0:256].rearrange("(t p) c -> p t c", p=128), r=['tm_d'], w=['xf'], eng='pool')
    kb.dma(dchf[:, :, :], dfch_d.rearrange("t c n -> c t n"), w=['dchf'])
    kb.op('dve', lambda E: E.tensor_copy(out=dch[:, :, :], in_=dchf[:, :, :]), r=['dchf'], w=['dch'])
    kb.dma(dfc[:, :, :, :], dftc_d.rearrange("t (k p) n -> p t k n", p=128), w=['dfc'])
    dbi = 0
    fo_i = 0
    for n in range(9):
        if n == 0:
            NT, nk, tcol = 256, 2, 0
        else:
            NT, nk, tcol = 512, 32, 256 + (n - 1) * 512
        for trig in range(2):
            if n == 0:
                rhs_of = lambda k: dfc[:, trig, k, :]
                rk = 'dfc'
            else:
                db, rk = dfb[dbi % 2], 'dfb%d' % (dbi % 2)
                dbi += 1
                kb.dma(db[:, :, :], dftx_d[trig, :, (n - 1) * 512:n * 512].rearrange("(k p) n -> p k n", p=128), w=[rk])
                rhs_of = lambda k: db[:, k, :]
            for g in range(2):
                for k in range(nk):
                    tile_i = k if n == 0 else 2 + k
                    kb.op('pe', lambda E: E.matmul(pY[g][trig][:, 0:NT], lhsT=xf[:, tile_i, g * 128:(g + 1) * 128], rhs=rhs_of(k),
                                                   start=(k == 0), stop=(k == nk - 1)), r=['xf', rk], w=['pY%d%d' % (g, trig)])
                kb.op('act', lambda E: E.copy(out=ysb[:, g, trig, 0:NT], in_=pY[g][trig][:, 0:NT]), r=['pY%d%d' % (g, trig)], w=['ysb%d%d' % (g, trig)])
        for g in range(2):
            pf, pfk = pFo[fo_i % 2], 'pFo%d' % (fo_i % 2)
            fo_i += 1
            for trig in range(2):
                kb.op('pe', lambda E: E.matmul(pf[:, 0:NT], lhsT=dch[:, trig, :], rhs=ysb[:, g, trig, 0:NT], start=(trig == 0), stop=(trig == 1)),
                      r=['dch', 'ysb%d%d' % (g, trig)], w=[pfk])
            kb.op('dve', lambda E: E.tensor_copy(out=fob[:, 0:NT], in_=pf[:, 0:NT]), r=[pfk], w=['fob'])
            kb.dma(Ov[512 + g * 128:512 + (g + 1) * 128, tcol // 128:(tcol + NT) // 128, :], fob[:, 0:NT].rearrange("p (t c) -> p t c", c=128), r=['fob'], w=['brT_d'])
    kb.barrier()
    es3.close()
    kb.es = es

    tri = kb.sb([128, 4, 128], F32, "tri")
    kb.dma(tri[:, :, :], tri_d.rearrange("m j i -> j m i"), w=['tri'])
    r16 = kb.sb([128, NTA, 16], F32, "r16")
    sp16 = kb.sb([128, NTA, 16], F32, "sp16")
    brow = kb.sb([128, 32], F32, "brow")
    la = kb.sb([128, NTA, 12], F32, "la")
    acs = kb.sb([128, NTA, 12], F32, "acs")
    eacs = kb.sb([128, NTA, 12], F32, "eacs")
    neacs = kb.sb([128, NTA, 12], F32, "neacs")
    wst = kb.sb([128, NTA, 12], F32, "wst")
    etot = kb.sb([128, NTA, 12], F32, "etot")
    beta = kb.sb([128, NTA, 4], F32, "beta")
    rowp = kb.sb([128, 3, 256], F32, "rowp")
    with nc.allow_non_contiguous_dma(reason="small strided loads"):
        kb.dma(r16[:, :, :], tm_d[:, 768:784].rearrange("(t p) c -> p t c", p=128), r=['tm_d'], w=['r16'])
    kb.dma(brow[:, :], rowp_d[0, 0:32].partition_broadcast(128), w=['brow'])
    for j in range(3):
        kb.dma(rowp[:, j, :], rowp_d[1 + j, :].partition_broadcast(128), w=['rowp'])
    es4 = ExitStack()
    kb.es = es4
    pc1 = kb.ps([128, 512], F32, "pc1")
    pc2 = kb.ps([128, 512], F32, "pc2")
    pc3 = kb.ps([128, 512], F32, "pc3")
    V_ = lambda fn, r, w: kb.op('dve', fn, r=r, w=w)
    A_ = lambda fn, r, w: kb.op('act', fn, r=r, w=w)
    G_ = lambda fn, r, w: kb.op('pool', fn, r=r, w=w)
    P_ = lambda fn, r, w: kb.op('pe', fn, r=r, w=w)
    A_(lambda E: E.activation(out=brow[:, 16:32], in_=brow[:, 16:32], func=AF.Exp), ['brow'], ['brow'])
    V_(lambda E: E.tensor_scalar(out=brow[:, 16:32], in0=brow[:, 16:32], scalar1=-1.0, scalar2=None, op0=ALU.mult), ['brow'], ['brow'])
    V_(lambda E: E.tensor_tensor(out=sp16[:, :, :], in0=r16[:, :, :], in1=brow[:, 0:16].unsqueeze(1).to_broadcast([128, NTA, 16]), op=ALU.add), ['r16', 'brow'], ['sp16'])
    A_(lambda E: E.activation(out=sp16[:, :, :], in_=sp16[:, :, :], func=AF.Exp), ['sp16'], ['sp16'])
    V_(lambda E: E.tensor_scalar(out=sp16[:, :, :], in0=sp16[:, :, :], scalar1=1.0, scalar2=None, op0=ALU.add), ['sp16'], ['sp16'])
    A_(lambda E: E.activation(out=sp16[:, :, :], in_=sp16[:, :, :], func=AF.Ln), ['sp16'], ['sp16'])
    A_(lambda E: E.activation(out=beta[:, :, :], in_=r16[:, :, 8:12], func=AF.Sigmoid), ['r16'], ['beta'])
    V_(lambda E: E.tensor_tensor(out=la[:, :, 0:8], in0=sp16[:, :, 0:8], in1=brow[:, 16:24].unsqueeze(1).to_broadcast([128, NTA, 8]), op=ALU.mult), ['sp16', 'brow'], ['la'])
    V_(lambda E: E.tensor_tensor(out=la[:, :, 8:12], in0=sp16[:, :, 12:16], in1=brow[:, 28:32].unsqueeze(1).to_broadcast([128, NTA, 4]), op=ALU.mult), ['sp16', 'brow', 'la'], ['la'])
    laf = la[:, :, :].rearrange("p t c -> p (t c)")
    P_(lambda E: E.matmul(pc1[:, 0:408], lhsT=tri[:, 0, :], rhs=laf, start=True, stop=True), ['tri', 'la'], ['pc1'])
    P_(lambda E: E.matmul(pc2[:, 0:408], lhsT=tri[:, 1, :], rhs=laf, start=True, stop=True), ['tri', 'la'], ['pc2'])
    P_(lambda E: E.matmul(pc3[:, 0:408], lhsT=ones_f[:, :], rhs=laf, start=True, stop=True), ['ones', 'la'], ['pc3'])
    p1v = pc1[:, 0:408].rearrange("p (t c) -> p t c", c=12)
    p2v = pc2[:, 0:408].rearrange("p (t c) -> p t c", c=12)
    p3v = pc3[:, 0:408].rearrange("p (t c) -> p t c", c=12)
    for (c0, c1, pv, pk) in ((0, 4, p1v, 'pc1'), (4, 8, p2v, 'pc2'), (8, 10, p1v, 'pc1'), (10, 12, p2v, 'pc2')):
        V_(lambda E: E.tensor_copy(out=acs[:, :, c0:c1], in_=pv[:, :, c0:c1]), [pk, 'acs'], ['acs'])
    A_(lambda E: E.activation(out=eacs[:, :, :], in_=acs[:, :, :], func=AF.Exp), ['acs'], ['eacs'])
    V_(lambda E: E.tensor_scalar(out=neacs[:, :, :], in0=eacs[:, :, :], scalar1=-1.0, scalar2=None, op0=ALU.mult), ['eacs'], ['neacs'])
    V_(lambda E: E.tensor_tensor(out=wst[:, :, :], in0=p3v, in1=acs[:, :, :], op=ALU.subtract), ['pc3', 'acs'], ['wst'])
    A_(lambda E: E.activation(out=wst[:, :, :], in_=wst[:, :, :], func=AF.Exp), ['wst'], ['wst'])
    A_(lambda E: E.activation(out=etot[:, :, :], in_=p3v, func=AF.Exp), ['pc3'], ['etot'])

    kb.barrier()
    es4.close()
    kb.es = es
    cwt = kb.sb([128, 10, 6], F32, "cwt")
    kb.dma(cwt[:, :, :], cw_d[:, :, :], w=['cwt'])
    PSH = {}

    def open_conv_psum():
        esc = ExitStack()
        kb.es = esc
        PSH['pTt'] = kb.ps([128, 8, 128], BF16, "pTt")
        PSH['pSS'] = kb.ps([128, 512], F32, "pSS")
        return esc
    segs = ((0, 256), (256, TA))

    def conv_chunk(ci, cin, cout):
        kb.dma(cin[:, :], fm_d[(4 + ci) * 128:(5 + ci) * 128, :], r=['fm_d'], w=['cin'])
        V_(lambda E: E.tensor_scalar(out=cout[:, :], in0=cin[:, :], scalar1=cwt[:, ci, 2:3], scalar2=cwt[:, ci, 5:6], op0=ALU.mult, op1=ALU.add), ['cin', 'cwt'], ['cout'])
        for k in (0, 1, 3, 4):
            dd = k - 2
            for (s0, s1) in segs:
                a, b = s0 + max(0, -dd), s1 - max(0, dd)
                V_(lambda E: E.scalar_tensor_tensor(out=cout[:, a:b], in0=cin[:, a + dd:b + dd], scalar=cwt[:, ci, k:k + 1], in1=cout[:, a:b],
                                                    op0=ALU.mult, op1=ALU.add), ['cin', 'cwt', 'cout'], ['cout'])
        A_(lambda E: E.activation(out=cout[:, :], in_=cout[:, :], func=AF.Silu), ['cout'], ['cout'])

    def to_tok(src_bf, skey, dst, dkey, c0):
        for t0 in range(0, NTA, 8):
            n = min(8, NTA - t0)
            for j in range(n):
                P_(lambda E: E.transpose(out=PSH['pTt'][:, j, :], in_=src_bf[:, (t0 + j) * 128:(t0 + j + 1) * 128], identity=ident_b[:, :]), [skey], ['pTt'])
            A_(lambda E: E.copy(out=dst[:, t0:t0 + n, c0:c0 + 128], in_=PSH['pTt'][:, 0:n, :]), ['pTt'], [dkey])

    def l2norm_to(cout, cin, dst, dkey, scale):
        A_(lambda E: E.activation(out=cin[:, :], in_=cout[:, :], func=AF.Square), ['cout'], ['cin'])
        for (t0, NT, v) in blocks:
            P_(lambda E: E.matmul(PSH['pSS'][:, 0:NT], lhsT=ones_f[:, :], rhs=cin[:, t0:t0 + NT], start=True, stop=True), ['ones', 'cin'], ['pSS'])
            V_(lambda E: E.tensor_scalar(out=cin[:, t0:t0 + NT], in0=PSH['pSS'][:, 0:NT], scalar1=1e-6, scalar2=None, op0=ALU.add), ['pSS', 'cin'], ['cin'])
        A_(lambda E: E.activation(out=cin[:, :], in_=cin[:, :], func=AF.Sqrt), ['cin'], ['cin'])
        V_(lambda E: E.reciprocal(out=cin[:, :], in_=cin[:, :]), ['cin'], ['cin'])
        V_(lambda E: E.scalar_tensor_tensor(out=dst, in0=cout[:, :], scalar=scale, in1=cin[:, :], op0=ALU.mult, op1=ALU.mult), ['cout', 'cin'], [dkey])

    order_f = list(range(NTA))
    order_b = [1, 0] + list(range(NTA - 1, 1, -1))

    def scan(units, p, yacc, es_):
        kb.es = es_
        lanes = []
        for ui, u in enumerate(units):
            for d in range(2):
                L = dict(u=u, d=d, id='%d_%d' % (ui, d))
                L['S'] = kb.sb([128, p], F32, "S" + L['id'])
                L['Sb'] = kb.sb([128, p], BF16, "Sb" + L['id'])
                L['labc'] = kb.sb([128, 128], F32, "labc" + L['id'])
                L['Dm'] = kb.sb([128, 128], F32, "Dm" + L['id'])
                L['PT'] = kb.sb([128, 128], BF16, "PT" + L['id'])
                L['Vb'] = kb.sb([128, p], BF16, "Vb" + L['id'])
                L['Vw'] = kb.sb([128, p], BF16, "Vw" + L['id'])
                L['yt'] = kb.sb([128, p], F32, "yt" + L['id'])
                if u['gdn']:
                    for nm in ('Es', 'N0', 'N1', 'A0', 'A1', 'P', 'r'):
                        L[nm] = kb.sb([128, 128], F32, nm + L['id'])
                V_(lambda E: E.memset(L['S'][:, :], 0.0), [], ['S' + L['id']])
                V_(lambda E: E.memset(L['Sb'][:, :], 0.0), [], ['Sb' + L['id']])
                lanes.append(L)
        pA = kb.ps([128, 512], F32, "pA")
        pKQ = kb.ps([128, 512], F32, "pKQ")
        pY = kb.ps([128, 512], F32, "pYs")
        pO = kb.ps([128, 512], F32, "pO")
        pS_ = kb.ps([128, 512], F32, "pSd")
        pG = [kb.ps([128, 512], F32, "pG%d" % i) for i in range(3)] if units[0]['gdn'] else None
        for step in range(NTA):
            for L in lanes:
                u, d, lid = L['u'], L['d'], L['id']
                t = (order_f if d == 0 else order_b)[step]
                col = u['col'](d)
                QT = u['QT'][:, t * 128:(t + 1) * 128]
                KT = u['KT'][:, t * 128:(t + 1) * 128]
                Ktok = u['Ktok'][:, t, :]
                k = lambda nm: nm + lid
                V_(lambda E: E.tensor_scalar(out=L['labc'][:, :], in0=ones_f[:, :], scalar1=la[:, t, col:col + 1], scalar2=None, op0=ALU.mult), ['ones', 'la'], [k('labc')])
                P_(lambda E: E.matmul(pA[:, 0:128], lhsT=L['labc'][:, :], rhs=tri[:, d, :], start=True, stop=True), [k('labc'), 'tri'], ['pA'])
                V_(lambda E: E.scalar_tensor_tensor(out=L['Dm'][:, :], in0=pA[:, 0:128], scalar=acs[:, t, col:col + 1], in1=tri[:, 2 + d, :],
                                                    op0=ALU.subtract, op1=ALU.add), ['pA', 'acs', 'tri'], [k('Dm')])
                A_(lambda E: E.activation(out=L['Dm'][:, :], in_=L['Dm'][:, :], func=AF.Exp), [k('Dm')], [k('Dm')])
                P_(lambda E: E.matmul(pKQ[:, 0:128], lhsT=KT, rhs=QT, start=True, stop=True), [u['KTk'], u['QTk']], ['pKQ'])
                V_(lambda E: E.tensor_tensor(out=L['PT'][:, :], in0=pKQ[:, 0:128], in1=L['Dm'][:, :], op=ALU.mult), ['pKQ', k('Dm')], [k('PT')])
                if u['gdn']:
                    bc = u['bcol'](d)
                    G_(lambda E: E.tensor_tensor(out=L['Es'][:, :], in0=L['Dm'][:, :], in1=ident_f[:, :], op=ALU.subtract), [k('Dm'), 'ident_f'], [k('Es')])
                    P_(lambda E: E.matmul(pG[0][:, 0:128], lhsT=KT, rhs=KT, start=True, stop=True), [u['KTk']], ['pG0'])
                    N, A, Nn, An = L['N0'], L['A0'], L['N1'], L['A1']
                    nk = [k('N0'), k('A0'), k('N1'), k('A1')]
                    V_(lambda E: E.scalar_tensor_tensor(out=N[:, :], in0=pG[0][:, 0:128], scalar=beta[:, t, bc:bc + 1], in1=L['Es'][:, :],
                                                        op0=ALU.mult, op1=ALU.mult), ['pG0', 'beta', k('Es')], [nk[0]])
                    P_(lambda E: E.transpose(out=pG[1][:, 0:128], in_=N[:, :], identity=ident_f[:, :]), [nk[0], 'ident_f'], ['pG1'])
                    A_(lambda E: E.copy(out=A[:, :], in_=pG[1][:, 0:128]), ['pG1'], [nk[1]])
                    G_(lambda E: E.tensor_tensor(out=L['P'][:, :], in0=ident_f[:, :], in1=N[:, :], op=ALU.subtract), [nk[0], 'ident_f'], [k('P')])
                    for it in range(1, 7):
                        P_(lambda E: E.matmul(pG[1][:, 0:128], lhsT=N[:, :], rhs=A[:, :], start=True, stop=True), [nk[0], nk[1]], ['pG1'])
                        A_(lambda E: E.copy(out=An[:, :], in_=pG[1][:, 0:128]), ['pG1'], [nk[3]])
                        if it < 6:
                            P_(lambda E: E.matmul(pG[0][:, 0:128], lhsT=A[:, :], rhs=N[:, :], start=True, stop=True), [nk[0], nk[1]], ['pG0'])
                            V_(lambda E: E.tensor_copy(out=Nn[:, :], in_=pG[0][:, 0:128]), ['pG0'], [nk[2]])
                        P_(lambda E: E.matmul(pG[2][:, 0:128], lhsT=An[:, :], rhs=L['P'][:, :], start=True, stop=True), [nk[3], k('P')], ['pG2'])
                        V_(lambda E: E.tensor_tensor(out=L['P'][:, :], in0=L['P'][:, :], in1=pG[2][:, 0:128], op=ALU.add), ['pG2', k('P')], [k('P')])
                        N, A, Nn, An = Nn, An, N, A
                        nk = [nk[2], nk[3], nk[0], nk[1]]
                    P_(lambda E: E.matmul(pO[:, 0:p], lhsT=KT, rhs=L['Sb'][:, :], start=True, stop=True), [u['KTk'], k('Sb')], ['pO'])
                    V_(lambda E: E.scalar_tensor_tensor(out=L['r'][:, :], in0=pO[:, 0:p], scalar=neacs[:, t, col:col + 1], in1=u['vtok'][:, t, :],
                                                        op0=ALU.mult, op1=ALU.add), ['pO', 'neacs', u['vk']], [k('r')])
                    P_(lambda E: E.matmul(pS_[:, 0:p], lhsT=L['P'][:, :], rhs=L['r'][:, :], start=True, stop=True), [k('P'), k('r')], ['pSd'])
                    V_(lambda E: E.tensor_scalar(out=L['Vb'][:, :], in0=pS_[:, 0:p], scalar1=beta[:, t, bc:bc + 1], scalar2=None, op0=ALU.mult), ['pSd', 'beta'], [k('Vb')])
                else:
                    u['vfn'](d, t, L['Vb'], k('Vb'))
                P_(lambda E: E.matmul(pY[:, 0:p], lhsT=L['PT'][:, :], rhs=L['Vb'][:, :], start=True, stop=True), [k('PT'), k('Vb')], ['pY'])
                P_(lambda E: E.matmul(pO[:, 0:p], lhsT=QT, rhs=L['Sb'][:, :], start=True, stop=True), [u['QTk'], k('Sb')], ['pO'])
                ys = yacc[:, t, u['y0']:u['y0'] + p]
                V_(lambda E: E.scalar_tensor_tensor(out=L['yt'][:, :], in0=pO[:, 0:p], scalar=eacs[:, t, col:col + 1], in1=ys, op0=ALU.mult, op1=ALU.add),
                   ['pO', 'eacs', 'yacc%d' % t], [k('yt')])
                V_(lambda E: E.tensor_tensor(out=ys, in0=L['yt'][:, :], in1=pY[:, 0:p], op=ALU.add), [k('yt'), 'pY'], ['yacc%d' % t])
                V_(lambda E: E.tensor_scalar(out=L['Vw'][:, :], in0=L['Vb'][:, :], scalar1=wst[:, t, col:col + 1], scalar2=None, op0=ALU.mult), [k('Vb'), 'wst'], [k('Vw')])
                P_(lambda E: E.matmul(pS_[:, 0:p], lhsT=Ktok, rhs=L['Vw'][:, :], start=True, stop=True), [u['Ktokk'], k('Vw')], ['pSd'])
                V_(lambda E: E.scalar_tensor_tensor(out=L['S'][:, :], in0=L['S'][:, :], scalar=etot[:, t, col:col + 1], in1=pS_[:, 0:p], op0=ALU.mult, op1=ALU.add),
                   ['pSd', 'etot', k('S')], [k('S')])
                A_(lambda E: E.copy(out=L['Sb'][:, :], in_=L['S'][:, :]), [k('S')], [k('Sb')])

    def post_store(gsrc, gkey, row0, t, ob, pTo):
        for j in range(2):
            P_(lambda E: E.transpose(out=pTo[:, j, :], in_=gsrc[:, j * 128:(j + 1) * 128], identity=ident_b[:, :]), [gkey], ['pTo'])
        A_(lambda E: E.copy(out=ob[:, :, :], in_=pTo[:, :, :]), ['pTo'], ['ob'])
        for j in range(2):
            kb.dma(Ov[row0 + j * 128:row0 + (j + 1) * 128, t, :], ob[:, j, :], r=['ob'], w=['brT_d'])

    es5 = ExitStack()
    kb.es = es5
    cin = kb.sb([128, TA], F32, "cin")
    cout = kb.sb([128, TA], F32, "cout")
    cbf = kb.sb([128, TA], BF16, "cbf")
    BT = kb.sb([128, TA], BF16, "BT")
    CT = kb.sb([128, TA], BF16, "CT")
    xtok = kb.sb([128, NTA, 256], BF16, "xtok")
    Btok = kb.sb([128, NTA, 128], BF16, "Btok")
    yacc = kb.sb([128, NTA, 256], F32, "yacc")
    G_(lambda E: E.memset(yacc[:, :, :], 0.0), [], ['yacc%d' % t for t in range(NTA)])
    esc = open_conv_psum()
    for ci in range(2):
        conv_chunk(ci, cin, cout)
        A_(lambda E: E.copy(out=cbf[:, :], in_=cout[:, :]), ['cout'], ['cbf'])
        to_tok(cbf, 'cbf', xtok, 'xtok', ci * 128)
    conv_chunk(2, cin, cout)
    A_(lambda E: E.copy(out=BT[:, :], in_=cout[:, :]), ['cout'], ['BT'])
    to_tok(BT, 'BT', Btok, 'Btok', 0)
    conv_chunk(3, cin, cout)
    A_(lambda E: E.copy(out=CT[:, :], in_=cout[:, :]), ['cout'], ['CT'])

    def ssd_unit(h):
        def vfn(d, t, Vb, vkey):
            V_(lambda E: E.tensor_scalar(out=Vb[:, :], in0=xtok[:, t, h * 64:(h + 1) * 64], scalar1=sp16[:, t, d * 4 + h:d * 4 + h + 1], scalar2=None, op0=ALU.mult),
               ['xtok', 'sp16'], [vkey])
        return dict(QT=CT, QTk='CT', KT=BT, KTk='BT', Ktok=Btok, Ktokk='Btok', col=lambda d: d * 4 + h, vfn=vfn, gdn=False, y0=h * 64)
    kb.barrier()
    esc.close()
    pu_l = nc.dram_tensor("peer_u_%d" % l, [16384, D], F32, kind="ExternalInput").ap()
    pv_l = nc.dram_tensor("peer_v_%d" % l, [16384, D], F32, kind="ExternalInput").ap()
    C['pu_%d' % l], C['pv_%d' % l] = pu_l, pv_l
    for i in range(16):
        kb.dma(C['pub'][i * 1024:(i + 1) * 1024, :], pu_l[i * 1024:(i + 1) * 1024, :], w=['pub%d' % i], eng='pool')
        kb.dma(C['pvb'][i * 1024:(i + 1) * 1024, :], pv_l[i * 1024:(i + 1) * 1024, :], w=['pvb%d' % i], eng='pool')
    es5s = ExitStack()
    scan([ssd_unit(h) for h in range(4)], 64, yacc, es5s)
    kb.barrier()
    es5s.close()
    kb.es = es5
    zt = kb.sb([128, 256], F32, "zt")
    g1 = kb.sb([128, 256], F32, "g1")
    g2 = kb.sb([128, 256], F32, "g2")
    gb = kb.sb([128, 256], BF16, "gbf")
    ss = kb.sb([128, 2], F32, "ss")
    ob = kb.sb([128, 2, 128], BF16, "ob")
    pTo = kb.ps([128, 2, 128], BF16, "pTo")
    for t in range(NTA):
        kb.dma(zt[:, :], tm_d[t * 128:(t + 1) * 128, 256:512], r=['tm_d'], w=['zt'])
        A_(lambda E: E.activation(out=zt[:, :], in_=zt[:, :], func=AF.Silu), ['zt'], ['zt'])
        V_(lambda E: E.tensor_tensor(out=g1[:, :], in0=xtok[:, t, :], in1=rowp[:, 0, :], op=ALU.mult), ['xtok', 'rowp'], ['g1'])
        V_(lambda E: E.tensor_tensor(out=g1[:, :], in0=g1[:, :], in1=yacc[:, t, :], op=ALU.add), ['g1', 'yacc%d' % t], ['g1'])
        V_(lambda E: E.tensor_tensor(out=g1[:, :], in0=g1[:, :], in1=zt[:, :], op=ALU.mult), ['g1', 'zt'], ['g1'])
        V_(lambda E: E.scalar_tensor_tensor(out=g2[:, :], in0=g1[:, :], scalar=1.0 / 256.0, in1=g1[:, :], op0=ALU.mult, op1=ALU.mult, accum_out=ss[:, 0:1]), ['g1'], ['g2', 'ss'])
        V_(lambda E: E.tensor_scalar(out=ss[:, 0:1], in0=ss[:, 0:1], scalar1=EPS, scalar2=None, op0=ALU.add), ['ss'], ['ss'])
        A_(lambda E: E.activation(out=ss[:, 0:1], in_=ss[:, 0:1], func=AF.Sqrt), ['ss'], ['ss'])
        V_(lambda E: E.reciprocal(out=ss[:, 0:1], in_=ss[:, 0:1]), ['ss'], ['ss'])
        V_(lambda E: E.scalar_tensor_tensor(out=gb[:, :], in0=g1[:, :], scalar=ss[:, 0:1], in1=rowp[:, 1, :], op0=ALU.mult, op1=ALU.mult), ['g1', 'ss', 'rowp'], ['gb'])
        post_store(gb, 'gb', 768, t, ob, pTo)
    kb.barrier()
    es5.close()

    es6 = ExitStack()
    kb.es = es6
    cin = kb.sb([128, TA], F32, "cin6")
    cout = kb.sb([128, TA], F32, "cout6")
    cbf = kb.sb([128, TA], BF16, "cbf6")
    qT = [kb.sb([128, TA], BF16, "qT%d" % h) for h in range(2)]
    kT = [kb.sb([128, TA], BF16, "kT%d" % h) for h in range(2)]
    ktok = [kb.sb([128, NTA, 128], BF16, "ktok%d" % h) for h in range(2)]
    vtok = [kb.sb([128, NTA, 128], BF16, "vtok%d" % h) for h in range(2)]
    oacc = kb.sb([128, NTA, 256], F32, "oacc")
    G_(lambda E: E.memset(oacc[:, :, :], 0.0), [], ['yacc%d' % t for t in range(NTA)])
    esc = open_conv_psum()
    for h in range(2):
        conv_chunk(4 + h, cin, cout)
        l2norm_to(cout, cin, qT[h][:, :], 'qT%d' % h, 128 ** -0.5)
        conv_chunk(6 + h, cin, cout)
        l2norm_to(cout, cin, kT[h][:, :], 'kT%d' % h, 1.0)
        to_tok(kT[h], 'kT%d' % h, ktok[h], 'ktok%d' % h, 0)
        conv_chunk(8 + h, cin, cout)
        A_(lambda E: E.copy(out=cbf[:, :], in_=cout[:, :]), ['cout'], ['cbf'])
        to_tok(cbf, 'cbf', vtok[h], 'vtok%d' % h, 0)

    def gdn_unit(h):
        return dict(QT=qT[h], QTk='qT%d' % h, KT=kT[h], KTk='kT%d' % h, Ktok=ktok[h], Ktokk='ktok%d' % h, col=lambda d: 8 + d * 2 + h,
                    bcol=lambda d: d * 2 + h, vtok=vtok[h], vk='vtok%d' % h, gdn=True, y0=h * 128)
    kb.barrier()
    esc.close()
    es6s = ExitStack()
    scan([gdn_unit(h) for h in range(2)], 128, oacc, es6s)
    kb.barrier()
    es6s.close()
    kb.es = es6
    zt = kb.sb([128, 256], F32, "zt6")
    g1 = kb.sb([128, 256], F32, "g16")
    g2 = kb.sb([128, 256], F32, "g26")
    gb = kb.sb([128, 256], BF16, "gbf6")
    ss = kb.sb([128, 2], F32, "ss6")
    ob = kb.sb([128, 2, 128], BF16, "ob6")
    pTo = kb.ps([128, 2, 128], BF16, "pTo6")
    for t in range(NTA):
        kb.dma(zt[:, :], tm_d[t * 128:(t + 1) * 128, 512:768], r=['tm_d'], w=['zt'])
        A_(lambda E: E.activation(out=zt[:, :], in_=zt[:, :], func=AF.Silu), ['zt'], ['zt'])
        for h in range(2):
            hs = slice(h * 128, (h + 1) * 128)
            V_(lambda E: E.scalar_tensor_tensor(out=g2[:, hs], in0=oacc[:, t, hs], scalar=1.0 / 128.0, in1=oacc[:, t, hs], op0=ALU.mult, op1=ALU.mult,
                                                accum_out=ss[:, h:h + 1]), ['yacc%d' % t], ['g2', 'ss'])
        V_(lambda E: E.tensor_scalar(out=ss[:, :], in0=ss[:, :], scalar1=EPS, scalar2=None, op0=ALU.add), ['ss'], ['ss'])
        A_(lambda E: E.activation(out=ss[:, :], in_=ss[:, :], func=AF.Sqrt), ['ss'], ['ss'])
        V_(lambda E: E.reciprocal(out=ss[:, :], in_=ss[:, :]), ['ss'], ['ss'])
        for h in range(2):
            hs = slice(h * 128, (h + 1) * 128)
            V_(lambda E: E.scalar_tensor_tensor(out=g1[:, hs], in0=oacc[:, t, hs], scalar=ss[:, h:h + 1], in1=rowp[:, 2, hs], op0=ALU.mult, op1=ALU.mult),
               ['yacc%d' % t, 'ss', 'rowp'], ['g1'])
        V_(lambda E: E.tensor_tensor(out=gb[:, :], in0=g1[:, :], in1=zt[:, :], op=ALU.mult), ['g1', 'zt'], ['gb'])
        post_store(gb, 'gb', 1024, t, ob, pTo)
    kb.barrier()
    es6.close()
    kb.barrier()
    es.close()


def _cnt(n, win):
    lo = win // 2
    hi = win - 1 - lo
    t = np.arange(n)
    return (np.minimum(t + hi, n - 1) - np.maximum(t - lo, 0) + 1).astype(np.float64)


def consts_A():
    c = consts()
    if 'dft_x' not in c:
        import ml_dtypes
        bf = ml_dtypes.bfloat16
        t = np.arange(4096, dtype=np.int64)
        m = (t[:, None] * t[None, :]) % 4096
        ang = 2.0 * np.pi * m.astype(np.float64) / 4096.0
        c['dft_x'] = np.stack([(np.cos(ang) / 64.0).astype(np.float32).astype(bf), (np.sin(ang) / 64.0).astype(np.float32).astype(bf)], 0)
        t = np.arange(256, dtype=np.int64)
        ang = 2.0 * np.pi * ((t[:, None] * t[None, :]) % 256).astype(np.float64) / 256.0
        c['dft_c'] = np.stack([(np.cos(ang) / 16.0).astype(np.float32).astype(bf), (np.sin(ang) / 16.0).astype(np.float32).astype(bf)], 0)
        t = np.arange(128, dtype=np.int64)
        ang = 2.0 * np.pi * ((t[:, None] * t[None, :]) % 128).astype(np.float64) / 128.0
        c['dft_ch'] = np.stack([np.cos(ang) / np.sqrt(128.0), -np.sin(ang) / np.sqrt(128.0)], 0).astype(np.float32)
        j = np.arange(128)[:, None]
        i = np.arange(128)[None, :]
        c['tri'] = np.stack([(j <= i).astype(np.float32), (j >= i).astype(np.float32),
                             np.where(i >= j, 0.0, NEG).astype(np.float32), np.where(i <= j, 0.0, NEG).astype(np.float32)], 0)
        c['invcnt'] = np.stack([(1.0 / (_cnt(64, w)[:, None] * _cnt(64, w)[None, :])).reshape(-1) for w in POOL_WINDOWS], 0).astype(np.float32)
        c['invcnt_c'] = np.stack([1.0 / _cnt(256, w) for w in POOL_WINDOWS], 0).astype(np.float32)
    return c


def layer_weights_A(inp, l, s):
    w_in = inp['w_in'][l]
    cat = np.concatenate
    fm_cols = cat([np.arange(0, 512), 1024 + s * 256 + np.arange(256), 1536 + s * 128 + np.arange(128), 1792 + s * 128 + np.arange(128),
                   2576 + s * 256 + np.arange(256), 3088 + s * 256 + np.arange(256), 3600 + s * 256 + np.arange(256)])
    tm_cols = cat([512 + s * 256 + np.arange(256), 2048 + s * 256 + np.arange(256), 4112 + s * 256 + np.arange(256),
                   2560 + 4 * s + np.arange(4), 2568 + 4 * s + np.arange(4), 4624 + 2 * s + np.arange(2), 4628 + 2 * s + np.arange(2),
                   4632 + 2 * s + np.arange(2), 4636 + 2 * s + np.arange(2)])
    scw, scb, gcw = inp['ssd_conv_w'][l], inp['ssd_conv_b'][l], inp['gdn_conv_w'][l]
    ch_ssd = cat([s * 256 + np.arange(256), 512 + s * 128 + np.arange(128), 768 + s * 128 + np.arange(128)])
    ch_gdn = cat([s * 256 + np.arange(256), 512 + s * 256 + np.arange(256), 1024 + s * 256 + np.arange(256)])
    cw = np.zeros((10, 128, 6), np.float32)
    cw[0:4, :, 0:5] = scw[:, ch_ssd].T.reshape(4, 128, 5)
    cw[0:4, :, 5] = scb[ch_ssd].reshape(4, 128)
    cw[4:10, :, 0:5] = gcw[:, ch_gdn].T.reshape(6, 128, 5)
    rowp = np.zeros((5, 256), np.float32)
    rowp[0, 0:4] = inp['ssd_dt_bias'][l][0, 4 * s:4 * s + 4]
    rowp[0, 4:8] = inp['ssd_dt_bias'][l][1, 4 * s:4 * s + 4]
    rowp[0, 12:14] = inp['gdn_dt_bias'][l][0, 2 * s:2 * s + 2]
    rowp[0, 14:16] = inp['gdn_dt_bias'][l][1, 2 * s:2 * s + 2]
    rowp[0, 16:20] = inp['ssd_a_log'][l][0, 4 * s:4 * s + 4]
    rowp[0, 20:24] = inp['ssd_a_log'][l][1, 4 * s:4 * s + 4]
    rowp[0, 28:30] = inp['gdn_a_log'][l][0, 2 * s:2 * s + 2]
    rowp[0, 30:32] = inp['gdn_a_log'][l][1, 2 * s:2 * s + 2]
    rowp[1] = np.repeat(inp['ssd_d'][l][4 * s:4 * s + 4], 64)
    rowp[2] = inp['ssd_norm_w'][l][s * 256:(s + 1) * 256]
    rowp[3] = np.tile(inp['gdn_norm_w'][l], 2)
    c = consts_A()
    return {
        'w_mod': np.ascontiguousarray(inp['w_mod'][l][:, :2 * D]), 'b_mod': np.ascontiguousarray(inp['b_mod'][l][:2 * D]),
        'wa_fm': np.ascontiguousarray(w_in[:, fm_cols]), 'wa_tm': np.ascontiguousarray(w_in[:, tm_cols]),
        'convw': np.ascontiguousarray(cw.transpose(1, 0, 2)), 'pool_w': inp['pool_w'][l],
        'pool_scale': np.ascontiguousarray(inp['pool_scale'][l].reshape(4, 128).T), 'invcnt': c['invcnt'], 'invcnt_c': c['invcnt_c'],
        'rowp': rowp, 'dft_x': c['dft_x'], 'dft_c': c['dft_c'], 'dft_ch': c['dft_ch'], 'tri': c['tri'], 'ident': c['ident'],
    }


RG_PAIRS = [[0, 1], [2, 3], [4, 5], [6, 7]]
DEPTH = 4
O_CH = [(i * 6400, min((i + 1) * 6400, NTA * 1280)) for i in range(7)]
X_CH = [(i * 512, min((i + 1) * 512, NTOK)) for i in range(5)]


def _gathered_row(chunks, rank, j):
    for (a, b) in chunks:
        if a <= j < b:
            return 2 * a + rank * (b - a) + (j - a)
    raise ValueError(j)


def build_fused(depth=DEPTH):
    es = ExitStack()
    nc = bass.Bass("TRN2", target_bir_lowering=False)
    kb = KB(nc, es)
    kb.es_top = es
    di = lambda n, s, d=F32: nc.dram_tensor(n, list(s), d, kind="ExternalInput").ap()
    dint = lambda n, s, d=F32: nc.dram_tensor(n, list(s), d, addr_space="Local", kind="Internal").ap()
    C = {
        'cvec': di("cvec", [2, 128, 8]), 'ident': di("ident", [128, 128]), 'iota16': di("iota16", [128, 16]),
        'idxB': di("idxB", [128, 17, 16], U32), 'invcnt': di("invcnt", [4, 4096]), 'invcnt_c': di("invcnt_c", [4, 256]),
        'dft_x': di("dft_x", [2, 4096, 4096], BF16), 'dft_c': di("dft_c", [2, 256, 256], BF16), 'dft_ch': di("dft_ch", [2, 128, 128]),
        'tri': di("tri", [4, 128, 128]),
        'pub': dint("pub_scr", [16384, D], BF16), 'pvb': dint("pvb_scr", [16384, D], BF16),
        'x1': dint("x1_scr", [NTOK, D]), 'tm': dint("tm_scr", [TA, 784]), 'fm': dint("fm_scr", [1792, TA]),
    }
    for l in range(depth):
        C['w_mod_%d' % l] = di("w_mod_%d" % l, [D, 6 * D])
        C['b_mod_%d' % l] = di("b_mod_%d" % l, [6 * D])
    xa0 = di("xa0", [TA, D])
    xmy0 = di("xmy0", [NTOK, D])
    out_d = nc.dram_tensor("out", [NTOK, D], F32, kind="ExternalOutput").ap()
    O_d = dint("O_loc", [NTA * 1280, 128], BF16)
    G_d = dint("G_all", [2 * NTA * 1280, 128], BF16)
    xg = [xa0] + [dint("xg_%d" % l, [TA, D]) for l in range(1, depth)]
    xmy = [xmy0] + [dint("xmy_%d" % l, [NTOK, D]) for l in range(1, depth)]
    for l in range(depth):
        emit_A(nc, kb, l, C, xg[l], O_d)
        kb.collective("AllGather", [O_d[a:b, :] for (a, b) in O_CH], [G_d[2 * a:2 * b, :] for (a, b) in O_CH])
        o = out_d if l == depth - 1 else xmy[l + 1]
        emit_B(nc, kb, l, C, xmy[l], G_d, o)
        if l < depth - 1:
            kb.collective("AllGather", [xmy[l + 1][a:b, :] for (a, b) in X_CH], [xg[l + 1][2 * a:2 * b, :] for (a, b) in X_CH])
    kb.finish()
    return nc, es


def idx_table(s):
    idx = np.zeros((128, 17, 16), np.uint32)
    p = np.arange(128)
    for tau in range(17):
        T = s if tau == 0 else 2 + 16 * s + (tau - 1)
        for kc in range(16):
            br, j = kc // 4, kc % 4
            if br == 0:
                rs, row0 = 0, j * 128
            else:
                rs, row0 = j // 2, 512 + (br - 1) * 256 + (j % 2) * 128
            idx[:, tau, kc] = _gathered_row(O_CH, rs, T * 1280 + row0) + p
    return idx


_PROG = {}


def kernel(**inp):
    inp = {k: np.asarray(v) for k, v in inp.items()}
    depth = inp['w_in'].shape[0]
    if 'nc' not in _PROG:
        _PROG['nc'] = build_fused(depth)
    nc, _ = _PROG['nc']
    cA = consts_A()
    shared = {'ident': cA['ident'], 'iota16': cA['iota16'], 'invcnt': cA['invcnt'], 'invcnt_c': cA['invcnt_c'],
              'dft_x': cA['dft_x'], 'dft_c': cA['dft_c'], 'dft_ch': cA['dft_ch'], 'tri': cA['tri']}
    skipA = ('w_mod', 'b_mod', 'invcnt', 'invcnt_c', 'dft_x', 'dft_c', 'dft_ch', 'tri', 'ident')
    skipB = ('ident', 'iota16')
    per_half = []
    for s in range(2):
        m = {}
        for l in range(depth):
            for k, v in layer_weights_A(inp, l, s).items():
                if k not in skipA:
                    m['%s_%d' % (k, l)] = v
        m['idxB'] = idx_table(s)
        per_half.append(m)
    common = dict(shared)
    for l in range(depth):
        for k, v in layer_weights_B(inp, l).items():
            if k not in skipB:
                common['%s_%d' % (k, l)] = v
    x, ctx = inp['x'], inp['ctx']
    in_maps = []
    for core in range(8):
        b, s = core // 2, core % 2
        halves = [np.concatenate([ctx[b, r * 128:(r + 1) * 128], x[b, r * 2048:(r + 1) * 2048]], 0) for r in range(2)]
        m = dict(common)
        m.update(per_half[s])
        m['xa0'] = np.ascontiguousarray(np.concatenate([ctx[b], x[b]], 0))
        m['xmy0'] = np.ascontiguousarray(halves[s])
        m['cvec'] = cvec_layout(inp['c_ctx'], inp['c'][b])
        in_maps.append(m)
    res = run_bass_kernel_spmd(nc, in_maps, core_ids=list(range(8))).results
    out = np.empty_like(x)
    for core in range(8):
        b, s = core // 2, core % 2
        out[b, s * 2048:(s + 1) * 2048] = np.asarray(res[core]['out'])[128:]
    return out
```

```python
import numpy as np
from contextlib import ExitStack
import concourse.bass as bass
import concourse.mybir as mybir
from concourse.bass_utils import run_bass_kernel_spmd

F32 = mybir.dt.float32
BF16 = mybir.dt.bfloat16
U32 = mybir.dt.uint32
AF = mybir.ActivationFunctionType
ALU = mybir.AluOpType
AX = mybir.AxisListType

NDS = 64
NHW = 16


class KB:
    def __init__(self, nc, es):
        self.nc, self.es = nc, es
        self.engs = {'pe': nc.tensor, 'act': nc.scalar, 'dve': nc.vector, 'pool': nc.gpsimd, 'sp': nc.sync}
        self.sem, self.cnt = {}, {}
        for e in ('pe', 'act', 'dve', 'pool'):
            self.sem[('e', e)] = es.enter_context(nc.semaphore('s_' + e))
            self.cnt[e] = 0
        self.dcnt = [0] * NDS
        for i in range(NDS):
            self.sem[('d', i)] = es.enter_context(nc.semaphore('d%d' % i))
        self.dnext = 0
        self.dnext_sw = 0
        self.seen = {e: {} for e in self.engs}
        self.lastw, self.readers = {}, {}
        self.nbuf = 0
        self.snap = {}
        self.snap_order = []

    def sb(self, shape, dt, name=None):
        self.nbuf += 1
        return self.es.enter_context(self.nc.sbuf_tensor('sb%d_' % self.nbuf + (name or 'b'), list(shape), dt))

    def ps(self, shape, dt, name=None):
        self.nbuf += 1
        return self.es.enter_context(self.nc.psum_tensor('ps%d_' % self.nbuf + (name or 'p'), list(shape), dt))

    def op(self, eng, fn, r=(), w=(), dma=False):
        deps = {}

        def need(tok):
            if tok is not None and deps.get(tok[0], 0) < tok[1]:
                deps[tok[0]] = tok[1]
        for key in r:
            need(self.lastw.get(key))
        for key in w:
            need(self.lastw.get(key))
            for k, v in self.readers.get(key, {}).items():
                need((k, v))
        E = self.engs[eng]
        if dma:
            if eng == 'pool':
                i = NHW + self.dnext_sw
                self.dnext_sw = (self.dnext_sw + 1) % (NDS - NHW)
            else:
                i = self.dnext
                self.dnext = (i + 1) % NHW
            if self.dcnt[i] > 0:
                need((('d', i), self.dcnt[i]))
        self._wait(eng, E, deps)
        inst = fn(E)
        if dma:
            self.dcnt[i] += 16
            inst.then_inc(self.sem[('d', i)], 16)
            tok = (('d', i), self.dcnt[i])
        else:
            self.cnt[eng] += 1
            inst.then_inc(self.sem[('e', eng)], 1)
            tok = (('e', eng), self.cnt[eng])
        self.snap[tok] = dict(self.seen[eng])
        self.snap_order.append(tok)
        if len(self.snap_order) > 6000:
            for t_ in self.snap_order[:2000]:
                self.snap.pop(t_, None)
            del self.snap_order[:2000]
        for key in r:
            d = self.readers.setdefault(key, {})
            if d.get(tok[0], 0) < tok[1]:
                d[tok[0]] = tok[1]
        for key in w:
            self.lastw[key] = tok
            self.readers[key] = {}
        return tok

    def _wait(self, eng, E, deps):
        sn = self.seen[eng]
        for k, v in sorted(deps.items(), key=lambda kv: -kv[1]):
            if eng == 'pe' and k == ('e', 'pe'):
                continue
            if sn.get(k, 0) >= v:
                continue
            E.wait_ge(self.sem[k], v)
            sn[k] = v
            inh = self.snap.get((k, v))
            if inh:
                for k2, v2 in inh.items():
                    if sn.get(k2, 0) < v2:
                        sn[k2] = v2

    def prewait(self, eng, wkeys):
        deps = {}
        for key in wkeys:
            for tok in [self.lastw.get(key)] + list(self.readers.get(key, {}).items()):
                if tok is not None and deps.get(tok[0], 0) < tok[1]:
                    deps[tok[0]] = tok[1]
        self._wait(eng, self.engs[eng], deps)

    def dma(self, out, in_, r=(), w=(), eng='sp', **kw):
        return self.op(eng, lambda E: E.dma_start(out=out, in_=in_, **kw), r=r, w=w, dma=True)

    def finish(self):
        E = self.engs['sp']
        for i in range(NDS):
            if self.dcnt[i] > 0 and self.seen['sp'].get(('d', i), 0) < self.dcnt[i]:
                E.wait_ge(self.sem[('d', i)], self.dcnt[i])
        for e in ('pe', 'act', 'dve', 'pool'):
            if self.cnt[e] > 0:
                E.wait_ge(self.sem[('e', e)], self.cnt[e])

    def barrier(self):
        for en, E in self.engs.items():
            for i in range(NDS):
                if self.dcnt[i] > 0 and self.seen[en].get(('d', i), 0) < self.dcnt[i]:
                    E.wait_ge(self.sem[('d', i)], self.dcnt[i])
                    self.seen[en][('d', i)] = self.dcnt[i]
            for e in ('pe', 'act', 'dve', 'pool'):
                if e != en and self.cnt[e] > 0 and self.seen[en].get(('e', e), 0) < self.cnt[e]:
                    E.wait_ge(self.sem[('e', e)], self.cnt[e])
                    self.seen[en][('e', e)] = self.cnt[e]
        self.lastw, self.readers = {}, {}


    def collective(self, kind, ins, outs):
        E = self.engs['pool']
        if not hasattr(self, 'ccsem'):
            self.ccsem = self.es_top.enter_context(self.nc.semaphore('cc_sem'))
            self.sem[('c', 0)] = self.ccsem
            self.cccnt = 0
        self.barrier()
        for i_, o_ in zip(ins, outs):
            E.collective_compute(kind, ALU.bypass, replica_groups=RG_PAIRS, ins=[i_], outs=[o_]).then_inc(self.ccsem)
            self.cccnt += 1
        for en, EE in self.engs.items():
            EE.wait_ge(self.ccsem, self.cccnt)
            self.seen[en][('c', 0)] = self.cccnt


D = 1024
ALPHA = 8 ** 0.25
EPS = 1e-6
NTOK = 2176
NEG = -1.0e30


def ln_normalize(kb, xin, xkey, hn, hnkey, st, mv, rstd):
    kb.op('dve', lambda E: E.bn_stats(out=st[:, 0, :], in_=xin[:, 0:512]), r=[xkey], w=['st0'])
    kb.op('dve', lambda E: E.bn_stats(out=st[:, 1, :], in_=xin[:, 512:1024]), r=[xkey], w=['st1'])
    kb.op('dve', lambda E: E.bn_aggr(out=mv[:, :], in_=st[:, :, :]), r=['st0', 'st1'], w=['mv'])
    kb.op('dve', lambda E: E.tensor_scalar(out=rstd[:, :], in0=mv[:, 1:2], scalar1=EPS, scalar2=None,
                                           op0=ALU.add), r=['mv'], w=['rstd'])
    kb.op('act', lambda E: E.activation(out=rstd[:, :], in_=rstd[:, :], func=AF.Sqrt), r=['rstd'], w=['rstd'])
    kb.op('dve', lambda E: E.reciprocal(out=rstd[:, :], in_=rstd[:, :]), r=['rstd'], w=['rstd'])
    kb.op('dve', lambda E: E.tensor_scalar(out=hn, in0=xin, scalar1=mv[:, 0:1], scalar2=rstd[:, 0:1],
                                           op0=ALU.subtract, op1=ALU.mult), r=[xkey, 'mv', 'rstd'], w=[hnkey])


def emit_B(nc, kb, l, C, x_d, G_d, out_d):
    es = ExitStack()
    kb.es = es
    dt = lambda n, s, d=F32, kind="ExternalInput": nc.dram_tensor(n + "_%d" % l, list(s), d, kind=kind).ap()
    cvec_d = C['cvec']
    wmod_d = C['w_mod_%d' % l]
    bmod_d = C['b_mod_%d' % l]
    wg_d = dt("wg", [8, D, 512])
    wb_d = dt("wb", [2048, D])
    wo_d = dt("wo", [D, D])
    lnp_d = dt("lnp", [4, D])
    wq_d = dt("wq", [D, 2048])
    keysT_d = dt("keysT", [128, 16, 128])
    pub_d, pvb_d = C['pub'], C['pvb']
    ident_d, iota_d, x1_d, idx_d = C['ident'], C['iota16'], C['x1'], C['idxB']
    idxb = kb.sb([128, 17, 16], U32, "idxb")
    kb.dma(idxb[:, :, :], idx_d[:, :, :], w=['idxb'])

    ident_f = kb.sb([128, 128], F32, "ident_f")
    ident_b = kb.sb([128, 128], BF16, "ident_b")
    ones_b = kb.sb([128, 128], F32, "ones_f")
    iota16 = kb.sb([128, 16], F32, "iota16")
    rows = kb.sb([128, 2, 6, D], F32, "modrows")
    lnrow = kb.sb([128, 4, D], F32, "lnrow")
    es0 = ExitStack()
    kb.es = es0
    cs = kb.sb([128, 2, 8], F32, "cs")
    csrep = kb.sb([128, 2, 8, 128], BF16, "csrep")
    bmrow = kb.sb([128, 512], F32, "bmrow")
    wmb = [kb.sb([128, 8, 512], BF16, "wmb%d" % i) for i in range(2)]
    pmod = [kb.ps([128, 512], F32, "pmod%d" % i) for i in range(2)]

    kb.dma(ident_f[:, :], ident_d[:, :], w=['ident_f'])
    kb.dma(iota16[:, :], iota_d[:, :], w=['iota16'])
    kb.op('dve', lambda E: E.tensor_copy(out=ident_b[:, :], in_=ident_f[:, :]), r=['ident_f'], w=['ident_b'])
    kb.op('pool', lambda E: E.memset(ones_b[:, :], 1.0), w=['ones'])
    for v in range(2):
        kb.dma(cs[:, v, :], cvec_d[v, :, :], w=['cs%d' % v])
        kb.op('act', lambda E: E.activation(out=cs[:, v, :], in_=cs[:, v, :], func=AF.Silu), r=['cs%d' % v], w=['cs%d' % v])
        for k in range(8):
            kb.op('dve', lambda E: E.tensor_scalar(out=csrep[:, v, k, :], in0=ones_b[:, :], scalar1=cs[:, v, k:k + 1],
                                                   scalar2=None, op0=ALU.mult), r=['cs%d' % v, 'ones'], w=['csrep%d' % v])
    for j in range(4):
        kb.dma(lnrow[:, j, :], lnp_d[j, :].partition_broadcast(128), w=['lnrow'])
    ci = 0
    for j in range(6):
        for hf in range(2):
            c0 = j * D + hf * 512
            wbuf = wmb[ci % 2]
            wk = 'wmb%d' % (ci % 2)
            kb.dma(wbuf[:, :, :], wmod_d[:, c0:c0 + 512].rearrange("(k p) n -> p k n", p=128), w=[wk], eng='pool')
            kb.dma(bmrow[:, :], bmod_d[c0:c0 + 512].partition_broadcast(128), w=['bmrow'])
            for v in range(2):
                pk = 'pmod%d' % v
                for k in range(8):
                    kb.op('pe', lambda E: E.matmul(pmod[v][:, :], lhsT=csrep[:, v, k, :], rhs=wbuf[:, k, :],
                                                   start=(k == 0), stop=(k == 7)), r=['csrep%d' % v, wk], w=[pk])
                if j in (1, 4):
                    kb.op('dve', lambda E: E.scalar_tensor_tensor(out=rows[:, v, j, hf * 512:(hf + 1) * 512], in0=pmod[v][:, :],
                                                                  scalar=1.0, in1=bmrow[:, :], op0=ALU.add, op1=ALU.add),
                          r=[pk, 'bmrow'], w=['rows'])
                else:
                    kb.op('dve', lambda E: E.tensor_tensor(out=rows[:, v, j, hf * 512:(hf + 1) * 512], in0=pmod[v][:, :],
                                                           in1=bmrow[:, :], op=ALU.add), r=[pk, 'bmrow'], w=['rows'])
            ci += 1
    kb.barrier()
    es0.close()

    es1 = ExitStack()
    kb1 = kb
    kb.es = es1
    wb = kb.sb([128, 16, D], BF16, "wb")
    wo = kb.sb([128, 8, D], BF16, "wo")
    wg = [kb.sb([128, 8, 512], BF16, "wg%d" % i) for i in range(2)]
    xb = kb.sb([128, 4, D], F32, "xb")
    hn = kb.sb([128, D], F32, "hn")
    t1 = kb.sb([128, D], F32, "t1")
    hb = kb.sb([128, D], BF16, "hb")
    hT = kb.sb([128, 8, 512], BF16, "hT")
    brTb = kb.sb([128, 16, 512], BF16, "brTb")
    gs = kb.sb([128, 4, 512], BF16, "gs")
    tb = kb.sb([128, 4, 512], F32, "tb")
    mT = kb.sb([128, 8, 512], BF16, "mT")
    st = kb.sb([128, 2, 6], F32, "st")
    mv = kb.sb([128, 2], F32, "mv")
    rstd = kb.sb([128, 1], F32, "rstd")
    pT = kb.ps([128, 8, 128], BF16, "pT")
    pG = [kb.ps([128, 512], F32, "pG%d" % i) for i in range(2)]
    pP = [kb.ps([128, 512], F32, "pP%d" % i) for i in range(2)]
    pM = kb.ps([128, D], F32, "pM")

    kb.dma(wb[:, :, :], wb_d.rearrange("(k p) n -> p k n", p=128), w=['wb'], eng='pool')
    kb.dma(wo[:, :, :], wo_d.rearrange("(k p) n -> p k n", p=128), w=['wo'], eng='pool')

    blocks = [(0, 128, 0)] + [(128 + 512 * i, 512, 1) for i in range(4)]
    wgi = 0
    for (t0, NT, v) in blocks:
        nt = NT // 128
        for kc in range(16):
            for ti in range(nt):
                tau = t0 // 128 + ti
                kb.op('pool', lambda E: E.indirect_dma_start(out=brTb[:, kc, ti * 128:(ti + 1) * 128], out_offset=None, in_=G_d[:, :],
                                                             in_offset=bass.IndirectOffsetOnAxis(ap=idxb[:, tau, kc:kc + 1], axis=0)),
                      r=['idxb', 'G_d'], w=['brTb'], dma=True)
        for ti in range(nt):
            xk = 'xb%d' % ti
            kb.dma(xb[:, ti, :], x_d[t0 + ti * 128:t0 + (ti + 1) * 128, :], r=['x_in'], w=[xk])
            ln_normalize(kb, xb[:, ti, :], xk, hn[:, :], 'hn', st, mv, rstd)
            kb.op('pool', lambda E: E.tensor_tensor(out=t1[:, :], in0=hn[:, :], in1=rows[:, v, 1, :], op=ALU.mult), r=['hn'], w=['t1'])
            kb.op('pool', lambda E: E.tensor_tensor(out=hb[:, :], in0=t1[:, :], in1=rows[:, v, 0, :], op=ALU.add), r=['t1'], w=['hb'])
            for k in range(8):
                kb.op('pe', lambda E: E.transpose(out=pT[:, k, :], in_=hb[:, k * 128:(k + 1) * 128], identity=ident_b[:, :]), r=['hb'], w=['pT'])
            kb.op('act', lambda E: E.copy(out=hT[:, :, ti * 128:(ti + 1) * 128], in_=pT[:, :, :]), r=['pT'], w=['hT'])
        for oc in range(8):
            wgb = wg[wgi % 2]
            wgk = 'wg%d' % (wgi % 2)
            wgi += 1
            kb.dma(wgb[:, :, :], wg_d[oc].rearrange("(k p) n -> p k n", p=128), w=[wgk], eng='pool')
            for br in range(4):
                pg, pgk = pG[br % 2], 'pG%d' % (br % 2)
                pp, ppk = pP[br % 2], 'pP%d' % (br % 2)
                for k in range(8):
                    kb.op('pe', lambda E: E.matmul(pg[:, 0:NT], lhsT=wgb[:, k, br * 128:(br + 1) * 128], rhs=hT[:, k, 0:NT],
                                                   start=(k == 0), stop=(k == 7)), r=[wgk, 'hT'], w=[pgk])
                kb.op('act', lambda E: E.activation(out=gs[:, br, 0:NT], in_=pg[:, 0:NT], func=AF.Sigmoid), r=[pgk], w=['gs%d' % br])
                for k in range(4):
                    kb.op('pe', lambda E: E.matmul(pp[:, 0:NT], lhsT=wb[:, br * 4 + k, oc * 128:(oc + 1) * 128], rhs=brTb[:, br * 4 + k, 0:NT],
                                                   start=(k == 0), stop=(k == 3)), r=['wb', 'brTb'], w=[ppk])
                kb.op('dve', lambda E: E.tensor_tensor(out=tb[:, br, 0:NT], in0=pp[:, 0:NT], in1=gs[:, br, 0:NT], op=ALU.mult),
                      r=[ppk, 'gs%d' % br], w=['tb%d' % br])
            kb.op('pool', lambda E: E.tensor_tensor(out=tb[:, 0, 0:NT], in0=tb[:, 0, 0:NT], in1=tb[:, 1, 0:NT], op=ALU.add), r=['tb0', 'tb1'], w=['tb0'])
            kb.op('pool', lambda E: E.tensor_tensor(out=tb[:, 2, 0:NT], in0=tb[:, 2, 0:NT], in1=tb[:, 3, 0:NT], op=ALU.add), r=['tb2', 'tb3'], w=['tb2'])
            kb.op('pool', lambda E: E.tensor_tensor(out=mT[:, oc, 0:NT], in0=tb[:, 0, 0:NT], in1=tb[:, 2, 0:NT], op=ALU.add), r=['tb0', 'tb2'], w=['mT'])
        for ti in range(nt):
            xk = 'xb%d' % ti
            for hf in range(2):
                for k in range(8):
                    kb.op('pe', lambda E: E.matmul(pM[:, hf * 512:(hf + 1) * 512], lhsT=mT[:, k, ti * 128:(ti + 1) * 128],
                                                   rhs=wo[:, k, hf * 512:(hf + 1) * 512], start=(k == 0), stop=(k == 7)), r=['mT', 'wo'], w=['pM'])
            kb.op('dve', lambda E: E.tensor_tensor(out=t1[:, :], in0=pM[:, :], in1=rows[:, v, 2, :], op=ALU.mult), r=['pM'], w=['t1'])
            kb.op('dve', lambda E: E.scalar_tensor_tensor(out=t1[:, :], in0=xb[:, ti, :], scalar=ALPHA, in1=t1[:, :], op0=ALU.mult, op1=ALU.add),
                  r=[xk, 't1'], w=['t1'])
            ln_normalize(kb, t1[:, :], 't1', hn[:, :], 'hn', st, mv, rstd)
            kb.op('pool', lambda E: E.tensor_tensor(out=hn[:, :], in0=hn[:, :], in1=lnrow[:, 0, :], op=ALU.mult), r=['hn'], w=['hn'])
            kb.op('pool', lambda E: E.tensor_tensor(out=xb[:, ti, :], in0=hn[:, :], in1=lnrow[:, 1, :], op=ALU.add), r=['hn'], w=[xk])
            kb.dma(x1_d[t0 + ti * 128:t0 + (ti + 1) * 128, :], xb[:, ti, :], r=[xk], w=['x1d'])
    kb.barrier()
    es1.close()

    es2 = ExitStack()
    kb.es = es2
    wq = kb.sb([128, 8, 2048], BF16, "wq")
    keysT = kb.sb([128, 16, 128], BF16, "keysTb")
    xt = kb.sb([128, D], F32, "xt")
    hn = kb.sb([128, D], F32, "hn2")
    h2b = kb.sb([128, D], BF16, "h2b")
    h2T = kb.sb([128, 8, 128], BF16, "h2T")
    qT = kb.sb([128, 16, 128], BF16, "qT")
    sc = kb.sb([128, 16, 128], F32, "sc")
    sc2 = sc
    vv = kb.sb([128, 16, 16], F32, "vv")
    ix = kb.sb([128, 16, 16], U32, "ix")
    ixf = kb.sb([128, 16, 16], F32, "ixf")
    cand = kb.sb([128, 8, 256], F32, "cand")
    cand2 = cand
    best = kb.sb([128, 8, 16], F32, "best")
    pos = kb.sb([128, 8, 16], U32, "pos")
    pa = kb.sb([128, 8, 16], U32, "pa")
    pb_ = kb.sb([128, 8, 16], U32, "pb")
    paf = kb.sb([128, 2, 8, 16], F32, "paf")
    eq = kb.sb([128, 8, 16, 16], F32, "eq")
    sel = kb.sb([128, 2, 8, 16], F32, "sel")
    eidf = kb.sb([128, 128], F32, "eidf")
    eid = kb.sb([128, 128], U32, "eid")
    gate = kb.sb([128, 8, 16], F32, "gate")
    gsum = kb.sb([128, 8], F32, "gsum")
    act = kb.sb([128, 128], F32, "actv")
    wgt = kb.sb([128, 128], F32, "wgt")
    junk = kb.sb([128, D], BF16, "junk")
    Wd = [kb.sb([128, 16, 128], BF16, "Wd%d" % i) for i in range(2)]
    wgtb = kb.sb([128, 128], BF16, "wgtb")
    t1 = kb.sb([128, D], F32, "t1b")
    st = kb.sb([128, 2, 6], F32, "st2")
    mv = kb.sb([128, 2], F32, "mv2")
    rstd = kb.sb([128, 1], F32, "rstd2")
    NG = 16
    ug = [kb.sb([128, D], BF16, "ug%d" % i) for i in range(NG)]
    pT = kb.ps([128, 8, 128], BF16, "pT2")
    pQ = [kb.ps([128, 512], F32, "pQ0")]
    pS = kb.ps([128, 16, 128], F32, "pS")
    pV = kb.ps([128, D], F32, "pV")

    kb.dma(wq[:, :, :], wq_d.rearrange("(k p) n -> p k n", p=128), w=['wq'], eng='pool')
    kb.dma(keysT[:, :, :], keysT_d[:, :, :], w=['keysT'], eng='pool')
    xt2 = [xt, kb.sb([128, D], F32, 'xt_b')]
    h2b2 = [h2b, kb.sb([128, D], BF16, 'h2b_b')]
    eid2 = [eid, kb.sb([128, 128], U32, 'eid_b')]
    gate2 = [gate, kb.sb([128, 8, 16], F32, 'gate_b')]
    gstate = {'gi': 0}

    def topk(ti):
        v = 0 if ti == 0 else 1
        r0 = ti * 128
        q = ti % 2
        xtq, h2bq, eidq, gateq = xt2[q], h2b2[q], eid2[q], gate2[q]
        xk, hk, ek, gtk = 'xt%d' % q, 'h2b%d' % q, 'eid%d' % q, 'gate%d' % q
        kb.dma(xtq[:, :], x1_d[r0:r0 + 128, :], r=['x1d'], w=[xk])
        ln_normalize(kb, xtq[:, :], xk, hn[:, :], 'hn', st, mv, rstd)
        kb.op('pool', lambda E: E.tensor_tensor(out=t1[:, :], in0=hn[:, :], in1=rows[:, v, 4, :], op=ALU.mult), r=['hn'], w=['t1'])
        kb.op('pool', lambda E: E.tensor_tensor(out=h2bq[:, :], in0=t1[:, :], in1=rows[:, v, 3, :], op=ALU.add), r=['t1'], w=[hk])
        for k in range(8):
            kb.op('pe', lambda E: E.transpose(out=pT[:, k, :], in_=h2bq[:, k * 128:(k + 1) * 128], identity=ident_b[:, :]), r=[hk], w=['pT'])
        kb.op('act', lambda E: E.copy(out=h2T[:, :, :], in_=pT[:, :, :]), r=['pT'], w=['h2T'])
        yield
        for g4 in range(4):
            pq, pqk = pQ[0], 'pQ0'
            for j in range(4):
                hs = g4 * 4 + j
                for k in range(8):
                    kb.op('pe', lambda E: E.matmul(pq[:, j * 128:(j + 1) * 128], lhsT=wq[:, k, hs * 128:(hs + 1) * 128], rhs=h2T[:, k, :],
                                                   start=(k == 0), stop=(k == 7)), r=['wq', 'h2T'], w=[pqk])
            kb.op('act', lambda E: E.copy(out=qT[:, g4 * 4:(g4 + 1) * 4, :], in_=pq[:, :].rearrange("p (j t) -> p j t", j=4)), r=[pqk], w=['qT%d' % g4])
        yield
        for hs in range(16):
            kb.op('pe', lambda E: E.matmul(pS[:, hs, :], lhsT=qT[:, hs, :], rhs=keysT[:, hs, :], start=True, stop=True),
                  r=['qT%d' % (hs // 4), 'keysT'], w=['pS'])
        kb.op('act', lambda E: E.copy(out=sc[:, :, :], in_=pS[:, :, :]), r=['pS'], w=['sc%d' % i for i in range(16)])
        yield
        for hs in range(16):
            sk = 'sc%d' % hs
            kb.op('dve', lambda E: E.max(out=vv[:, hs, 0:8], in_=sc[:, hs, :]), r=[sk], w=['vva%d' % hs])
            kb.op('dve', lambda E: E.max_index(out=ix[:, hs, 0:8], in_max=vv[:, hs, 0:8], in_values=sc[:, hs, :]), r=[sk, 'vva%d' % hs], w=['ixa%d' % hs])
            kb.op('dve', lambda E: E.match_replace(out=sc2[:, hs, :], in_to_replace=vv[:, hs, 0:8], in_values=sc[:, hs, :], imm_value=NEG),
                  r=[sk, 'vva%d' % hs], w=[sk])
            kb.op('dve', lambda E: E.max(out=vv[:, hs, 8:16], in_=sc2[:, hs, :]), r=[sk], w=['vvb%d' % hs])
            kb.op('dve', lambda E: E.max_index(out=ix[:, hs, 8:16], in_max=vv[:, hs, 8:16], in_values=sc2[:, hs, :]), r=[sk, 'vvb%d' % hs], w=['ixb%d' % hs])
        yield
        allv = ['vva%d' % i for i in range(16)] + ['vvb%d' % i for i in range(16)]
        alli = ['ixa%d' % i for i in range(16)] + ['ixb%d' % i for i in range(16)]
        kb.op('dve', lambda E: E.tensor_copy(out=ixf[:, :, :], in_=ix[:, :, :]), r=alli, w=['ixf'])
        v4 = vv[:, :, :].rearrange("p (h s) a -> p h s a", s=2)
        yield
        for h in range(8):
            kb.op('dve', lambda E: E.tensor_tensor(out=cand[:, h, :].rearrange("p (a b) -> p a b", a=16),
                                                   in0=vv[:, 2 * h, :].unsqueeze(2).to_broadcast([128, 16, 16]),
                                                   in1=vv[:, 2 * h + 1, :].unsqueeze(1).to_broadcast([128, 16, 16]), op=ALU.add), r=allv, w=['cand%d' % h])
            ck = 'cand%d' % h
            kb.op('dve', lambda E: E.max(out=best[:, h, 0:8], in_=cand[:, h, :]), r=[ck], w=['besta%d' % h])
            kb.op('dve', lambda E: E.max_index(out=pos[:, h, 0:8], in_max=best[:, h, 0:8], in_values=cand[:, h, :]), r=[ck, 'besta%d' % h], w=['posa%d' % h])
            kb.op('dve', lambda E: E.match_replace(out=cand2[:, h, :], in_to_replace=best[:, h, 0:8], in_values=cand[:, h, :], imm_value=NEG),
                  r=[ck, 'besta%d' % h], w=[ck])
            kb.op('dve', lambda E: E.max(out=best[:, h, 8:16], in_=cand2[:, h, :]), r=[ck], w=['bestb%d' % h])
            kb.op('dve', lambda E: E.max_index(out=pos[:, h, 8:16], in_max=best[:, h, 8:16], in_values=cand2[:, h, :]), r=[ck, 'bestb%d' % h], w=['posb%d' % h])
        yield
        allb = ['besta%d' % i for i in range(8)] + ['bestb%d' % i for i in range(8)]
        allp = ['posa%d' % i for i in range(8)] + ['posb%d' % i for i in range(8)]
        kb.op('dve', lambda E: E.tensor_scalar(out=pa[:, :, :], in0=pos[:, :, :], scalar1=4, scalar2=None, op0=ALU.logical_shift_right), r=allp, w=['pa'])
        kb.op('dve', lambda E: E.tensor_scalar(out=pb_[:, :, :], in0=pos[:, :, :], scalar1=15, scalar2=None, op0=ALU.bitwise_and), r=allp, w=['pb'])
        kb.op('dve', lambda E: E.tensor_copy(out=paf[:, 0, :, :], in_=pa[:, :, :]), r=['pa'], w=['paf0'])
        kb.op('dve', lambda E: E.tensor_copy(out=paf[:, 1, :, :], in_=pb_[:, :, :]), r=['pb'], w=['paf1'])
        yield
        for s_ in range(2):
            for h in range(8):
                kb.op('dve', lambda E: E.tensor_tensor(out=eq[:, h, :, :], in0=iota16[:, :].unsqueeze(1).to_broadcast([128, 16, 16]),
                                                       in1=paf[:, s_, h, :].unsqueeze(2).to_broadcast([128, 16, 16]), op=ALU.is_equal),
                      r=['paf%d' % s_, 'iota16'], w=['eq%d' % h])
                kb.op('dve', lambda E: E.tensor_tensor(out=eq[:, h, :, :], in0=eq[:, h, :, :],
                                                       in1=ixf[:, 2 * h + s_, :].unsqueeze(1).to_broadcast([128, 16, 16]), op=ALU.mult),
                      r=['eq%d' % h, 'ixf'], w=['eq%d' % h])
            kb.op('dve', lambda E: E.tensor_reduce(out=sel[:, s_, :, :], in_=eq[:, :, :, :], axis=AX.X, op=ALU.add), r=['eq%d' % h for h in range(8)], w=['sel%d' % s_])
        kb.op('dve', lambda E: E.scalar_tensor_tensor(out=eidf[:, :], in0=sel[:, 0, :, :].rearrange("p h k -> p (h k)"), scalar=128.0,
                                                      in1=sel[:, 1, :, :].rearrange("p h k -> p (h k)"), op0=ALU.mult, op1=ALU.add), r=['sel0', 'sel1'], w=['eidf'])
        kb.op('dve', lambda E: E.tensor_copy(out=eidq[:, :], in_=eidf[:, :]), r=['eidf'], w=[ek])
        yield
        kb.op('dve', lambda E: E.tensor_tensor(out=gateq[:, :, :], in0=best[:, :, :], in1=best[:, :, 0:1].to_broadcast([128, 8, 16]), op=ALU.subtract), r=allb, w=[gtk])
        kb.op('act', lambda E: E.activation(out=gateq[:, :, :], in_=gateq[:, :, :], func=AF.Exp), r=[gtk], w=[gtk])
        kb.op('dve', lambda E: E.tensor_reduce(out=gsum[:, :], in_=gateq[:, :, :], axis=AX.X, op=ALU.add), r=[gtk], w=['gsum'])
        kb.op('dve', lambda E: E.reciprocal(out=gsum[:, :], in_=gsum[:, :]), r=['gsum'], w=['gsum'])
        kb.op('dve', lambda E: E.tensor_tensor(out=gateq[:, :, :], in0=gateq[:, :, :], in1=gsum[:, :].unsqueeze(2).to_broadcast([128, 8, 16]), op=ALU.mult), r=[gtk, 'gsum'], w=[gtk])
        yield

    def gphase(ti, nxt):
        v = 0 if ti == 0 else 1
        r0 = ti * 128
        q = ti % 2
        xtq, h2bq, eidq, gateq = xt2[q], h2b2[q], eid2[q], gate2[q]
        xk, hk, ek, gtk = 'xt%d' % q, 'h2b%d' % q, 'eid%d' % q, 'gate%d' % q
        gi = gstate['gi']
        for r_ in range(128):
            gb, gk = ug[gi % NG], 'ug%d' % (gi % NG)
            if r_ % 8 == 0:
                kb.prewait('pool', ['ug%d' % ((gi + j_) % NG) for j_ in range(8)])
            gi += 1
            if r_ % 16 == 15 and nxt is not None:
                next(nxt, None)
            kb.op('pool', lambda E: E.indirect_dma_start(out=gb[:, :], out_offset=None, in_=pub_d[:, :],
                                                         in_offset=bass.IndirectOffsetOnAxis(ap=eidq[:, r_:r_ + 1], axis=0)),
                  r=[ek], w=[gk], dma=True)
            kb.op('dve', lambda E: E.scalar_tensor_tensor(out=junk[:, :], in0=gb[:, :], scalar=1.0, in1=h2bq[:, :], op0=ALU.mult, op1=ALU.mult,
                                                          accum_out=act[:, r_:r_ + 1]), r=[gk, hk], w=['junk', 'act'])
        kb.op('act', lambda E: E.activation(out=act[:, :], in_=act[:, :], func=AF.Gelu), r=['act'], w=['act'])
        kb.op('dve', lambda E: E.tensor_tensor(out=wgtb[:, :], in0=act[:, :], in1=gateq[:, :, :].rearrange("p h k -> p (h k)"), op=ALU.mult), r=['act', gtk], w=['wgtb'])
        for r_ in range(128):
            gb, gk = ug[gi % NG], 'ug%d' % (gi % NG)
            if r_ % 8 == 0:
                kb.prewait('pool', ['ug%d' % ((gi + j_) % NG) for j_ in range(8)])
            gi += 1
            if r_ % 16 == 15 and nxt is not None:
                next(nxt, None)
            if r_ % 16 == 0:
                wdb, wdk = Wd[(r_ // 16) % 2], 'Wd%d' % ((r_ // 16) % 2)
                kb.op('dve', lambda E: E.tensor_tensor(out=wdb[:, :, :], in0=ident_b[:, :].unsqueeze(1).to_broadcast([128, 16, 128]),
                                                       in1=wgtb[:, r_:r_ + 16].unsqueeze(2).to_broadcast([128, 16, 128]), op=ALU.mult),
                      r=['ident_b', 'wgtb'], w=[wdk])
            kb.op('pool', lambda E: E.indirect_dma_start(out=gb[:, :], out_offset=None, in_=pvb_d[:, :],
                                                         in_offset=bass.IndirectOffsetOnAxis(ap=eidq[:, r_:r_ + 1], axis=0)),
                  r=[ek], w=[gk], dma=True)
            for hf in range(2):
                kb.op('pe', lambda E: E.matmul(pV[:, hf * 512:(hf + 1) * 512], lhsT=wdb[:, r_ % 16, :], rhs=gb[:, hf * 512:(hf + 1) * 512],
                                               start=(r_ == 0), stop=(r_ == 127)), r=[wdk, gk], w=['pV'])
        kb.op('dve', lambda E: E.tensor_tensor(out=t1[:, :], in0=pV[:, :], in1=rows[:, v, 5, :], op=ALU.mult), r=['pV'], w=['t1'])
        kb.op('dve', lambda E: E.scalar_tensor_tensor(out=t1[:, :], in0=xtq[:, :], scalar=ALPHA, in1=t1[:, :], op0=ALU.mult, op1=ALU.add), r=[xk, 't1'], w=['t1'])
        ln_normalize(kb, t1[:, :], 't1', hn[:, :], 'hn', st, mv, rstd)
        kb.op('pool', lambda E: E.tensor_tensor(out=hn[:, :], in0=hn[:, :], in1=lnrow[:, 2, :], op=ALU.mult), r=['hn'], w=['hn'])
        kb.op('pool', lambda E: E.tensor_tensor(out=t1[:, :], in0=hn[:, :], in1=lnrow[:, 3, :], op=ALU.add), r=['hn'], w=['t1'])
        kb.dma(out_d[r0:r0 + 128, :], t1[:, :], r=['t1'], w=['outd'])
        gstate['gi'] = gi

    NTI = NTOK // 128
    for _ in topk(0):
        pass
    for ti in range(NTI):
        nxt = topk(ti + 1) if ti + 1 < NTI else None
        gphase(ti, nxt)
        if nxt is not None:
            for _ in nxt:
                pass
    kb.barrier()
    es2.close()
    kb.barrier()
    es.close()


GATE0 = 8736 - 4096
_CONST = {}


def consts():
    if not _CONST:
        _CONST['ident'] = np.eye(128, dtype=np.float32)
        _CONST['iota16'] = np.tile(np.arange(16, dtype=np.float32)[None, :], (128, 1))
    return _CONST


def layer_weights_B(inp, l):
    w_in = inp['w_in'][l]
    wg = np.ascontiguousarray(w_in[:, GATE0:].reshape(D, 4, 8, 128).transpose(2, 0, 1, 3).reshape(8, D, 512))
    keysT = np.ascontiguousarray(inp['peer_keys'][l].reshape(16, 128, 128).transpose(2, 0, 1))
    c = consts()
    return {
        'w_mod': inp['w_mod'][l], 'b_mod': inp['b_mod'][l], 'wg': wg,
        'wb': np.ascontiguousarray(inp['w_branch'][l].reshape(2048, D)), 'wo': inp['w_out'][l],
        'lnp': np.stack([inp['ln1_g'][l], inp['ln1_b'][l], inp['ln2_g'][l], inp['ln2_b'][l]], 0),
        'wq': inp['peer_wq'][l], 'keysT': keysT, 'peer_u': inp['peer_u'][l], 'peer_v': inp['peer_v'][l],
        'ident': c['ident'], 'iota16': c['iota16'],
    }


def cvec_layout(c_ctx, c_b):
    return np.ascontiguousarray(np.stack([c_ctx, c_b], 0).reshape(2, 8, 128).transpose(0, 2, 1))


TA = 4352
NTA = 34
POOL_WINDOWS = (2, 4, 8, 16)


def emit_A(nc, kb, l, C, xa_d, O_d):
    es = ExitStack()
    kb.es = es
    dt = lambda n, s, d=F32, kind="ExternalInput": nc.dram_tensor(n + "_%d" % l, list(s), d, kind=kind).ap()
    cvec_d = C['cvec']
    wmod_d = C['w_mod_%d' % l]
    bmod_d = C['b_mod_%d' % l]
    wfm_d = dt("wa_fm", [D, 1792])
    wtm_d = dt("wa_tm", [D, 784])
    cw_d = dt("convw", [128, 10, 6])
    poolw_d = dt("pool_w", [4, 128, 128])
    pscale_d = dt("pool_scale", [128, 4])
    invc_d, invcc_d = C['invcnt'], C['invcnt_c']
    rowp_d = dt("rowp", [5, 256])
    dftx_d, dftc_d, dfch_d, tri_d, ident_d, tm_d, fm_d = C['dft_x'], C['dft_c'], C['dft_ch'], C['tri'], C['ident'], C['tm'], C['fm']
    Ov = O_d.rearrange("(t r) c -> r t c", r=1280)
    xrow = (lambda q: q * 128) if l == 0 else (lambda q: _gathered_row(X_CH, *((0, 0) if q == 0 else ((1, 0) if q == 1 else ((0, 128 + (q - 2) * 128) if q < 18 else (1, 128 + (q - 18) * 128))))))

    ident_f = kb.sb([128, 128], F32, "ident_f")
    ident_b = kb.sb([128, 128], BF16, "ident_b")
    ones_f = kb.sb([128, 128], F32, "ones_f")
    kb.dma(ident_f[:, :], ident_d[:, :], w=['ident_f'])
    kb.op('dve', lambda E: E.tensor_copy(out=ident_b[:, :], in_=ident_f[:, :]), r=['ident_f'], w=['ident_b'])
    kb.op('pool', lambda E: E.memset(ones_f[:, :], 1.0), w=['ones'])

    es0 = ExitStack()
    kb.es = es0
    rows = kb.sb([128, 2, 2, D], F32, "modrows")
    cs = kb.sb([128, 2, 8], F32, "cs")
    csrep = kb.sb([128, 2, 8, 128], BF16, "csrep")
    bmrow = kb.sb([128, 512], F32, "bmrow")
    wmb = [kb.sb([128, 8, 512], BF16, "wmb%d" % i) for i in range(2)]
    pmod = [kb.ps([128, 512], F32, "pmod%d" % i) for i in range(2)]
    for v in range(2):
        kb.dma(cs[:, v, :], cvec_d[v, :, :], w=['cs%d' % v])
        kb.op('act', lambda E: E.activation(out=cs[:, v, :], in_=cs[:, v, :], func=AF.Silu), r=['cs%d' % v], w=['cs%d' % v])
        for k in range(8):
            kb.op('dve', lambda E: E.tensor_scalar(out=csrep[:, v, k, :], in0=ones_f[:, :], scalar1=cs[:, v, k:k + 1],
                                                   scalar2=None, op0=ALU.mult), r=['cs%d' % v, 'ones'], w=['csrep%d' % v])
    ci = 0
    for j in range(2):
        for hf in range(2):
            c0 = j * D + hf * 512
            wbuf, wk = wmb[ci % 2], 'wmb%d' % (ci % 2)
            kb.dma(wbuf[:, :, :], wmod_d[:, c0:c0 + 512].rearrange("(k p) n -> p k n", p=128), w=[wk], eng='pool')
            kb.dma(bmrow[:, :], bmod_d[c0:c0 + 512].partition_broadcast(128), w=['bmrow'])
            for v in range(2):
                pk = 'pmod%d' % v
                for k in range(8):
                    kb.op('pe', lambda E: E.matmul(pmod[v][:, :], lhsT=csrep[:, v, k, :], rhs=wbuf[:, k, :],
                                                   start=(k == 0), stop=(k == 7)), r=['csrep%d' % v, wk], w=[pk])
                kb.op('dve', lambda E: E.scalar_tensor_tensor(out=rows[:, v, j, hf * 512:(hf + 1) * 512], in0=pmod[v][:, :],
                                                              scalar=float(j), in1=bmrow[:, :], op0=ALU.add, op1=ALU.add),
                      r=[pk, 'bmrow'], w=['rows'])
            ci += 1
    wfm = kb.sb([128, 8, 1792], BF16, "wfm")
    wtm = kb.sb([128, 8, 784], BF16, "wtm")
    kb.dma(wfm[:, :, :], wfm_d.rearrange("(k p) n -> p k n", p=128), w=['wfm'], eng='pool')
    kb.dma(wtm[:, :, :], wtm_d.rearrange("(k p) n -> p k n", p=128), w=['wtm'], eng='pool')
    xt = kb.sb([128, D], F32, "xt")
    hn = kb.sb([128, D], F32, "hn")
    t1 = kb.sb([128, D], F32, "t1")
    hb = kb.sb([128, D], BF16, "hb")
    hT = kb.sb([128, 8, 512], BF16, "hT")
    tmo = kb.sb([128, 784], F32, "tmo")
    fmo = kb.sb([128, 512], F32, "fmo")
    st = kb.sb([128, 2, 6], F32, "st")
    mv = kb.sb([128, 2], F32, "mv")
    rstd = kb.sb([128, 1], F32, "rstd")
    pT = kb.ps([128, 8, 128], BF16, "pT")
    pTM = kb.ps([128, 1024], F32, "pTM")
    pFM = [kb.ps([128, 512], F32, "pFM%d" % i) for i in range(2)]
    blocks = [(0, 256, 0)] + [(256 + 512 * i, 512, 1) for i in range(8)]
    for (t0, NT, v) in blocks:
        nt = NT // 128
        for ti in range(nt):
            r0 = t0 + ti * 128
            kb.dma(xt[:, :], xa_d[xrow(r0 // 128):xrow(r0 // 128) + 128, :], r=['xa_d'], w=['xt'])
            ln_normalize(kb, xt[:, :], 'xt', hn[:, :], 'hn', st, mv, rstd)
            kb.op('pool', lambda E: E.tensor_tensor(out=t1[:, :], in0=hn[:, :], in1=rows[:, v, 1, :], op=ALU.mult), r=['hn'], w=['t1'])
            kb.op('pool', lambda E: E.tensor_tensor(out=hb[:, :], in0=t1[:, :], in1=rows[:, v, 0, :], op=ALU.add), r=['t1'], w=['hb'])
            for k in range(8):
                kb.op('pe', lambda E: E.transpose(out=pT[:, k, :], in_=hb[:, k * 128:(k + 1) * 128], identity=ident_b[:, :]), r=['hb'], w=['pT'])
            kb.op('act', lambda E: E.copy(out=hT[:, :, ti * 128:(ti + 1) * 128], in_=pT[:, :, :]), r=['pT'], w=['hT'])
            for (c0, c1) in ((0, 512), (512, 784)):
                for k in range(8):
                    kb.op('pe', lambda E: E.matmul(pTM[:, c0:c1], lhsT=hT[:, k, ti * 128:(ti + 1) * 128], rhs=wtm[:, k, c0:c1],
                                                   start=(k == 0), stop=(k == 7)), r=['hT', 'wtm'], w=['pTM'])
            kb.op('act', lambda E: E.copy(out=tmo[:, :], in_=pTM[:, 0:784]), r=['pTM'], w=['tmo'])
            kb.dma(tm_d[r0:r0 + 128, :], tmo[:, :], r=['tmo'], w=['tm_d'])
        for c in range(14):
            pf, pfk = pFM[c % 2], 'pFM%d' % (c % 2)
            for k in range(8):
                kb.op('pe', lambda E: E.matmul(pf[:, 0:NT], lhsT=wfm[:, k, c * 128:(c + 1) * 128], rhs=hT[:, k, 0:NT],
                                               start=(k == 0), stop=(k == 7)), r=['hT', 'wfm'], w=[pfk])
            kb.op('dve', lambda E: E.tensor_copy(out=fmo[:, 0:NT], in_=pf[:, 0:NT]), r=[pfk], w=['fmo'])
            kb.dma(fm_d[c * 128:(c + 1) * 128, t0:t0 + NT], fmo[:, 0:NT], r=['fmo'], w=['fm_d'])
    kb.barrier()
    es0.close()

    es2 = ExitStack()
    kb.es = es2
    pin = kb.sb([128, TA], F32, "pin")
    bufs = [kb.sb([128, 80, 80], F32, "pbA"), kb.sb([128, 80, 80], F32, "pbB")]
    cbufs = [kb.sb([128, 272], F32, "pcA"), kb.sb([128, 272], F32, "pcB")]
    invc = kb.sb([128, 4096], F32, "invc")
    invcc = kb.sb([128, 256], F32, "invcc")
    ptmp = kb.sb([128, 4096], F32, "ptmp")
    dB = kb.sb([128, TA], BF16, "dB")
    pwf = kb.sb([128, 4, 128], F32, "pwf")
    pw = kb.sb([128, 4, 128], BF16, "pw")
    psc = kb.sb([128, 4], F32, "psc")
    pob = kb.sb([128, 512], BF16, "pob")
    pPo = [kb.ps([128, 512], F32, "pPo%d" % i) for i in range(2)]
    kb.dma(pwf[:, :, :], poolw_d.rearrange("g c n -> c g n"), w=['pwf'])
    kb.op('dve', lambda E: E.tensor_copy(out=pw[:, :, :], in_=pwf[:, :, :]), r=['pwf'], w=['pw'])
    kb.dma(psc[:, :], pscale_d[:, :], w=['psc'])
    for g in range(4):
        w_ = POOL_WINDOWS[g]
        lo = w_ // 2
        kb.dma(pin[:, :], fm_d[g * 128:(g + 1) * 128, :], r=['fm_d'], w=['pin'])
        kb.dma(invc[:, :], invc_d[g, :].partition_broadcast(128), w=['invc'])
        kb.dma(invcc[:, :], invcc_d[g, :].partition_broadcast(128), w=['invcc'])
        for i in range(2):
            kb.op('pool', lambda E: E.memset(bufs[i][:, :, :], 0.0), w=['pb%d' % i])
            kb.op('pool', lambda E: E.memset(cbufs[i][:, :], 0.0), w=['pc%d' % i])
        kb.op('act', lambda E: E.copy(out=bufs[0][:, 8:72, 8:72], in_=pin[:, 256:TA].rearrange("p (r c) -> p r c", r=64)), r=['pin'], w=['pb0'])
        kb.op('act', lambda E: E.copy(out=cbufs[0][:, 8:264], in_=pin[:, 0:256]), r=['pin'], w=['pc0'])
        cur = 0
        step = 1
        while step < w_:
            L = 80 - 2 * step + 1
            kb.op('dve', lambda E: E.tensor_tensor(out=bufs[1 - cur][:, :, 0:L], in0=bufs[cur][:, :, 0:L], in1=bufs[cur][:, :, step:step + L], op=ALU.add),
                  r=['pb%d' % cur], w=['pb%d' % (1 - cur)])
            Lc = 272 - 2 * step + 1
            kb.op('pool', lambda E: E.tensor_tensor(out=cbufs[1 - cur][:, 0:Lc], in0=cbufs[cur][:, 0:Lc], in1=cbufs[cur][:, step:step + Lc], op=ALU.add),
                  r=['pc%d' % cur], w=['pc%d' % (1 - cur)])
            cur = 1 - cur
            step *= 2
        ccur = cur
        step = 1
        while step < w_:
            L = 80 - 2 * step + 1
            kb.op('dve', lambda E: E.tensor_tensor(out=bufs[1 - cur][:, 0:L, :], in0=bufs[cur][:, 0:L, :], in1=bufs[cur][:, step:step + L, :], op=ALU.add),
                  r=['pb%d' % cur], w=['pb%d' % (1 - cur)])
            cur = 1 - cur
            step *= 2
        a0 = 8 - lo
        kb.op('dve', lambda E: E.tensor_tensor(out=ptmp[:, :].rearrange("p (r c) -> p r c", r=64), in0=bufs[cur][:, a0:a0 + 64, a0:a0 + 64],
                                               in1=invc[:, :].rearrange("p (r c) -> p r c", r=64), op=ALU.mult), r=['pb%d' % cur, 'invc'], w=['ptmp'])
        kb.op('dve', lambda E: E.tensor_tensor(out=dB[:, 256:TA], in0=ptmp[:, :], in1=pin[:, 256:TA], op=ALU.subtract), r=['ptmp', 'pin'], w=['dB'])
        kb.op('pool', lambda E: E.tensor_tensor(out=cbufs[1 - ccur][:, 0:256], in0=cbufs[ccur][:, a0:a0 + 256], in1=invcc[:, :], op=ALU.mult),
              r=['pc%d' % ccur, 'invcc'], w=['pc%d' % (1 - ccur)])
        kb.op('pool', lambda E: E.tensor_tensor(out=dB[:, 0:256], in0=cbufs[1 - ccur][:, 0:256], in1=pin[:, 0:256], op=ALU.subtract), r=['pc%d' % (1 - ccur), 'pin'], w=['dB'])
        for bi, (t0, NT, v) in enumerate(blocks):
            pp, ppk = pPo[bi % 2], 'pPo%d' % (bi % 2)
            kb.op('pe', lambda E: E.matmul(pp[:, 0:NT], lhsT=pw[:, g, :], rhs=dB[:, t0:t0 + NT], start=True, stop=True), r=['pw', 'dB'], w=[ppk])
            kb.op('act', lambda E: E.activation(out=pob[:, 0:NT], in_=pp[:, 0:NT], func=AF.Copy, scale=psc[:, g:g + 1]), r=[ppk, 'psc'], w=['pob'])
            kb.dma(Ov[g * 128:(g + 1) * 128, t0 // 128:(t0 + NT) // 128, :], pob[:, 0:NT].rearrange("p (t c) -> p t c", c=128), r=['pob'], w=['brT_d'])
    kb.barrier()
    es2.close()

    es3 = ExitStack()
    kb.es = es3
    xf = kb.sb([128, NTA, 256], BF16, "xf")
    dfb = [kb.sb([128, 32, 512], BF16, "dfb%d" % i) for i in range(2)]
    dfc = kb.sb([128, 2, 2, 256], BF16, "dfc")
    dchf = kb.sb([128, 2, 128], F32, "dchf")
    dch = kb.sb([128, 2, 128], BF16, "dch")
    ysb = kb.sb([128, 2, 2, 512], BF16, "ysb")
    fob = kb.sb([128, 512], BF16, "fob")
    pY = [[kb.ps([128, 512], F32, "pY%d%d" % (g, t)) for t in range(2)] for g in range(2)]
    pFo = [kb.ps([128, 512], F32, "pFo%d" % i) for i in range(2)]
    kb.dma(xf[:, :, :], tm_d[:, 0:256].rearrange("(t p) c -> p t c", p=128), r=['tm_d'], w=['xf'], eng='pool')
    kb.dma(dchf[:, :, :], dfch_d.rearrange("t c n -> c t n"), w=['dchf'])
    kb.op('dve', lambda E: E.tensor_copy(out=dch[:, :, :], in_=dchf[:, :, :]), r=['dchf'], w=['dch'])
    kb.dma(dfc[:, :, :, :], dftc_d.rearrange("t (k p) n -> p t k n", p=128), w=['dfc'])
    dbi = 0
    fo_i = 0
    for n in range(9):
        if n == 0:
            NT, nk, tcol = 256, 2, 0
        else:
            NT, nk, tcol = 512, 32, 256 + (n - 1) * 512
        for trig in range(2):
            if n == 0:
                rhs_of = lambda k: dfc[:, trig, k, :]
                rk = 'dfc'
            else:
                db, rk = dfb[dbi % 2], 'dfb%d' % (dbi % 2)
                dbi += 1
                kb.dma(db[:, :, :], dftx_d[trig, :, (n - 1) * 512:n * 512].rearrange("(k p) n -> p k n", p=128), w=[rk])
                rhs_of = lambda k: db[:, k, :]
            for g in range(2):
                for k in range(nk):
                    tile_i = k if n == 0 else 2 + k
                    kb.op('pe', lambda E: E.matmul(pY[g][trig][:, 0:NT], lhsT=xf[:, tile_i, g * 128:(g + 1) * 128], rhs=rhs_of(k),
                                                   start=(k == 0), stop=(k == nk - 1)), r=['xf', rk], w=['pY%d%d' % (g, trig)])
                kb.op('act', lambda E: E.copy(out=ysb[:, g, trig, 0:NT], in_=pY[g][trig][:, 0:NT]), r=['pY%d%d' % (g, trig)], w=['ysb%d%d' % (g, trig)])
        for g in range(2):
            pf, pfk = pFo[fo_i % 2], 'pFo%d' % (fo_i % 2)
            fo_i += 1
            for trig in range(2):
                kb.op('pe', lambda E: E.matmul(pf[:, 0:NT], lhsT=dch[:, trig, :], rhs=ysb[:, g, trig, 0:NT], start=(trig == 0), stop=(trig == 1)),
                      r=['dch', 'ysb%d%d' % (g, trig)], w=[pfk])
            kb.op('dve', lambda E: E.tensor_copy(out=fob[:, 0:NT], in_=pf[:, 0:NT]), r=[pfk], w=['fob'])
            kb.dma(Ov[512 + g * 128:512 + (g + 1) * 128, tcol // 128:(tcol + NT) // 128, :], fob[:, 0:NT].rearrange("p (t c) -> p t c", c=128), r=['fob'], w=['brT_d'])
    kb.barrier()
    es3.close()
    kb.es = es

    tri = kb.sb([128, 4, 128], F32, "tri")
    kb.dma(tri[:, :, :], tri_d.rearrange("m j i -> j m i"), w=['tri'])
    r16 = kb.sb([128, NTA, 16], F32, "r16")
    sp16 = kb.sb([128, NTA, 16], F32, "sp16")
    brow = kb.sb([128, 32], F32, "brow")
    la = kb.sb([128, NTA, 12], F32, "la")
    acs = kb.sb([128, NTA, 12], F32, "acs")
    eacs = kb.sb([128, NTA, 12], F32, "eacs")
    neacs = kb.sb([128, NTA, 12], F32, "neacs")
    wst = kb.sb([128, NTA, 12], F32, "wst")
    etot = kb.sb([128, NTA, 12], F32, "etot")
    beta = kb.sb([128, NTA, 4], F32, "beta")
    rowp = kb.sb([128, 3, 256], F32, "rowp")
    with nc.allow_non_contiguous_dma(reason="small strided loads"):
        kb.dma(r16[:, :, :], tm_d[:, 768:784].rearrange("(t p) c -> p t c", p=128), r=['tm_d'], w=['r16'])
    kb.dma(brow[:, :], rowp_d[0, 0:32].partition_broadcast(128), w=['brow'])
    for j in range(3):
        kb.dma(rowp[:, j, :], rowp_d[1 + j, :].partition_broadcast(128), w=['rowp'])
    es4 = ExitStack()
    kb.es = es4
    pc1 = kb.ps([128, 512], F32, "pc1")
    pc2 = kb.ps([128, 512], F32, "pc2")
    pc3 = kb.ps([128, 512], F32, "pc3")
    V_ = lambda fn, r, w: kb.op('dve', fn, r=r, w=w)
    A_ = lambda fn, r, w: kb.op('act', fn, r=r, w=w)
    G_ = lambda fn, r, w: kb.op('pool', fn, r=r, w=w)
    P_ = lambda fn, r, w: kb.op('pe', fn, r=r, w=w)
    A_(lambda E: E.activation(out=brow[:, 16:32], in_=brow[:, 16:32], func=AF.Exp), ['brow'], ['brow'])
    V_(lambda E: E.tensor_scalar(out=brow[:, 16:32], in0=brow[:, 16:32], scalar1=-1.0, scalar2=None, op0=ALU.mult), ['brow'], ['brow'])
    V_(lambda E: E.tensor_tensor(out=sp16[:, :, :], in0=r16[:, :, :], in1=brow[:, 0:16].unsqueeze(1).to_broadcast([128, NTA, 16]), op=ALU.add), ['r16', 'brow'], ['sp16'])
    A_(lambda E: E.activation(out=sp16[:, :, :], in_=sp16[:, :, :], func=AF.Exp), ['sp16'], ['sp16'])
    V_(lambda E: E.tensor_scalar(out=sp16[:, :, :], in0=sp16[:, :, :], scalar1=1.0, scalar2=None, op0=ALU.add), ['sp16'], ['sp16'])
    A_(lambda E: E.activation(out=sp16[:, :, :], in_=sp16[:, :, :], func=AF.Ln), ['sp16'], ['sp16'])
    A_(lambda E: E.activation(out=beta[:, :, :], in_=r16[:, :, 8:12], func=AF.Sigmoid), ['r16'], ['beta'])
    V_(lambda E: E.tensor_tensor(out=la[:, :, 0:8], in0=sp16[:, :, 0:8], in1=brow[:, 16:24].unsqueeze(1).to_broadcast([128, NTA, 8]), op=ALU.mult), ['sp16', 'brow'], ['la'])
    V_(lambda E: E.tensor_tensor(out=la[:, :, 8:12], in0=sp16[:, :, 12:16], in1=brow[:, 28:32].unsqueeze(1).to_broadcast([128, NTA, 4]), op=ALU.mult), ['sp16', 'brow', 'la'], ['la'])
    laf = la[:, :, :].rearrange("p t c -> p (t c)")
    P_(lambda E: E.matmul(pc1[:, 0:408], lhsT=tri[:, 0, :], rhs=laf, start=True, stop=True), ['tri', 'la'], ['pc1'])
    P_(lambda E: E.matmul(pc2[:, 0:408], lhsT=tri[:, 1, :], rhs=laf, start=True, stop=True), ['tri', 'la'], ['pc2'])
    P_(lambda E: E.matmul(pc3[:, 0:408], lhsT=ones_f[:, :], rhs=laf, start=True, stop=True), ['ones', 'la'], ['pc3'])
    p1v = pc1[:, 0:408].rearrange("p (t c) -> p t c", c=12)
    p2v = pc2[:, 0:408].rearrange("p (t c) -> p t c", c=12)
    p3v = pc3[:, 0:408].rearrange("p (t c) -> p t c", c=12)
    for (c0, c1, pv, pk) in ((0, 4, p1v, 'pc1'), (4, 8, p2v, 'pc2'), (8, 10, p1v, 'pc1'), (10, 12, p2v, 'pc2')):
        V_(lambda E: E.tensor_copy(out=acs[:, :, c0:c1], in_=pv[:, :, c0:c1]), [pk, 'acs'], ['acs'])
    A_(lambda E: E.activation(out=eacs[:, :, :], in_=acs[:, :, :], func=AF.Exp), ['acs'], ['eacs'])
    V_(lambda E: E.tensor_scalar(out=neacs[:, :, :], in0=eacs[:, :, :], scalar1=-1.0, scalar2=None, op0=ALU.mult), ['eacs'], ['neacs'])
    V_(lambda E: E.tensor_tensor(out=wst[:, :, :], in0=p3v, in1=acs[:, :, :], op=ALU.subtract), ['pc3', 'acs'], ['wst'])
    A_(lambda E: E.activation(out=wst[:, :, :], in_=wst[:, :, :], func=AF.Exp), ['wst'], ['wst'])
    A_(lambda E: E.activation(out=etot[:, :, :], in_=p3v, func=AF.Exp), ['pc3'], ['etot'])

    kb.barrier()
    es4.close()
    kb.es = es
    cwt = kb.sb([128, 10, 6], F32, "cwt")
    kb.dma(cwt[:, :, :], cw_d[:, :, :], w=['cwt'])
    PSH = {}

    def open_conv_psum():
        esc = ExitStack()
        kb.es = esc
        PSH['pTt'] = kb.ps([128, 8, 128], BF16, "pTt")
        PSH['pSS'] = kb.ps([128, 512], F32, "pSS")
        return esc
    segs = ((0, 256), (256, TA))

    def conv_chunk(ci, cin, cout):
        kb.dma(cin[:, :], fm_d[(4 + ci) * 128:(5 + ci) * 128, :], r=['fm_d'], w=['cin'])
        V_(lambda E: E.tensor_scalar(out=cout[:, :], in0=cin[:, :], scalar1=cwt[:, ci, 2:3], scalar2=cwt[:, ci, 5:6], op0=ALU.mult, op1=ALU.add), ['cin', 'cwt'], ['cout'])
        for k in (0, 1, 3, 4):
            dd = k - 2
            for (s0, s1) in segs:
                a, b = s0 + max(0, -dd), s1 - max(0, dd)
                V_(lambda E: E.scalar_tensor_tensor(out=cout[:, a:b], in0=cin[:, a + dd:b + dd], scalar=cwt[:, ci, k:k + 1], in1=cout[:, a:b],
                                                    op0=ALU.mult, op1=ALU.add), ['cin', 'cwt', 'cout'], ['cout'])
        A_(lambda E: E.activation(out=cout[:, :], in_=cout[:, :], func=AF.Silu), ['cout'], ['cout'])

    def to_tok(src_bf, skey, dst, dkey, c0):
        for t0 in range(0, NTA, 8):
            n = min(8, NTA - t0)
            for j in range(n):
                P_(lambda E: E.transpose(out=PSH['pTt'][:, j, :], in_=src_bf[:, (t0 + j) * 128:(t0 + j + 1) * 128], identity=ident_b[:, :]), [skey], ['pTt'])
            A_(lambda E: E.copy(out=dst[:, t0:t0 + n, c0:c0 + 128], in_=PSH['pTt'][:, 0:n, :]), ['pTt'], [dkey])

    def l2norm_to(cout, cin, dst, dkey, scale):
        A_(lambda E: E.activation(out=cin[:, :], in_=cout[:, :], func=AF.Square), ['cout'], ['cin'])
        for (t0, NT, v) in blocks:
            P_(lambda E: E.matmul(PSH['pSS'][:, 0:NT], lhsT=ones_f[:, :], rhs=cin[:, t0:t0 + NT], start=True, stop=True), ['ones', 'cin'], ['pSS'])
            V_(lambda E: E.tensor_scalar(out=cin[:, t0:t0 + NT], in0=PSH['pSS'][:, 0:NT], scalar1=1e-6, scalar2=None, op0=ALU.add), ['pSS', 'cin'], ['cin'])
        A_(lambda E: E.activation(out=cin[:, :], in_=cin[:, :], func=AF.Sqrt), ['cin'], ['cin'])
        V_(lambda E: E.reciprocal(out=cin[:, :], in_=cin[:, :]), ['cin'], ['cin'])
        V_(lambda E: E.scalar_tensor_tensor(out=dst, in0=cout[:, :], scalar=scale, in1=cin[:, :], op0=ALU.mult, op1=ALU.mult), ['cout', 'cin'], [dkey])

    order_f = list(range(NTA))
    order_b = [1, 0] + list(range(NTA - 1, 1, -1))

    def scan(units, p, yacc, es_):
        kb.es = es_
        lanes = []
        for ui, u in enumerate(units):
            for d in range(2):
                L = dict(u=u, d=d, id='%d_%d' % (ui, d))
                L['S'] = kb.sb([128, p], F32, "S" + L['id'])
                L['Sb'] = kb.sb([128, p], BF16, "Sb" + L['id'])
                L['labc'] = kb.sb([128, 128], F32, "labc" + L['id'])
                L['Dm'] = kb.sb([128, 128], F32, "Dm" + L['id'])
                L['PT'] = kb.sb([128, 128], BF16, "PT" + L['id'])
                L['Vb'] = kb.sb([128, p], BF16, "Vb" + L['id'])
                L['Vw'] = kb.sb([128, p], BF16, "Vw" + L['id'])
                L['yt'] = kb.sb([128, p], F32, "yt" + L['id'])
                if u['gdn']:
                    for nm in ('Es', 'N0', 'N1', 'A0', 'A1', 'P', 'r'):
                        L[nm] = kb.sb([128, 128], F32, nm + L['id'])
                V_(lambda E: E.memset(L['S'][:, :], 0.0), [], ['S' + L['id']])
                V_(lambda E: E.memset(L['Sb'][:, :], 0.0), [], ['Sb' + L['id']])
                lanes.append(L)
        pA = kb.ps([128, 512], F32, "pA")
        pKQ = kb.ps([128, 512], F32, "pKQ")
        pY = kb.ps([128, 512], F32, "pYs")
        pO = kb.ps([128, 512], F32, "pO")
        pS_ = kb.ps([128, 512], F32, "pSd")
        pG = [kb.ps([128, 512], F32, "pG%d" % i) for i in range(3)] if units[0]['gdn'] else None
        for step in range(NTA):
            for L in lanes:
                u, d, lid = L['u'], L['d'], L['id']
                t = (order_f if d == 0 else order_b)[step]
                col = u['col'](d)
                QT = u['QT'][:, t * 128:(t + 1) * 128]
                KT = u['KT'][:, t * 128:(t + 1) * 128]
                Ktok = u['Ktok'][:, t, :]
                k = lambda nm: nm + lid
                V_(lambda E: E.tensor_scalar(out=L['labc'][:, :], in0=ones_f[:, :], scalar1=la[:, t, col:col + 1], scalar2=None, op0=ALU.mult), ['ones', 'la'], [k('labc')])
                P_(lambda E: E.matmul(pA[:, 0:128], lhsT=L['labc'][:, :], rhs=tri[:, d, :], start=True, stop=True), [k('labc'), 'tri'], ['pA'])
                V_(lambda E: E.scalar_tensor_tensor(out=L['Dm'][:, :], in0=pA[:, 0:128], scalar=acs[:, t, col:col + 1], in1=tri[:, 2 + d, :],
                                                    op0=ALU.subtract, op1=ALU.add), ['pA', 'acs', 'tri'], [k('Dm')])
                A_(lambda E: E.activation(out=L['Dm'][:, :], in_=L['Dm'][:, :], func=AF.Exp), [k('Dm')], [k('Dm')])
                P_(lambda E: E.matmul(pKQ[:, 0:128], lhsT=KT, rhs=QT, start=True, stop=True), [u['KTk'], u['QTk']], ['pKQ'])
                V_(lambda E: E.tensor_tensor(out=L['PT'][:, :], in0=pKQ[:, 0:128], in1=L['Dm'][:, :], op=ALU.mult), ['pKQ', k('Dm')], [k('PT')])
                if u['gdn']:
                    bc = u['bcol'](d)
                    G_(lambda E: E.tensor_tensor(out=L['Es'][:, :], in0=L['Dm'][:, :], in1=ident_f[:, :], op=ALU.subtract), [k('Dm'), 'ident_f'], [k('Es')])
                    P_(lambda E: E.matmul(pG[0][:, 0:128], lhsT=KT, rhs=KT, start=True, stop=True), [u['KTk']], ['pG0'])
                    N, A, Nn, An = L['N0'], L['A0'], L['N1'], L['A1']
                    nk = [k('N0'), k('A0'), k('N1'), k('A1')]
                    V_(lambda E: E.scalar_tensor_tensor(out=N[:, :], in0=pG[0][:, 0:128], scalar=beta[:, t, bc:bc + 1], in1=L['Es'][:, :],
                                                        op0=ALU.mult, op1=ALU.mult), ['pG0', 'beta', k('Es')], [nk[0]])
                    P_(lambda E: E.transpose(out=pG[1][:, 0:128], in_=N[:, :], identity=ident_f[:, :]), [nk[0], 'ident_f'], ['pG1'])
                    A_(lambda E: E.copy(out=A[:, :], in_=pG[1][:, 0:128]), ['pG1'], [nk[1]])
                    G_(lambda E: E.tensor_tensor(out=L['P'][:, :], in0=ident_f[:, :], in1=N[:, :], op=ALU.subtract), [nk[0], 'ident_f'], [k('P')])
                    for it in range(1, 7):
                        P_(lambda E: E.matmul(pG[1][:, 0:128], lhsT=N[:, :], rhs=A[:, :], start=True, stop=True), [nk[0], nk[1]], ['pG1'])
                        A_(lambda E: E.copy(out=An[:, :], in_=pG[1][:, 0:128]), ['pG1'], [nk[3]])
                        if it < 6:
                            P_(lambda E: E.matmul(pG[0][:, 0:128], lhsT=A[:, :], rhs=N[:, :], start=True, stop=True), [nk[0], nk[1]], ['pG0'])
                            V_(lambda E: E.tensor_copy(out=Nn[:, :], in_=pG[0][:, 0:128]), ['pG0'], [nk[2]])
                        P_(lambda E: E.matmul(pG[2][:, 0:128], lhsT=An[:, :], rhs=L['P'][:, :], start=True, stop=True), [nk[3], k('P')], ['pG2'])
                        V_(lambda E: E.tensor_tensor(out=L['P'][:, :], in0=L['P'][:, :], in1=pG[2][:, 0:128], op=ALU.add), ['pG2', k('P')], [k('P')])
                        N, A, Nn, An = Nn, An, N, A
                        nk = [nk[2], nk[3], nk[0], nk[1]]
                    P_(lambda E: E.matmul(pO[:, 0:p], lhsT=KT, rhs=L['Sb'][:, :], start=True, stop=True), [u['KTk'], k('Sb')], ['pO'])
                    V_(lambda E: E.scalar_tensor_tensor(out=L['r'][:, :], in0=pO[:, 0:p], scalar=neacs[:, t, col:col + 1], in1=u['vtok'][:, t, :],
                                                        op0=ALU.mult, op1=ALU.add), ['pO', 'neacs', u['vk']], [k('r')])
                    P_(lambda E: E.matmul(pS_[:, 0:p], lhsT=L['P'][:, :], rhs=L['r'][:, :], start=True, stop=True), [k('P'), k('r')], ['pSd'])
                    V_(lambda E: E.tensor_scalar(out=L['Vb'][:, :], in0=pS_[:, 0:p], scalar1=beta[:, t, bc:bc + 1], scalar2=None, op0=ALU.mult), ['pSd', 'beta'], [k('Vb')])
                else:
                    u['vfn'](d, t, L['Vb'], k('Vb'))
                P_(lambda E: E.matmul(pY[:, 0:p], lhsT=L['PT'][:, :], rhs=L['Vb'][:, :], start=True, stop=True), [k('PT'), k('Vb')], ['pY'])
                P_(lambda E: E.matmul(pO[:, 0:p], lhsT=QT, rhs=L['Sb'][:, :], start=True, stop=True), [u['QTk'], k('Sb')], ['pO'])
                ys = yacc[:, t, u['y0']:u['y0'] + p]
                V_(lambda E: E.scalar_tensor_tensor(out=L['yt'][:, :], in0=pO[:, 0:p], scalar=eacs[:, t, col:col + 1], in1=ys, op0=ALU.mult, op1=ALU.add),
                   ['pO', 'eacs', 'yacc%d' % t], [k('yt')])
                V_(lambda E: E.tensor_tensor(out=ys, in0=L['yt'][:, :], in1=pY[:, 0:p], op=ALU.add), [k('yt'), 'pY'], ['yacc%d' % t])
                V_(lambda E: E.tensor_scalar(out=L['Vw'][:, :], in0=L['Vb'][:, :], scalar1=wst[:, t, col:col + 1], scalar2=None, op0=ALU.mult), [k('Vb'), 'wst'], [k('Vw')])
                P_(lambda E: E.matmul(pS_[:, 0:p], lhsT=Ktok, rhs=L['Vw'][:, :], start=True, stop=True), [u['Ktokk'], k('Vw')], ['pSd'])
                V_(lambda E: E.scalar_tensor_tensor(out=L['S'][:, :], in0=L['S'][:, :], scalar=etot[:, t, col:col + 1], in1=pS_[:, 0:p], op0=ALU.mult, op1=ALU.add),
                   ['pSd', 'etot', k('S')], [k('S')])
                A_(lambda E: E.copy(out=L['Sb'][:, :], in_=L['S'][:, :]), [k('S')], [k('Sb')])

    def post_store(gsrc, gkey, row0, t, ob, pTo):
        for j in range(2):
            P_(lambda E: E.transpose(out=pTo[:, j, :], in_=gsrc[:, j * 128:(j + 1) * 128], identity=ident_b[:, :]), [gkey], ['pTo'])
        A_(lambda E: E.copy(out=ob[:, :, :], in_=pTo[:, :, :]), ['pTo'], ['ob'])
        for j in range(2):
            kb.dma(Ov[row0 + j * 128:row0 + (j + 1) * 128, t, :], ob[:, j, :], r=['ob'], w=['brT_d'])

    es5 = ExitStack()
    kb.es = es5
    cin = kb.sb([128, TA], F32, "cin")
    cout = kb.sb([128, TA], F32, "cout")
    cbf = kb.sb([128, TA], BF16, "cbf")
    BT = kb.sb([128, TA], BF16, "BT")
    CT = kb.sb([128, TA], BF16, "CT")
    xtok = kb.sb([128, NTA, 256], BF16, "xtok")
    Btok = kb.sb([128, NTA, 128], BF16, "Btok")
    yacc = kb.sb([128, NTA, 256], F32, "yacc")
    G_(lambda E: E.memset(yacc[:, :, :], 0.0), [], ['yacc%d' % t for t in range(NTA)])
    esc = open_conv_psum()
    for ci in range(2):
        conv_chunk(ci, cin, cout)
        A_(lambda E: E.copy(out=cbf[:, :], in_=cout[:, :]), ['cout'], ['cbf'])
        to_tok(cbf, 'cbf', xtok, 'xtok', ci * 128)
    conv_chunk(2, cin, cout)
    A_(lambda E: E.copy(out=BT[:, :], in_=cout[:, :]), ['cout'], ['BT'])
    to_tok(BT, 'BT', Btok, 'Btok', 0)
    conv_chunk(3, cin, cout)
    A_(lambda E: E.copy(out=CT[:, :], in_=cout[:, :]), ['cout'], ['CT'])

    def ssd_unit(h):
        def vfn(d, t, Vb, vkey):
            V_(lambda E: E.tensor_scalar(out=Vb[:, :], in0=xtok[:, t, h * 64:(h + 1) * 64], scalar1=sp16[:, t, d * 4 + h:d * 4 + h + 1], scalar2=None, op0=ALU.mult),
               ['xtok', 'sp16'], [vkey])
        return dict(QT=CT, QTk='CT', KT=BT, KTk='BT', Ktok=Btok, Ktokk='Btok', col=lambda d: d * 4 + h, vfn=vfn, gdn=False, y0=h * 64)
    kb.barrier()
    esc.close()
    pu_l = nc.dram_tensor("peer_u_%d" % l, [16384, D], F32, kind="ExternalInput").ap()
    pv_l = nc.dram_tensor("peer_v_%d" % l, [16384, D], F32, kind="ExternalInput").ap()
    C['pu_%d' % l], C['pv_%d' % l] = pu_l, pv_l
    for i in range(16):
        kb.dma(C['pub'][i * 1024:(i + 1) * 1024, :], pu_l[i * 1024:(i + 1) * 1024, :], w=['pub%d' % i], eng='pool')
        kb.dma(C['pvb'][i * 1024:(i + 1) * 1024, :], pv_l[i * 1024:(i + 1) * 1024, :], w=['pvb%d' % i], eng='pool')
    es5s = ExitStack()
    scan([ssd_unit(h) for h in range(4)], 64, yacc, es5s)
    kb.barrier()
    es5s.close()
    kb.es = es5
    zt = kb.sb([128, 256], F32, "zt")
    g1 = kb.sb([128, 256], F32, "g1")
    g2 = kb.sb([128, 256], F32, "g2")
    gb = kb.sb([128, 256], BF16, "gbf")
    ss = kb.sb([128, 2], F32, "ss")
    ob = kb.sb([128, 2, 128], BF16, "ob")
    pTo = kb.ps([128, 2, 128], BF16, "pTo")
    for t in range(NTA):
        kb.dma(zt[:, :], tm_d[t * 128:(t + 1) * 128, 256:512], r=['tm_d'], w=['zt'])
        A_(lambda E: E.activation(out=zt[:, :], in_=zt[:, :], func=AF.Silu), ['zt'], ['zt'])
        V_(lambda E: E.tensor_tensor(out=g1[:, :], in0=xtok[:, t, :], in1=rowp[:, 0, :], op=ALU.mult), ['xtok', 'rowp'], ['g1'])
        V_(lambda E: E.tensor_tensor(out=g1[:, :], in0=g1[:, :], in1=yacc[:, t, :], op=ALU.add), ['g1', 'yacc%d' % t], ['g1'])
        V_(lambda E: E.tensor_tensor(out=g1[:, :], in0=g1[:, :], in1=zt[:, :], op=ALU.mult), ['g1', 'zt'], ['g1'])
        V_(lambda E: E.scalar_tensor_tensor(out=g2[:, :], in0=g1[:, :], scalar=1.0 / 256.0, in1=g1[:, :], op0=ALU.mult, op1=ALU.mult, accum_out=ss[:, 0:1]), ['g1'], ['g2', 'ss'])
        V_(lambda E: E.tensor_scalar(out=ss[:, 0:1], in0=ss[:, 0:1], scalar1=EPS, scalar2=None, op0=ALU.add), ['ss'], ['ss'])
        A_(lambda E: E.activation(out=ss[:, 0:1], in_=ss[:, 0:1], func=AF.Sqrt), ['ss'], ['ss'])
        V_(lambda E: E.reciprocal(out=ss[:, 0:1], in_=ss[:, 0:1]), ['ss'], ['ss'])
        V_(lambda E: E.scalar_tensor_tensor(out=gb[:, :], in0=g1[:, :], scalar=ss[:, 0:1], in1=rowp[:, 1, :], op0=ALU.mult, op1=ALU.mult), ['g1', 'ss', 'rowp'], ['gb'])
        post_store(gb, 'gb', 768, t, ob, pTo)
    kb.barrier()
    es5.close()

    es6 = ExitStack()
    kb.es = es6
    cin = kb.sb([128, TA], F32, "cin6")
    cout = kb.sb([128, TA], F32, "cout6")
    cbf = kb.sb([128, TA], BF16, "cbf6")
    qT = [kb.sb([128, TA], BF16, "qT%d" % h) for h in range(2)]
    kT = [kb.sb([128, TA], BF16, "kT%d" % h) for h in range(2)]
    ktok = [kb.sb([128, NTA, 128], BF16, "ktok%d" % h) for h in range(2)]
    vtok = [kb.sb([128, NTA, 128], BF16, "vtok%d" % h) for h in range(2)]
    oacc = kb.sb([128, NTA, 256], F32, "oacc")
    G_(lambda E: E.memset(oacc[:, :, :], 0.0), [], ['yacc%d' % t for t in range(NTA)])
    esc = open_conv_psum()
    for h in range(2):
        conv_chunk(4 + h, cin, cout)
        l2norm_to(cout, cin, qT[h][:, :], 'qT%d' % h, 128 ** -0.5)
        conv_chunk(6 + h, cin, cout)
        l2norm_to(cout, cin, kT[h][:, :], 'kT%d' % h, 1.0)
        to_tok(kT[h], 'kT%d' % h, ktok[h], 'ktok%d' % h, 0)
        conv_chunk(8 + h, cin, cout)
        A_(lambda E: E.copy(out=cbf[:, :], in_=cout[:, :]), ['cout'], ['cbf'])
        to_tok(cbf, 'cbf', vtok[h], 'vtok%d' % h, 0)

    def gdn_unit(h):
        return dict(QT=qT[h], QTk='qT%d' % h, KT=kT[h], KTk='kT%d' % h, Ktok=ktok[h], Ktokk='ktok%d' % h, col=lambda d: 8 + d * 2 + h,
                    bcol=lambda d: d * 2 + h, vtok=vtok[h], vk='vtok%d' % h, gdn=True, y0=h * 128)
    kb.barrier()
    esc.close()
    es6s = ExitStack()
    scan([gdn_unit(h) for h in range(2)], 128, oacc, es6s)
    kb.barrier()
    es6s.close()
    kb.es = es6
    zt = kb.sb([128, 256], F32, "zt6")
    g1 = kb.sb([128, 256], F32, "g16")
    g2 = kb.sb([128, 256], F32, "g26")
    gb = kb.sb([128, 256], BF16, "gbf6")
    ss = kb.sb([128, 2], F32, "ss6")
    ob = kb.sb([128, 2, 128], BF16, "ob6")
    pTo = kb.ps([128, 2, 128], BF16, "pTo6")
    for t in range(NTA):
        kb.dma(zt[:, :], tm_d[t * 128:(t + 1) * 128, 512:768], r=['tm_d'], w=['zt'])
        A_(lambda E: E.activation(out=zt[:, :], in_=zt[:, :], func=AF.Silu), ['zt'], ['zt'])
        for h in range(2):
            hs = slice(h * 128, (h + 1) * 128)
            V_(lambda E: E.scalar_tensor_tensor(out=g2[:, hs], in0=oacc[:, t, hs], scalar=1.0 / 128.0, in1=oacc[:, t, hs], op0=ALU.mult, op1=ALU.mult,
                                                accum_out=ss[:, h:h + 1]), ['yacc%d' % t], ['g2', 'ss'])
        V_(lambda E: E.tensor_scalar(out=ss[:, :], in0=ss[:, :], scalar1=EPS, scalar2=None, op0=ALU.add), ['ss'], ['ss'])
        A_(lambda E: E.activation(out=ss[:, :], in_=ss[:, :], func=AF.Sqrt), ['ss'], ['ss'])
        V_(lambda E: E.reciprocal(out=ss[:, :], in_=ss[:, :]), ['ss'], ['ss'])
        for h in range(2):
            hs = slice(h * 128, (h + 1) * 128)
            V_(lambda E: E.scalar_tensor_tensor(out=g1[:, hs], in0=oacc[:, t, hs], scalar=ss[:, h:h + 1], in1=rowp[:, 2, hs], op0=ALU.mult, op1=ALU.mult),
               ['yacc%d' % t, 'ss', 'rowp'], ['g1'])
        V_(lambda E: E.tensor_tensor(out=gb[:, :], in0=g1[:, :], in1=zt[:, :], op=ALU.mult), ['g1', 'zt'], ['gb'])
        post_store(gb, 'gb', 1024, t, ob, pTo)
    kb.barrier()
    es6.close()
    kb.barrier()
    es.close()


def _cnt(n, win):
    lo = win // 2
    hi = win - 1 - lo
    t = np.arange(n)
    return (np.minimum(t + hi, n - 1) - np.maximum(t - lo, 0) + 1).astype(np.float64)


def consts_A():
    c = consts()
    if 'dft_x' not in c:
        import ml_dtypes
        bf = ml_dtypes.bfloat16
        t = np.arange(4096, dtype=np.int64)
        m = (t[:, None] * t[None, :]) % 4096
        ang = 2.0 * np.pi * m.astype(np.float64) / 4096.0
        c['dft_x'] = np.stack([(np.cos(ang) / 64.0).astype(np.float32).astype(bf), (np.sin(ang) / 64.0).astype(np.float32).astype(bf)], 0)
        t = np.arange(256, dtype=np.int64)
        ang = 2.0 * np.pi * ((t[:, None] * t[None, :]) % 256).astype(np.float64) / 256.0
        c['dft_c'] = np.stack([(np.cos(ang) / 16.0).astype(np.float32).astype(bf), (np.sin(ang) / 16.0).astype(np.float32).astype(bf)], 0)
        t = np.arange(128, dtype=np.int64)
        ang = 2.0 * np.pi * ((t[:, None] * t[None, :]) % 128).astype(np.float64) / 128.0
        c['dft_ch'] = np.stack([np.cos(ang) / np.sqrt(128.0), -np.sin(ang) / np.sqrt(128.0)], 0).astype(np.float32)
        j = np.arange(128)[:, None]
        i = np.arange(128)[None, :]
        c['tri'] = np.stack([(j <= i).astype(np.float32), (j >= i).astype(np.float32),
                             np.where(i >= j, 0.0, NEG).astype(np.float32), np.where(i <= j, 0.0, NEG).astype(np.float32)], 0)
        c['invcnt'] = np.stack([(1.0 / (_cnt(64, w)[:, None] * _cnt(64, w)[None, :])).reshape(-1) for w in POOL_WINDOWS], 0).astype(np.float32)
        c['invcnt_c'] = np.stack([1.0 / _cnt(256, w) for w in POOL_WINDOWS], 0).astype(np.float32)
    return c


def layer_weights_A(inp, l, s):
    w_in = inp['w_in'][l]
    cat = np.concatenate
    fm_cols = cat([np.arange(0, 512), 1024 + s * 256 + np.arange(256), 1536 + s * 128 + np.arange(128), 1792 + s * 128 + np.arange(128),
                   2576 + s * 256 + np.arange(256), 3088 + s * 256 + np.arange(256), 3600 + s * 256 + np.arange(256)])
    tm_cols = cat([512 + s * 256 + np.arange(256), 2048 + s * 256 + np.arange(256), 4112 + s * 256 + np.arange(256),
                   2560 + 4 * s + np.arange(4), 2568 + 4 * s + np.arange(4), 4624 + 2 * s + np.arange(2), 4628 + 2 * s + np.arange(2),
                   4632 + 2 * s + np.arange(2), 4636 + 2 * s + np.arange(2)])
    scw, scb, gcw = inp['ssd_conv_w'][l], inp['ssd_conv_b'][l], inp['gdn_conv_w'][l]
    ch_ssd = cat([s * 256 + np.arange(256), 512 + s * 128 + np.arange(128), 768 + s * 128 + np.arange(128)])
    ch_gdn = cat([s * 256 + np.arange(256), 512 + s * 256 + np.arange(256), 1024 + s * 256 + np.arange(256)])
    cw = np.zeros((10, 128, 6), np.float32)
    cw[0:4, :, 0:5] = scw[:, ch_ssd].T.reshape(4, 128, 5)
    cw[0:4, :, 5] = scb[ch_ssd].reshape(4, 128)
    cw[4:10, :, 0:5] = gcw[:, ch_gdn].T.reshape(6, 128, 5)
    rowp = np.zeros((5, 256), np.float32)
    rowp[0, 0:4] = inp['ssd_dt_bias'][l][0, 4 * s:4 * s + 4]
    rowp[0, 4:8] = inp['ssd_dt_bias'][l][1, 4 * s:4 * s + 4]
    rowp[0, 12:14] = inp['gdn_dt_bias'][l][0, 2 * s:2 * s + 2]
    rowp[0, 14:16] = inp['gdn_dt_bias'][l][1, 2 * s:2 * s + 2]
    rowp[0, 16:20] = inp['ssd_a_log'][l][0, 4 * s:4 * s + 4]
    rowp[0, 20:24] = inp['ssd_a_log'][l][1, 4 * s:4 * s + 4]
    rowp[0, 28:30] = inp['gdn_a_log'][l][0, 2 * s:2 * s + 2]
    rowp[0, 30:32] = inp['gdn_a_log'][l][1, 2 * s:2 * s + 2]
    rowp[1] = np.repeat(inp['ssd_d'][l][4 * s:4 * s + 4], 64)
    rowp[2] = inp['ssd_norm_w'][l][s * 256:(s + 1) * 256]
    rowp[3] = np.tile(inp['gdn_norm_w'][l], 2)
    c = consts_A()
    return {
        'w_mod': np.ascontiguousarray(inp['w_mod'][l][:, :2 * D]), 'b_mod': np.ascontiguousarray(inp['b_mod'][l][:2 * D]),
        'wa_fm': np.ascontiguousarray(w_in[:, fm_cols]), 'wa_tm': np.ascontiguousarray(w_in[:, tm_cols]),
        'convw': np.ascontiguousarray(cw.transpose(1, 0, 2)), 'pool_w': inp['pool_w'][l],
        'pool_scale': np.ascontiguousarray(inp['pool_scale'][l].reshape(4, 128).T), 'invcnt': c['invcnt'], 'invcnt_c': c['invcnt_c'],
        'rowp': rowp, 'dft_x': c['dft_x'], 'dft_c': c['dft_c'], 'dft_ch': c['dft_ch'], 'tri': c['tri'], 'ident': c['ident'],
    }


RG_PAIRS = [[0, 1], [2, 3], [4, 5], [6, 7]]
DEPTH = 4
O_CH = [(i * 6400, min((i + 1) * 6400, NTA * 1280)) for i in range(7)]
X_CH = [(i * 512, min((i + 1) * 512, NTOK)) for i in range(5)]


def _gathered_row(chunks, rank, j):
    for (a, b) in chunks:
        if a <= j < b:
            return 2 * a + rank * (b - a) + (j - a)
    raise ValueError(j)


def build_fused(depth=DEPTH):
    es = ExitStack()
    nc = bass.Bass("TRN2", target_bir_lowering=False)
    kb = KB(nc, es)
    kb.es_top = es
    di = lambda n, s, d=F32: nc.dram_tensor(n, list(s), d, kind="ExternalInput").ap()
    dint = lambda n, s, d=F32: nc.dram_tensor(n, list(s), d, addr_space="Local", kind="Internal").ap()
    C = {
        'cvec': di("cvec", [2, 128, 8]), 'ident': di("ident", [128, 128]), 'iota16': di("iota16", [128, 16]),
        'idxB': di("idxB", [128, 17, 16], U32), 'invcnt': di("invcnt", [4, 4096]), 'invcnt_c': di("invcnt_c", [4, 256]),
        'dft_x': di("dft_x", [2, 4096, 4096], BF16), 'dft_c': di("dft_c", [2, 256, 256], BF16), 'dft_ch': di("dft_ch", [2, 128, 128]),
        'tri': di("tri", [4, 128, 128]),
        'pub': dint("pub_scr", [16384, D], BF16), 'pvb': dint("pvb_scr", [16384, D], BF16),
        'x1': dint("x1_scr", [NTOK, D]), 'tm': dint("tm_scr", [TA, 784]), 'fm': dint("fm_scr", [1792, TA]),
    }
    for l in range(depth):
        C['w_mod_%d' % l] = di("w_mod_%d" % l, [D, 6 * D])
        C['b_mod_%d' % l] = di("b_mod_%d" % l, [6 * D])
    xa0 = di("xa0", [TA, D])
    xmy0 = di("xmy0", [NTOK, D])
    out_d = nc.dram_tensor("out", [NTOK, D], F32, kind="ExternalOutput").ap()
    O_d = dint("O_loc", [NTA * 1280, 128], BF16)
    G_d = dint("G_all", [2 * NTA * 1280, 128], BF16)
    xg = [xa0] + [dint("xg_%d" % l, [TA, D]) for l in range(1, depth)]
    xmy = [xmy0] + [dint("xmy_%d" % l, [NTOK, D]) for l in range(1, depth)]
    for l in range(depth):
        emit_A(nc, kb, l, C, xg[l], O_d)
        kb.collective("AllGather", [O_d[a:b, :] for (a, b) in O_CH], [G_d[2 * a:2 * b, :] for (a, b) in O_CH])
        o = out_d if l == depth - 1 else xmy[l + 1]
        emit_B(nc, kb, l, C, xmy[l], G_d, o)
        if l < depth - 1:
            kb.collective("AllGather", [xmy[l + 1][a:b, :] for (a, b) in X_CH], [xg[l + 1][2 * a:2 * b, :] for (a, b) in X_CH])
    kb.finish()
    return nc, es


def idx_table(s):
    idx = np.zeros((128, 17, 16), np.uint32)
    p = np.arange(128)
    for tau in range(17):
        T = s if tau == 0 else 2 + 16 * s + (tau - 1)
        for kc in range(16):
            br, j = kc // 4, kc % 4
            if br == 0:
                rs, row0 = 0, j * 128
            else:
                rs, row0 = j // 2, 512 + (br - 1) * 256 + (j % 2) * 128
            idx[:, tau, kc] = _gathered_row(O_CH, rs, T * 1280 + row0) + p
    return idx


_PROG = {}


def kernel(**inp):
    inp = {k: np.asarray(v) for k, v in inp.items()}
    depth = inp['w_in'].shape[0]
    if 'nc' not in _PROG:
        _PROG['nc'] = build_fused(depth)
    nc, _ = _PROG['nc']
    cA = consts_A()
    shared = {'ident': cA['ident'], 'iota16': cA['iota16'], 'invcnt': cA['invcnt'], 'invcnt_c': cA['invcnt_c'],
              'dft_x': cA['dft_x'], 'dft_c': cA['dft_c'], 'dft_ch': cA['dft_ch'], 'tri': cA['tri']}
    skipA = ('w_mod', 'b_mod', 'invcnt', 'invcnt_c', 'dft_x', 'dft_c', 'dft_ch', 'tri', 'ident')
    skipB = ('ident', 'iota16')
    per_half = []
    for s in range(2):
        m = {}
        for l in range(depth):
            for k, v in layer_weights_A(inp, l, s).items():
                if k not in skipA:
                    m['%s_%d' % (k, l)] = v
        m['idxB'] = idx_table(s)
        per_half.append(m)
    common = dict(shared)
    for l in range(depth):
        for k, v in layer_weights_B(inp, l).items():
            if k not in skipB:
                common['%s_%d' % (k, l)] = v
    x, ctx = inp['x'], inp['ctx']
    in_maps = []
    for core in range(8):
        b, s = core // 2, core % 2
        halves = [np.concatenate([ctx[b, r * 128:(r + 1) * 128], x[b, r * 2048:(r + 1) * 2048]], 0) for r in range(2)]
        m = dict(common)
        m.update(per_half[s])
        m['xa0'] = np.ascontiguousarray(np.concatenate([ctx[b], x[b]], 0))
        m['xmy0'] = np.ascontiguousarray(halves[s])
        m['cvec'] = cvec_layout(inp['c_ctx'], inp['c'][b])
        in_maps.append(m)
    res = run_bass_kernel_spmd(nc, in_maps, core_ids=list(range(8))).results
    out = np.empty_like(x)
    for core in range(8):
        b, s = core // 2, core % 2
        out[b, s * 2048:(s + 1) * 2048] = np.asarray(res[core]['out'])[128:]
    return out
```

```python
import numpy as np
from contextlib import ExitStack
import concourse.bass as bass
import concourse.mybir as mybir
from concourse.bass_utils import run_bass_kernel_spmd

F32 = mybir.dt.float32
BF16 = mybir.dt.bfloat16
U32 = mybir.dt.uint32
AF = mybir.ActivationFunctionType
ALU = mybir.AluOpType
AX = mybir.AxisListType

NDS = 64
NHW = 16


class KB:
    def __init__(self, nc, es):
        self.nc, self.es = nc, es
        self.engs = {'pe': nc.tensor, 'act': nc.scalar, 'dve': nc.vector, 'pool': nc.gpsimd, 'sp': nc.sync}
        self.sem, self.cnt = {}, {}
        for e in ('pe', 'act', 'dve', 'pool'):
            self.sem[('e', e)] = es.enter_context(nc.semaphore('s_' + e))
            self.cnt[e] = 0
        self.dcnt = [0] * NDS
        for i in range(NDS):
            self.sem[('d', i)] = es.enter_context(nc.semaphore('d%d' % i))
        self.dnext = 0
        self.dnext_sw = 0
        self.seen = {e: {} for e in self.engs}
        self.lastw, self.readers = {}, {}
        self.nbuf = 0
        self.snap = {}
        self.snap_order = []

    def sb(self, shape, dt, name=None):
        self.nbuf += 1
        return self.es.enter_context(self.nc.sbuf_tensor('sb%d_' % self.nbuf + (name or 'b'), list(shape), dt))

    def ps(self, shape, dt, name=None):
        self.nbuf += 1
        return self.es.enter_context(self.nc.psum_tensor('ps%d_' % self.nbuf + (name or 'p'), list(shape), dt))

    def op(self, eng, fn, r=(), w=(), dma=False):
        deps = {}

        def need(tok):
            if tok is not None and deps.get(tok[0], 0) < tok[1]:
                deps[tok[0]] = tok[1]
        for key in r:
            need(self.lastw.get(key))
        for key in w:
            need(self.lastw.get(key))
            for k, v in self.readers.get(key, {}).items():
                need((k, v))
        E = self.engs[eng]
        if dma:
            if eng == 'pool':
                i = NHW + self.dnext_sw
                self.dnext_sw = (self.dnext_sw + 1) % (NDS - NHW)
            else:
                i = self.dnext
                self.dnext = (i + 1) % NHW
            if self.dcnt[i] > 0:
                need((('d', i), self.dcnt[i]))
        self._wait(eng, E, deps)
        inst = fn(E)
        if dma:
            self.dcnt[i] += 16
            inst.then_inc(self.sem[('d', i)], 16)
            tok = (('d', i), self.dcnt[i])
        else:
            self.cnt[eng] += 1
            inst.then_inc(self.sem[('e', eng)], 1)
            tok = (('e', eng), self.cnt[eng])
        self.snap[tok] = dict(self.seen[eng])
        self.snap_order.append(tok)
        if len(self.snap_order) > 6000:
            for t_ in self.snap_order[:2000]:
                self.snap.pop(t_, None)
            del self.snap_order[:2000]
        for key in r:
            d = self.readers.setdefault(key, {})
            if d.get(tok[0], 0) < tok[1]:
                d[tok[0]] = tok[1]
        for key in w:
            self.lastw[key] = tok
            self.readers[key] = {}
        return tok

    def _wait(self, eng, E, deps):
        sn = self.seen[eng]
        for k, v in sorted(deps.items(), key=lambda kv: -kv[1]):
            if eng == 'pe' and k == ('e', 'pe'):
                continue
            if sn.get(k, 0) >= v:
                continue
            E.wait_ge(self.sem[k], v)
            sn[k] = v
            inh = self.snap.get((k, v))
            if inh:
                for k2, v2 in inh.items():
                    if sn.get(k2, 0) < v2:
                        sn[k2] = v2

    def prewait(self, eng, wkeys):
        deps = {}
        for key in wkeys:
            for tok in [self.lastw.get(key)] + list(self.readers.get(key, {}).items()):
                if tok is not None and deps.get(tok[0], 0) < tok[1]:
                    deps[tok[0]] = tok[1]
        self._wait(eng, self.engs[eng], deps)

    def dma(self, out, in_, r=(), w=(), eng='sp', **kw):
        return self.op(eng, lambda E: E.dma_start(out=out, in_=in_, **kw), r=r, w=w, dma=True)

    def finish(self):
        E = self.engs['sp']
        for i in range(NDS):
            if self.dcnt[i] > 0 and self.seen['sp'].get(('d', i), 0) < self.dcnt[i]:
                E.wait_ge(self.sem[('d', i)], self.dcnt[i])
        for e in ('pe', 'act', 'dve', 'pool'):
            if self.cnt[e] > 0:
                E.wait_ge(self.sem[('e', e)], self.cnt[e])

    def barrier(self):
        for en, E in self.engs.items():
            for i in range(NDS):
                if self.dcnt[i] > 0 and self.seen[en].get(('d', i), 0) < self.dcnt[i]:
                    E.wait_ge(self.sem[('d', i)], self.dcnt[i])
                    self.seen[en][('d', i)] = self.dcnt[i]
            for e in ('pe', 'act', 'dve', 'pool'):
                if e != en and self.cnt[e] > 0 and self.seen[en].get(('e', e), 0) < self.cnt[e]:
                    E.wait_ge(self.sem[('e', e)], self.cnt[e])
                    self.seen[en][('e', e)] = self.cnt[e]
        self.lastw, self.readers = {}, {}


    def collective(self, kind, ins, outs):
        E = self.engs['pool']
        if not hasattr(self, 'ccsem'):
            self.ccsem = self.es_top.enter_context(self.nc.semaphore('cc_sem'))
            self.sem[('c', 0)] = self.ccsem
            self.cccnt = 0
        self.barrier()
        for i_, o_ in zip(ins, outs):
            E.collective_compute(kind, ALU.bypass, replica_groups=RG_PAIRS, ins=[i_], outs=[o_]).then_inc(self.ccsem)
            self.cccnt += 1
        for en, EE in self.engs.items():
            EE.wait_ge(self.ccsem, self.cccnt)
            self.seen[en][('c', 0)] = self.cccnt


D = 1024
ALPHA = 8 ** 0.25
EPS = 1e-6
NTOK = 2176
NEG = -1.0e30


def ln_normalize(kb, xin, xkey, hn, hnkey, st, mv, rstd, sfx=''):
    k0, k1, km, kr = 'st0' + sfx, 'st1' + sfx, 'mv' + sfx, 'rstd' + sfx
    kb.op('dve', lambda E: E.bn_stats(out=st[:, 0, :], in_=xin[:, 0:512]), r=[xkey], w=[k0])
    kb.op('dve', lambda E: E.bn_stats(out=st[:, 1, :], in_=xin[:, 512:1024]), r=[xkey], w=[k1])
    kb.op('dve', lambda E: E.bn_aggr(out=mv[:, :], in_=st[:, :, :]), r=[k0, k1], w=[km])
    kb.op('dve', lambda E: E.tensor_scalar(out=rstd[:, :], in0=mv[:, 1:2], scalar1=EPS, scalar2=None,
                                           op0=ALU.add), r=[km], w=[kr])
    kb.op('act', lambda E: E.activation(out=rstd[:, :], in_=rstd[:, :], func=AF.Sqrt), r=[kr], w=[kr])
    kb.op('dve', lambda E: E.reciprocal(out=rstd[:, :], in_=rstd[:, :]), r=[kr], w=[kr])
    kb.op('dve', lambda E: E.tensor_scalar(out=hn, in0=xin, scalar1=mv[:, 0:1], scalar2=rstd[:, 0:1],
                                           op0=ALU.subtract, op1=ALU.mult), r=[xkey, km, kr], w=[hnkey])


def emit_B(nc, kb, l, C, x_d, G_d, out_d):
    es = ExitStack()
    kb.es = es
    dt = lambda n, s, d=F32, kind="ExternalInput": nc.dram_tensor(n + "_%d" % l, list(s), d, kind=kind).ap()
    cvec_d = C['cvec']
    wmod_d = C['w_mod_%d' % l]
    bmod_d = C['b_mod_%d' % l]
    wg_d = dt("wg", [8, D, 512])
    wb_d = dt("wb", [2048, D])
    wo_d = dt("wo", [D, D])
    lnp_d = dt("lnp", [4, D])
    wq_d = dt("wq", [D, 2048])
    keysT_d = dt("keysT", [128, 16, 128])
    pub_d, pvb_d = C['pub'], C['pvb']
    ident_d, iota_d, x1_d, idx_d = C['ident'], C['iota16'], C['x1'], C['idxB']
    idxb = kb.sb([128, 17, 16], U32, "idxb")
    kb.dma(idxb[:, :, :], idx_d[:, :, :], w=['idxb'])

    ident_f = kb.sb([128, 128], F32, "ident_f")
    ident_b = kb.sb([128, 128], BF16, "ident_b")
    ones_b = kb.sb([128, 128], F32, "ones_f")
    iota16 = kb.sb([128, 16], F32, "iota16")
    rows = kb.sb([128, 2, 6, D], F32, "modrows")
    lnrow = kb.sb([128, 4, D], F32, "lnrow")
    es0 = ExitStack()
    kb.es = es0
    cs = kb.sb([128, 2, 8], F32, "cs")
    csrep = kb.sb([128, 2, 8, 128], BF16, "csrep")
    bmrow = kb.sb([128, 512], F32, "bmrow")
    wmb = [kb.sb([128, 8, 512], BF16, "wmb%d" % i) for i in range(2)]
    pmod = [kb.ps([128, 512], F32, "pmod%d" % i) for i in range(2)]

    kb.dma(ident_f[:, :], ident_d[:, :], w=['ident_f'])
    kb.dma(iota16[:, :], iota_d[:, :], w=['iota16'])
    kb.op('dve', lambda E: E.tensor_copy(out=ident_b[:, :], in_=ident_f[:, :]), r=['ident_f'], w=['ident_b'])
    kb.op('pool', lambda E: E.memset(ones_b[:, :], 1.0), w=['ones'])
    for v in range(2):
        kb.dma(cs[:, v, :], cvec_d[v, :, :], w=['cs%d' % v])
        kb.op('act', lambda E: E.activation(out=cs[:, v, :], in_=cs[:, v, :], func=AF.Silu), r=['cs%d' % v], w=['cs%d' % v])
        for k in range(8):
            kb.op('dve', lambda E: E.tensor_scalar(out=csrep[:, v, k, :], in0=ones_b[:, :], scalar1=cs[:, v, k:k + 1],
                                                   scalar2=None, op0=ALU.mult), r=['cs%d' % v, 'ones'], w=['csrep%d' % v])
    for j in range(4):
        kb.dma(lnrow[:, j, :], lnp_d[j, :].partition_broadcast(128), w=['lnrow'])
    ci = 0
    for j in range(6):
        for hf in range(2):
            c0 = j * D + hf * 512
            wbuf = wmb[ci % 2]
            wk = 'wmb%d' % (ci % 2)
            kb.dma(wbuf[:, :, :], wmod_d[:, c0:c0 + 512].rearrange("(k p) n -> p k n", p=128), w=[wk], eng='pool')
            kb.dma(bmrow[:, :], bmod_d[c0:c0 + 512].partition_broadcast(128), w=['bmrow'])
            for v in range(2):
                pk = 'pmod%d' % v
                for k in range(8):
                    kb.op('pe', lambda E: E.matmul(pmod[v][:, :], lhsT=csrep[:, v, k, :], rhs=wbuf[:, k, :],
                                                   start=(k == 0), stop=(k == 7)), r=['csrep%d' % v, wk], w=[pk])
                if j in (1, 4):
                    kb.op('dve', lambda E: E.scalar_tensor_tensor(out=rows[:, v, j, hf * 512:(hf + 1) * 512], in0=pmod[v][:, :],
                                                                  scalar=1.0, in1=bmrow[:, :], op0=ALU.add, op1=ALU.add),
                          r=[pk, 'bmrow'], w=['rows'])
                else:
                    kb.op('dve', lambda E: E.tensor_tensor(out=rows[:, v, j, hf * 512:(hf + 1) * 512], in0=pmod[v][:, :],
                                                           in1=bmrow[:, :], op=ALU.add), r=[pk, 'bmrow'], w=['rows'])
            ci += 1
    kb.barrier()
    es0.close()

    es1 = ExitStack()
    kb1 = kb
    kb.es = es1
    wb = kb.sb([128, 16, D], BF16, "wb")
    wo = kb.sb([128, 8, D], BF16, "wo")
    wg = [kb.sb([128, 8, 512], BF16, "wg%d" % i) for i in range(2)]
    xb = kb.sb([128, 4, D], F32, "xb")
    hn = kb.sb([128, D], F32, "hn")
    t1 = kb.sb([128, D], F32, "t1")
    hb = kb.sb([128, D], BF16, "hb")
    hT = kb.sb([128, 8, 512], BF16, "hT")
    brTb = kb.sb([128, 16, 512], BF16, "brTb")
    gs = kb.sb([128, 4, 512], BF16, "gs")
    tb = kb.sb([128, 4, 512], F32, "tb")
    mT = kb.sb([128, 8, 512], BF16, "mT")
    st = kb.sb([128, 2, 6], F32, "st")
    mv = kb.sb([128, 2], F32, "mv")
    rstd = kb.sb([128, 1], F32, "rstd")
    pT = kb.ps([128, 8, 128], BF16, "pT")
    pG = [kb.ps([128, 512], F32, "pG%d" % i) for i in range(2)]
    pP = [kb.ps([128, 512], F32, "pP%d" % i) for i in range(2)]
    pM = kb.ps([128, D], F32, "pM")

    kb.dma(wb[:, :, :], wb_d.rearrange("(k p) n -> p k n", p=128), w=['wb'], eng='pool')
    kb.dma(wo[:, :, :], wo_d.rearrange("(k p) n -> p k n", p=128), w=['wo'], eng='pool')

    blocks = [(0, 128, 0)] + [(128 + 512 * i, 512, 1) for i in range(4)]
    wgi = 0
    for (t0, NT, v) in blocks:
        nt = NT // 128
        for kc in range(16):
            for ti in range(nt):
                tau = t0 // 128 + ti
                kb.op('pool', lambda E: E.indirect_dma_start(out=brTb[:, kc, ti * 128:(ti + 1) * 128], out_offset=None, in_=G_d[:, :],
                                                             in_offset=bass.IndirectOffsetOnAxis(ap=idxb[:, tau, kc:kc + 1], axis=0)),
                      r=['idxb', 'G_d'], w=['brTb'], dma=True)
        for ti in range(nt):
            xk = 'xb%d' % ti
            kb.dma(xb[:, ti, :], x_d[t0 + ti * 128:t0 + (ti + 1) * 128, :], r=['x_in'], w=[xk])
            ln_normalize(kb, xb[:, ti, :], xk, hn[:, :], 'hn', st, mv, rstd)
            kb.op('pool', lambda E: E.tensor_tensor(out=t1[:, :], in0=hn[:, :], in1=rows[:, v, 1, :], op=ALU.mult), r=['hn'], w=['t1'])
            kb.op('pool', lambda E: E.tensor_tensor(out=hb[:, :], in0=t1[:, :], in1=rows[:, v, 0, :], op=ALU.add), r=['t1'], w=['hb'])
            for k in range(8):
                kb.op('pe', lambda E: E.transpose(out=pT[:, k, :], in_=hb[:, k * 128:(k + 1) * 128], identity=ident_b[:, :]), r=['hb'], w=['pT'])
            kb.op('act', lambda E: E.copy(out=hT[:, :, ti * 128:(ti + 1) * 128], in_=pT[:, :, :]), r=['pT'], w=['hT'])
        for oc in range(8):
            wgb = wg[wgi % 2]
            wgk = 'wg%d' % (wgi % 2)
            wgi += 1
            kb.dma(wgb[:, :, :], wg_d[oc].rearrange("(k p) n -> p k n", p=128), w=[wgk], eng='pool')
            for br in range(4):
                pg, pgk = pG[br % 2], 'pG%d' % (br % 2)
                pp, ppk = pP[br % 2], 'pP%d' % (br % 2)
                for k in range(8):
                    kb.op('pe', lambda E: E.matmul(pg[:, 0:NT], lhsT=wgb[:, k, br * 128:(br + 1) * 128], rhs=hT[:, k, 0:NT],
                                                   start=(k == 0), stop=(k == 7)), r=[wgk, 'hT'], w=[pgk])
                kb.op('act', lambda E: E.activation(out=gs[:, br, 0:NT], in_=pg[:, 0:NT], func=AF.Sigmoid), r=[pgk], w=['gs%d' % br])
                for k in range(4):
                    kb.op('pe', lambda E: E.matmul(pp[:, 0:NT], lhsT=wb[:, br * 4 + k, oc * 128:(oc + 1) * 128], rhs=brTb[:, br * 4 + k, 0:NT],
                                                   start=(k == 0), stop=(k == 3)), r=['wb', 'brTb'], w=[ppk])
                kb.op('dve', lambda E: E.tensor_tensor(out=tb[:, br, 0:NT], in0=pp[:, 0:NT], in1=gs[:, br, 0:NT], op=ALU.mult),
                      r=[ppk, 'gs%d' % br], w=['tb%d' % br])
            kb.op('pool', lambda E: E.tensor_tensor(out=tb[:, 0, 0:NT], in0=tb[:, 0, 0:NT], in1=tb[:, 1, 0:NT], op=ALU.add), r=['tb0', 'tb1'], w=['tb0'])
            kb.op('pool', lambda E: E.tensor_tensor(out=tb[:, 2, 0:NT], in0=tb[:, 2, 0:NT], in1=tb[:, 3, 0:NT], op=ALU.add), r=['tb2', 'tb3'], w=['tb2'])
            kb.op('pool', lambda E: E.tensor_tensor(out=mT[:, oc, 0:NT], in0=tb[:, 0, 0:NT], in1=tb[:, 2, 0:NT], op=ALU.add), r=['tb0', 'tb2'], w=['mT'])
        for ti in range(nt):
            xk = 'xb%d' % ti
            for hf in range(2):
                for k in range(8):
                    kb.op('pe', lambda E: E.matmul(pM[:, hf * 512:(hf + 1) * 512], lhsT=mT[:, k, ti * 128:(ti + 1) * 128],
                                                   rhs=wo[:, k, hf * 512:(hf + 1) * 512], start=(k == 0), stop=(k == 7)), r=['mT', 'wo'], w=['pM'])
            kb.op('dve', lambda E: E.tensor_tensor(out=t1[:, :], in0=pM[:, :], in1=rows[:, v, 2, :], op=ALU.mult), r=['pM'], w=['t1'])
            kb.op('dve', lambda E: E.scalar_tensor_tensor(out=t1[:, :], in0=xb[:, ti, :], scalar=ALPHA, in1=t1[:, :], op0=ALU.mult, op1=ALU.add),
                  r=[xk, 't1'], w=['t1'])
            ln_normalize(kb, t1[:, :], 't1', hn[:, :], 'hn', st, mv, rstd)
            kb.op('pool', lambda E: E.tensor_tensor(out=hn[:, :], in0=hn[:, :], in1=lnrow[:, 0, :], op=ALU.mult), r=['hn'], w=['hn'])
            kb.op('pool', lambda E: E.tensor_tensor(out=xb[:, ti, :], in0=hn[:, :], in1=lnrow[:, 1, :], op=ALU.add), r=['hn'], w=[xk])
            kb.dma(x1_d[t0 + ti * 128:t0 + (ti + 1) * 128, :], xb[:, ti, :], r=[xk], w=['x1d'])
    kb.barrier()
    es1.close()

    es2 = ExitStack()
    kb.es = es2
    wq = kb.sb([128, 8, 2048], BF16, "wq")
    keysT = kb.sb([128, 16, 128], BF16, "keysTb")
    xt = kb.sb([128, D], F32, "xt")
    hn = kb.sb([128, D], F32, "hn2")
    h2b = kb.sb([128, D], BF16, "h2b")
    h2T = kb.sb([128, 8, 128], BF16, "h2T")
    qT = kb.sb([128, 16, 128], BF16, "qT")
    sc = kb.sb([128, 16, 128], F32, "sc")
    sc2 = sc
    vv = kb.sb([128, 16, 16], F32, "vv")
    ix = kb.sb([128, 16, 16], U32, "ix")
    ixf = kb.sb([128, 16, 16], F32, "ixf")
    cand = kb.sb([128, 8, 256], F32, "cand")
    cand2 = cand
    best = kb.sb([128, 8, 16], F32, "best")
    pos = kb.sb([128, 8, 16], U32, "pos")
    pa = kb.sb([128, 8, 16], U32, "pa")
    pb_ = kb.sb([128, 8, 16], U32, "pb")
    paf = kb.sb([128, 2, 8, 16], F32, "paf")
    eq = kb.sb([128, 8, 16, 16], F32, "eq")
    sel = kb.sb([128, 2, 8, 16], F32, "sel")
    eidf = kb.sb([128, 128], F32, "eidf")
    eid = kb.sb([128, 128], U32, "eid")
    gate = kb.sb([128, 8, 16], F32, "gate")
    gsum = kb.sb([128, 8], F32, "gsum")
    act = kb.sb([128, 128], F32, "actv")
    wgt = kb.sb([128, 128], F32, "wgt")
    junk = kb.sb([128, D], BF16, "junk")
    Wd = [kb.sb([128, 16, 128], BF16, "Wd%d" % i) for i in range(2)]
    wgtb = kb.sb([128, 128], BF16, "wgtb")
    t1 = kb.sb([128, D], F32, "t1b")
    st = kb.sb([128, 2, 6], F32, "st2")
    mv = kb.sb([128, 2], F32, "mv2")
    rstd = kb.sb([128, 1], F32, "rstd2")
    NG = 16
    ug = [kb.sb([128, D], BF16, "ug%d" % i) for i in range(NG)]
    pT = kb.ps([128, 8, 128], BF16, "pT2")
    pQ = [kb.ps([128, 512], F32, "pQ0")]
    pS = kb.ps([128, 16, 128], F32, "pS")
    pV = kb.ps([128, D], F32, "pV")

    kb.dma(wq[:, :, :], wq_d.rearrange("(k p) n -> p k n", p=128), w=['wq'], eng='pool')
    kb.dma(keysT[:, :, :], keysT_d[:, :, :], w=['keysT'], eng='pool')
    xt2 = [xt, kb.sb([128, D], F32, 'xt_b')]
    h2b2 = [h2b, kb.sb([128, D], BF16, 'h2b_b')]
    eid2 = [eid, kb.sb([128, 128], U32, 'eid_b')]
    gate2 = [gate, kb.sb([128, 8, 16], F32, 'gate_b')]
    gstate = {'gi': 0}

    def topk(ti):
        v = 0 if ti == 0 else 1
        r0 = ti * 128
        q = ti % 2
        xtq, h2bq, eidq, gateq = xt2[q], h2b2[q], eid2[q], gate2[q]
        xk, hk, ek, gtk = 'xt%d' % q, 'h2b%d' % q, 'eid%d' % q, 'gate%d' % q
        kb.dma(xtq[:, :], x1_d[r0:r0 + 128, :], r=['x1d'], w=[xk])
        ln_normalize(kb, xtq[:, :], xk, hn[:, :], 'hn', st, mv, rstd)
        kb.op('pool', lambda E: E.tensor_tensor(out=t1[:, :], in0=hn[:, :], in1=rows[:, v, 4, :], op=ALU.mult), r=['hn'], w=['t1'])
        kb.op('pool', lambda E: E.tensor_tensor(out=h2bq[:, :], in0=t1[:, :], in1=rows[:, v, 3, :], op=ALU.add), r=['t1'], w=[hk])
        for k in range(8):
            kb.op('pe', lambda E: E.transpose(out=pT[:, k, :], in_=h2bq[:, k * 128:(k + 1) * 128], identity=ident_b[:, :]), r=[hk], w=['pT'])
        kb.op('act', lambda E: E.copy(out=h2T[:, :, :], in_=pT[:, :, :]), r=['pT'], w=['h2T'])
        yield
        for g4 in range(4):
            pq, pqk = pQ[0], 'pQ0'
            for j in range(4):
                hs = g4 * 4 + j
                for k in range(8):
                    kb.op('pe', lambda E: E.matmul(pq[:, j * 128:(j + 1) * 128], lhsT=wq[:, k, hs * 128:(hs + 1) * 128], rhs=h2T[:, k, :],
                                                   start=(k == 0), stop=(k == 7)), r=['wq', 'h2T'], w=[pqk])
            kb.op('act', lambda E: E.copy(out=qT[:, g4 * 4:(g4 + 1) * 4, :], in_=pq[:, :].rearrange("p (j t) -> p j t", j=4)), r=[pqk], w=['qT%d' % g4])
        yield
        for hs in range(16):
            kb.op('pe', lambda E: E.matmul(pS[:, hs, :], lhsT=qT[:, hs, :], rhs=keysT[:, hs, :], start=True, stop=True),
                  r=['qT%d' % (hs // 4), 'keysT'], w=['pS'])
        kb.op('act', lambda E: E.copy(out=sc[:, :, :], in_=pS[:, :, :]), r=['pS'], w=['sc%d' % i for i in range(16)])
        yield
        for hs in range(16):
            sk = 'sc%d' % hs
            kb.op('dve', lambda E: E.max(out=vv[:, hs, 0:8], in_=sc[:, hs, :]), r=[sk], w=['vva%d' % hs])
            kb.op('dve', lambda E: E.max_index(out=ix[:, hs, 0:8], in_max=vv[:, hs, 0:8], in_values=sc[:, hs, :]), r=[sk, 'vva%d' % hs], w=['ixa%d' % hs])
            kb.op('dve', lambda E: E.match_replace(out=sc2[:, hs, :], in_to_replace=vv[:, hs, 0:8], in_values=sc[:, hs, :], imm_value=NEG),
                  r=[sk, 'vva%d' % hs], w=[sk])
            kb.op('dve', lambda E: E.max(out=vv[:, hs, 8:16], in_=sc2[:, hs, :]), r=[sk], w=['vvb%d' % hs])
            kb.op('dve', lambda E: E.max_index(out=ix[:, hs, 8:16], in_max=vv[:, hs, 8:16], in_values=sc2[:, hs, :]), r=[sk, 'vvb%d' % hs], w=['ixb%d' % hs])
        yield
        allv = ['vva%d' % i for i in range(16)] + ['vvb%d' % i for i in range(16)]
        alli = ['ixa%d' % i for i in range(16)] + ['ixb%d' % i for i in range(16)]
        kb.op('dve', lambda E: E.tensor_copy(out=ixf[:, :, :], in_=ix[:, :, :]), r=alli, w=['ixf'])
        v4 = vv[:, :, :].rearrange("p (h s) a -> p h s a", s=2)
        yield
        for h in range(8):
            kb.op('dve', lambda E: E.tensor_tensor(out=cand[:, h, :].rearrange("p (a b) -> p a b", a=16),
                                                   in0=vv[:, 2 * h, :].unsqueeze(2).to_broadcast([128, 16, 16]),
                                                   in1=vv[:, 2 * h + 1, :].unsqueeze(1).to_broadcast([128, 16, 16]), op=ALU.add), r=allv, w=['cand%d' % h])
            ck = 'cand%d' % h
            kb.op('dve', lambda E: E.max(out=best[:, h, 0:8], in_=cand[:, h, :]), r=[ck], w=['besta%d' % h])
            kb.op('dve', lambda E: E.max_index(out=pos[:, h, 0:8], in_max=best[:, h, 0:8], in_values=cand[:, h, :]), r=[ck, 'besta%d' % h], w=['posa%d' % h])
            kb.op('dve', lambda E: E.match_replace(out=cand2[:, h, :], in_to_replace=best[:, h, 0:8], in_values=cand[:, h, :], imm_value=NEG),
                  r=[ck, 'besta%d' % h], w=[ck])
            kb.op('dve', lambda E: E.max(out=best[:, h, 8:16], in_=cand2[:, h, :]), r=[ck], w=['bestb%d' % h])
            kb.op('dve', lambda E: E.max_index(out=pos[:, h, 8:16], in_max=best[:, h, 8:16], in_values=cand2[:, h, :]), r=[ck, 'bestb%d' % h], w=['posb%d' % h])
        yield
        allb = ['besta%d' % i for i in range(8)] + ['bestb%d' % i for i in range(8)]
        allp = ['posa%d' % i for i in range(8)] + ['posb%d' % i for i in range(8)]
        kb.op('dve', lambda E: E.tensor_scalar(out=pa[:, :, :], in0=pos[:, :, :], scalar1=4, scalar2=None, op0=ALU.logical_shift_right), r=allp, w=['pa'])
        kb.op('dve', lambda E: E.tensor_scalar(out=pb_[:, :, :], in0=pos[:, :, :], scalar1=15, scalar2=None, op0=ALU.bitwise_and), r=allp, w=['pb'])
        kb.op('dve', lambda E: E.tensor_copy(out=paf[:, 0, :, :], in_=pa[:, :, :]), r=['pa'], w=['paf0'])
        kb.op('dve', lambda E: E.tensor_copy(out=paf[:, 1, :, :], in_=pb_[:, :, :]), r=['pb'], w=['paf1'])
        yield
        for s_ in range(2):
            for h in range(8):
                kb.op('dve', lambda E: E.tensor_tensor(out=eq[:, h, :, :], in0=iota16[:, :].unsqueeze(1).to_broadcast([128, 16, 16]),
                                                       in1=paf[:, s_, h, :].unsqueeze(2).to_broadcast([128, 16, 16]), op=ALU.is_equal),
                      r=['paf%d' % s_, 'iota16'], w=['eq%d' % h])
                kb.op('dve', lambda E: E.tensor_tensor(out=eq[:, h, :, :], in0=eq[:, h, :, :],
                                                       in1=ixf[:, 2 * h + s_, :].unsqueeze(1).to_broadcast([128, 16, 16]), op=ALU.mult),
                      r=['eq%d' % h, 'ixf'], w=['eq%d' % h])
            kb.op('dve', lambda E: E.tensor_reduce(out=sel[:, s_, :, :], in_=eq[:, :, :, :], axis=AX.X, op=ALU.add), r=['eq%d' % h for h in range(8)], w=['sel%d' % s_])
        kb.op('dve', lambda E: E.scalar_tensor_tensor(out=eidf[:, :], in0=sel[:, 0, :, :].rearrange("p h k -> p (h k)"), scalar=128.0,
                                                      in1=sel[:, 1, :, :].rearrange("p h k -> p (h k)"), op0=ALU.mult, op1=ALU.add), r=['sel0', 'sel1'], w=['eidf'])
        kb.op('dve', lambda E: E.tensor_copy(out=eidq[:, :], in_=eidf[:, :]), r=['eidf'], w=[ek])
        yield
        kb.op('dve', lambda E: E.tensor_tensor(out=gateq[:, :, :], in0=best[:, :, :], in1=best[:, :, 0:1].to_broadcast([128, 8, 16]), op=ALU.subtract), r=allb, w=[gtk])
        kb.op('act', lambda E: E.activation(out=gateq[:, :, :], in_=gateq[:, :, :], func=AF.Exp), r=[gtk], w=[gtk])
        kb.op('dve', lambda E: E.tensor_reduce(out=gsum[:, :], in_=gateq[:, :, :], axis=AX.X, op=ALU.add), r=[gtk], w=['gsum'])
        kb.op('dve', lambda E: E.reciprocal(out=gsum[:, :], in_=gsum[:, :]), r=['gsum'], w=['gsum'])
        kb.op('dve', lambda E: E.tensor_tensor(out=gateq[:, :, :], in0=gateq[:, :, :], in1=gsum[:, :].unsqueeze(2).to_broadcast([128, 8, 16]), op=ALU.mult), r=[gtk, 'gsum'], w=[gtk])
        yield

    def gphase(ti, nxt):
        v = 0 if ti == 0 else 1
        r0 = ti * 128
        q = ti % 2
        xtq, h2bq, eidq, gateq = xt2[q], h2b2[q], eid2[q], gate2[q]
        xk, hk, ek, gtk = 'xt%d' % q, 'h2b%d' % q, 'eid%d' % q, 'gate%d' % q
        gi = gstate['gi']
        for r_ in range(128):
            gb, gk = ug[gi % NG], 'ug%d' % (gi % NG)
            if r_ % 8 == 0:
                kb.prewait('pool', ['ug%d' % ((gi + j_) % NG) for j_ in range(8)])
            gi += 1
            if r_ % 16 == 15 and nxt is not None:
                next(nxt, None)
            kb.op('pool', lambda E: E.indirect_dma_start(out=gb[:, :], out_offset=None, in_=pub_d[:, :],
                                                         in_offset=bass.IndirectOffsetOnAxis(ap=eidq[:, r_:r_ + 1], axis=0)),
                  r=[ek], w=[gk], dma=True)
            kb.op('dve', lambda E: E.scalar_tensor_tensor(out=junk[:, :], in0=gb[:, :], scalar=1.0, in1=h2bq[:, :], op0=ALU.mult, op1=ALU.mult,
                                                          accum_out=act[:, r_:r_ + 1]), r=[gk, hk], w=['junk', 'act'])
        kb.op('act', lambda E: E.activation(out=act[:, :], in_=act[:, :], func=AF.Gelu), r=['act'], w=['act'])
        kb.op('dve', lambda E: E.tensor_tensor(out=wgtb[:, :], in0=act[:, :], in1=gateq[:, :, :].rearrange("p h k -> p (h k)"), op=ALU.mult), r=['act', gtk], w=['wgtb'])
        for r_ in range(128):
            gb, gk = ug[gi % NG], 'ug%d' % (gi % NG)
            if r_ % 8 == 0:
                kb.prewait('pool', ['ug%d' % ((gi + j_) % NG) for j_ in range(8)])
            gi += 1
            if r_ % 16 == 15 and nxt is not None:
                next(nxt, None)
            if r_ % 16 == 0:
                wdb, wdk = Wd[(r_ // 16) % 2], 'Wd%d' % ((r_ // 16) % 2)
                kb.op('dve', lambda E: E.tensor_tensor(out=wdb[:, :, :], in0=ident_b[:, :].unsqueeze(1).to_broadcast([128, 16, 128]),
                                                       in1=wgtb[:, r_:r_ + 16].unsqueeze(2).to_broadcast([128, 16, 128]), op=ALU.mult),
                      r=['ident_b', 'wgtb'], w=[wdk])
            kb.op('pool', lambda E: E.indirect_dma_start(out=gb[:, :], out_offset=None, in_=pvb_d[:, :],
                                                         in_offset=bass.IndirectOffsetOnAxis(ap=eidq[:, r_:r_ + 1], axis=0)),
                  r=[ek], w=[gk], dma=True)
            for hf in range(2):
                kb.op('pe', lambda E: E.matmul(pV[:, hf * 512:(hf + 1) * 512], lhsT=wdb[:, r_ % 16, :], rhs=gb[:, hf * 512:(hf + 1) * 512],
                                               start=(r_ == 0), stop=(r_ == 127)), r=[wdk, gk], w=['pV'])
        kb.op('dve', lambda E: E.tensor_tensor(out=t1[:, :], in0=pV[:, :], in1=rows[:, v, 5, :], op=ALU.mult), r=['pV'], w=['t1'])
        kb.op('dve', lambda E: E.scalar_tensor_tensor(out=t1[:, :], in0=xtq[:, :], scalar=ALPHA, in1=t1[:, :], op0=ALU.mult, op1=ALU.add), r=[xk, 't1'], w=['t1'])
        ln_normalize(kb, t1[:, :], 't1', hn[:, :], 'hn', st, mv, rstd)
        kb.op('pool', lambda E: E.tensor_tensor(out=hn[:, :], in0=hn[:, :], in1=lnrow[:, 2, :], op=ALU.mult), r=['hn'], w=['hn'])
        kb.op('pool', lambda E: E.tensor_tensor(out=t1[:, :], in0=hn[:, :], in1=lnrow[:, 3, :], op=ALU.add), r=['hn'], w=['t1'])
        kb.dma(out_d[r0:r0 + 128, :], t1[:, :], r=['t1'], w=['outd'])
        gstate['gi'] = gi

    NTI = NTOK // 128
    for _ in topk(0):
        pass
    for ti in range(NTI):
        nxt = topk(ti + 1) if ti + 1 < NTI else None
        gphase(ti, nxt)
        if nxt is not None:
            for _ in nxt:
                pass
    kb.barrier()
    es2.close()
    kb.barrier()
    es.close()


GATE0 = 8736 - 4096
_CONST = {}


def consts():
    if not _CONST:
        _CONST['ident'] = np.eye(128, dtype=np.float32)
        _CONST['iota16'] = np.tile(np.arange(16, dtype=np.float32)[None, :], (128, 1))
    return _CONST


def layer_weights_B(inp, l):
    w_in = inp['w_in'][l]
    wg = np.ascontiguousarray(w_in[:, GATE0:].reshape(D, 4, 8, 128).transpose(2, 0, 1, 3).reshape(8, D, 512))
    keysT = np.ascontiguousarray(inp['peer_keys'][l].reshape(16, 128, 128).transpose(2, 0, 1))
    c = consts()
    return {
        'w_mod': inp['w_mod'][l], 'b_mod': inp['b_mod'][l], 'wg': wg,
        'wb': np.ascontiguousarray(inp['w_branch'][l].reshape(2048, D)), 'wo': inp['w_out'][l],
        'lnp': np.stack([inp['ln1_g'][l], inp['ln1_b'][l], inp['ln2_g'][l], inp['ln2_b'][l]], 0),
        'wq': inp['peer_wq'][l], 'keysT': keysT, 'peer_u': inp['peer_u'][l], 'peer_v': inp['peer_v'][l],
        'ident': c['ident'], 'iota16': c['iota16'],
    }


def cvec_layout(c_ctx, c_b):
    return np.ascontiguousarray(np.stack([c_ctx, c_b], 0).reshape(2, 8, 128).transpose(0, 2, 1))


TA = 4352
NTA = 34
POOL_WINDOWS = (2, 4, 8, 16)


def emit_A(nc, kb, l, C, xa_d, O_d):
    es = ExitStack()
    kb.es = es
    dt = lambda n, s, d=F32, kind="ExternalInput": nc.dram_tensor(n + "_%d" % l, list(s), d, kind=kind).ap()
    cvec_d = C['cvec']
    wmod_d = C['w_mod_%d' % l]
    bmod_d = C['b_mod_%d' % l]
    wfm_d = dt("wa_fm", [D, 1792])
    wtm_d = dt("wa_tm", [D, 784])
    cw_d = dt("convw", [128, 10, 6])
    poolw_d = dt("pool_w", [4, 128, 128])
    pscale_d = dt("pool_scale", [128, 4])
    invc_d, invcc_d = C['invcnt'], C['invcnt_c']
    rowp_d = dt("rowp", [5, 256])
    dftx_d, dftc_d, dfch_d, tri_d, ident_d, tm_d, fm_d = C['dft_x'], C['dft_c'], C['dft_ch'], C['tri'], C['ident'], C['tm'], C['fm']
    Ov = O_d.rearrange("(t r) c -> r t c", r=1280)
    xrow = (lambda q: q * 128) if l == 0 else (lambda q: _gathered_row(X_CH, *((0, 0) if q == 0 else ((1, 0) if q == 1 else ((0, 128 + (q - 2) * 128) if q < 18 else (1, 128 + (q - 18) * 128))))))

    ident_f = kb.sb([128, 128], F32, "ident_f")
    ident_b = kb.sb([128, 128], BF16, "ident_b")
    ones_f = kb.sb([128, 128], F32, "ones_f")
    kb.dma(ident_f[:, :], ident_d[:, :], w=['ident_f'])
    kb.op('dve', lambda E: E.tensor_copy(out=ident_b[:, :], in_=ident_f[:, :]), r=['ident_f'], w=['ident_b'])
    kb.op('pool', lambda E: E.memset(ones_f[:, :], 1.0), w=['ones'])

    es0 = ExitStack()
    kb.es = es0
    rows = kb.sb([128, 2, 2, D], F32, "modrows")
    cs = kb.sb([128, 2, 8], F32, "cs")
    csrep = kb.sb([128, 2, 8, 128], BF16, "csrep")
    bmrow = kb.sb([128, 512], F32, "bmrow")
    wmb = [kb.sb([128, 8, 512], BF16, "wmb%d" % i) for i in range(2)]
    pmod = [kb.ps([128, 512], F32, "pmod%d" % i) for i in range(2)]
    for v in range(2):
        kb.dma(cs[:, v, :], cvec_d[v, :, :], w=['cs%d' % v])
        kb.op('act', lambda E: E.activation(out=cs[:, v, :], in_=cs[:, v, :], func=AF.Silu), r=['cs%d' % v], w=['cs%d' % v])
        for k in range(8):
            kb.op('dve', lambda E: E.tensor_scalar(out=csrep[:, v, k, :], in0=ones_f[:, :], scalar1=cs[:, v, k:k + 1],
                                                   scalar2=None, op0=ALU.mult), r=['cs%d' % v, 'ones'], w=['csrep%d' % v])
    ci = 0
    for j in range(2):
        for hf in range(2):
            c0 = j * D + hf * 512
            wbuf, wk = wmb[ci % 2], 'wmb%d' % (ci % 2)
            kb.dma(wbuf[:, :, :], wmod_d[:, c0:c0 + 512].rearrange("(k p) n -> p k n", p=128), w=[wk], eng='pool')
            kb.dma(bmrow[:, :], bmod_d[c0:c0 + 512].partition_broadcast(128), w=['bmrow'])
            for v in range(2):
                pk = 'pmod%d' % v
                for k in range(8):
                    kb.op('pe', lambda E: E.matmul(pmod[v][:, :], lhsT=csrep[:, v, k, :], rhs=wbuf[:, k, :],
                                                   start=(k == 0), stop=(k == 7)), r=['csrep%d' % v, wk], w=[pk])
                kb.op('dve', lambda E: E.scalar_tensor_tensor(out=rows[:, v, j, hf * 512:(hf + 1) * 512], in0=pmod[v][:, :],
                                                              scalar=float(j), in1=bmrow[:, :], op0=ALU.add, op1=ALU.add),
                      r=[pk, 'bmrow'], w=['rows'])
            ci += 1
    wfm = kb.sb([128, 8, 1792], BF16, "wfm")
    wtm = kb.sb([128, 8, 784], BF16, "wtm")
    kb.dma(wfm[:, :, :], wfm_d.rearrange("(k p) n -> p k n", p=128), w=['wfm'], eng='pool')
    kb.dma(wtm[:, :, :], wtm_d.rearrange("(k p) n -> p k n", p=128), w=['wtm'], eng='pool')
    xt = [kb.sb([128, D], F32, "xt%d" % i) for i in range(2)]
    hn = [kb.sb([128, D], F32, "hn%d" % i) for i in range(2)]
    t1 = [kb.sb([128, D], F32, "t1%d" % i) for i in range(2)]
    hb = [kb.sb([128, D], BF16, "hb%d" % i) for i in range(2)]
    hT = [kb.sb([128, 8, 512], BF16, "hT%d" % i) for i in range(2)]
    tmo = [kb.sb([128, 784], F32, "tmo%d" % i) for i in range(2)]
    fmo = [kb.sb([128, 512], F32, "fmo%d" % i) for i in range(2)]
    st = [kb.sb([128, 2, 6], F32, "st%d" % i) for i in range(2)]
    mv = [kb.sb([128, 2], F32, "mv%d" % i) for i in range(2)]
    rstd = [kb.sb([128, 1], F32, "rstd%d" % i) for i in range(2)]
    pT = kb.ps([128, 8, 128], BF16, "pT")
    pTM = kb.ps([128, 1024], F32, "pTM")
    pFM = [kb.ps([128, 512], F32, "pFM%d" % i) for i in range(2)]
    blocks = [(0, 256, 0)] + [(256 + 512 * i, 512, 1) for i in range(8)]
    for bi, (t0, NT, v) in enumerate(blocks):
        nt = NT // 128
        bq = bi % 2
        hTb = hT[bq]
        hks = ['hT%d_%d' % (bq, ti) for ti in range(nt)]
        for ti in range(nt):
            r0 = t0 + ti * 128
            q = (r0 // 128) % 2
            sx = '_%d' % q
            kb.dma(xt[q][:, :], xa_d[xrow(r0 // 128):xrow(r0 // 128) + 128, :], r=['xa_d'], w=['xt' + sx])
            ln_normalize(kb, xt[q][:, :], 'xt' + sx, hn[q][:, :], 'hn' + sx, st[q], mv[q], rstd[q], sx)
            kb.op('pool', lambda E: E.tensor_tensor(out=t1[q][:, :], in0=hn[q][:, :], in1=rows[:, v, 1, :], op=ALU.mult), r=['hn' + sx], w=['t1' + sx])
            kb.op('pool', lambda E: E.tensor_tensor(out=hb[q][:, :], in0=t1[q][:, :], in1=rows[:, v, 0, :], op=ALU.add), r=['t1' + sx], w=['hb' + sx])
            for k in range(8):
                kb.op('pe', lambda E: E.transpose(out=pT[:, k, :], in_=hb[q][:, k * 128:(k + 1) * 128], identity=ident_b[:, :]), r=['hb' + sx], w=['pT'])
            kb.op('act', lambda E: E.copy(out=hTb[:, :, ti * 128:(ti + 1) * 128], in_=pT[:, :, :]), r=['pT'], w=[hks[ti]])
            for (c0, c1) in ((0, 512), (512, 784)):
                for k in range(8):
                    kb.op('pe', lambda E: E.matmul(pTM[:, c0:c1], lhsT=hTb[:, k, ti * 128:(ti + 1) * 128], rhs=wtm[:, k, c0:c1],
                                                   start=(k == 0), stop=(k == 7)), r=[hks[ti], 'wtm'], w=['pTM'])
            kb.op('act', lambda E: E.copy(out=tmo[q][:, :], in_=pTM[:, 0:784]), r=['pTM'], w=['tmo' + sx])
            kb.dma(tm_d[r0:r0 + 128, :], tmo[q][:, :], r=['tmo' + sx], w=['tm_d'])
        for c in range(14):
            pf, pfk = pFM[c % 2], 'pFM%d' % (c % 2)
            fo, fok = fmo[c % 2], 'fmo%d' % (c % 2)
            for k in range(8):
                kb.op('pe', lambda E: E.matmul(pf[:, 0:NT], lhsT=wfm[:, k, c * 128:(c + 1) * 128], rhs=hTb[:, k, 0:NT],
                                               start=(k == 0), stop=(k == 7)), r=hks + ['wfm'], w=[pfk])
            kb.op('dve', lambda E: E.tensor_copy(out=fo[:, 0:NT], in_=pf[:, 0:NT]), r=[pfk], w=[fok])
            kb.dma(fm_d[c * 128:(c + 1) * 128, t0:t0 + NT], fo[:, 0:NT], r=[fok], w=['fm_d'])
    kb.barrier()
    es0.close()

    es2 = ExitStack()
    kb.es = es2
    pin = kb.sb([128, TA], F32, "pin")
    bufs = [kb.sb([128, 80, 80], F32, "pbA"), kb.sb([128, 80, 80], F32, "pbB")]
    cbufs = [kb.sb([128, 272], F32, "pcA"), kb.sb([128, 272], F32, "pcB")]
    invc = kb.sb([128, 4096], F32, "invc")
    invcc = kb.sb([128, 256], F32, "invcc")
    ptmp = kb.sb([128, 4096], F32, "ptmp")
    dB = kb.sb([128, TA], BF16, "dB")
    pwf = kb.sb([128, 4, 128], F32, "pwf")
    pw = kb.sb([128, 4, 128], BF16, "pw")
    psc = kb.sb([128, 4], F32, "psc")
    pob = kb.sb([128, 512], BF16, "pob")
    pPo = [kb.ps([128, 512], F32, "pPo%d" % i) for i in range(2)]
    kb.dma(pwf[:, :, :], poolw_d.rearrange("g c n -> c g n"), w=['pwf'])
    kb.op('dve', lambda E: E.tensor_copy(out=pw[:, :, :], in_=pwf[:, :, :]), r=['pwf'], w=['pw'])
    kb.dma(psc[:, :], pscale_d[:, :], w=['psc'])
    for g in range(4):
        w_ = POOL_WINDOWS[g]
        lo = w_ // 2
        kb.dma(pin[:, :], fm_d[g * 128:(g + 1) * 128, :], r=['fm_d'], w=['pin'])
        kb.dma(invc[:, :], invc_d[g, :].partition_broadcast(128), w=['invc'])
        kb.dma(invcc[:, :], invcc_d[g, :].partition_broadcast(128), w=['invcc'])
        for i in range(2):
            kb.op('pool', lambda E: E.memset(bufs[i][:, :, :], 0.0), w=['pb%d' % i])
            kb.op('pool', lambda E: E.memset(cbufs[i][:, :], 0.0), w=['pc%d' % i])
        kb.op('act', lambda E: E.copy(out=bufs[0][:, 8:72, 8:72], in_=pin[:, 256:TA].rearrange("p (r c) -> p r c", r=64)), r=['pin'], w=['pb0'])
        kb.op('act', lambda E: E.copy(out=cbufs[0][:, 8:264], in_=pin[:, 0:256]), r=['pin'], w=['pc0'])
        cur = 0
        step = 1
        while step < w_:
            L = 80 - 2 * step + 1
            kb.op('dve', lambda E: E.tensor_tensor(out=bufs[1 - cur][:, :, 0:L], in0=bufs[cur][:, :, 0:L], in1=bufs[cur][:, :, step:step + L], op=ALU.add),
                  r=['pb%d' % cur], w=['pb%d' % (1 - cur)])
            Lc = 272 - 2 * step + 1
            kb.op('pool', lambda E: E.tensor_tensor(out=cbufs[1 - cur][:, 0:Lc], in0=cbufs[cur][:, 0:Lc], in1=cbufs[cur][:, step:step + Lc], op=ALU.add),
                  r=['pc%d' % cur], w=['pc%d' % (1 - cur)])
            cur = 1 - cur
            step *= 2
        ccur = cur
        step = 1
        while step < w_:
            L = 80 - 2 * step + 1
            kb.op('dve', lambda E: E.tensor_tensor(out=bufs[1 - cur][:, 0:L, :], in0=bufs[cur][:, 0:L, :], in1=bufs[cur][:, step:step + L, :], op=ALU.add),
                  r=['pb%d' % cur], w=['pb%d' % (1 - cur)])
            cur = 1 - cur
            step *= 2
        a0 = 8 - lo
        kb.op('dve', lambda E: E.tensor_tensor(out=ptmp[:, :].rearrange("p (r c) -> p r c", r=64), in0=bufs[cur][:, a0:a0 + 64, a0:a0 + 64],
                                               in1=invc[:, :].rearrange("p (r c) -> p r c", r=64), op=ALU.mult), r=['pb%d' % cur, 'invc'], w=['ptmp'])
        kb.op('dve', lambda E: E.tensor_tensor(out=dB[:, 256:TA], in0=ptmp[:, :], in1=pin[:, 256:TA], op=ALU.subtract), r=['ptmp', 'pin'], w=['dB'])
        kb.op('pool', lambda E: E.tensor_tensor(out=cbufs[1 - ccur][:, 0:256], in0=cbufs[ccur][:, a0:a0 + 256], in1=invcc[:, :], op=ALU.mult),
              r=['pc%d' % ccur, 'invcc'], w=['pc%d' % (1 - ccur)])
        kb.op('pool', lambda E: E.tensor_tensor(out=dB[:, 0:256], in0=cbufs[1 - ccur][:, 0:256], in1=pin[:, 0:256], op=ALU.subtract), r=['pc%d' % (1 - ccur), 'pin'], w=['dB'])
        for bi, (t0, NT, v) in enumerate(blocks):
            pp, ppk = pPo[bi % 2], 'pPo%d' % (bi % 2)
            kb.op('pe', lambda E: E.matmul(pp[:, 0:NT], lhsT=pw[:, g, :], rhs=dB[:, t0:t0 + NT], start=True, stop=True), r=['pw', 'dB'], w=[ppk])
            kb.op('act', lambda E: E.activation(out=pob[:, 0:NT], in_=pp[:, 0:NT], func=AF.Copy, scale=psc[:, g:g + 1]), r=[ppk, 'psc'], w=['pob'])
            kb.dma(Ov[g * 128:(g + 1) * 128, t0 // 128:(t0 + NT) // 128, :], pob[:, 0:NT].rearrange("p (t c) -> p t c", c=128), r=['pob'], w=['brT_d'])
    kb.barrier()
    es2.close()

    es3 = ExitStack()
    kb.es = es3
    xf = kb.sb([128, NTA, 256], BF16, "xf")
    dfb = [kb.sb([128, 32, 512], BF16, "dfb%d" % i) for i in range(2)]
    dfc = kb.sb([128, 2, 2, 256], BF16, "dfc")
    dchf = kb.sb([128, 2, 128], F32, "dchf")
    dch = kb.sb([128, 2, 128], BF16, "dch")
    ysb = kb.sb([128, 2, 2, 512], BF16, "ysb")
    fob = kb.sb([128, 512], BF16, "fob")
    pY = [[kb.ps([128, 512], F32, "pY%d%d" % (g, t)) for t in range(2)] for g in range(2)]
    pFo = [kb.ps([128, 512], F32, "pFo%d" % i) for i in range(2)]
    kb.dma(xf[:, :, :], tm_d[:, 0:256].rearrange("(t p) c -> p t c", p=128), r=['tm_d'], w=['xf'], eng='pool')
    kb.dma(dchf[:, :, :], dfch_d.rearrange("t c n -> c t n"), w=['dchf'])
    kb.op('dve', lambda E: E.tensor_copy(out=dch[:, :, :], in_=dchf[:, :, :]), r=['dchf'], w=['dch'])
    kb.dma(dfc[:, :, :, :], dftc_d.rearrange("t (k p) n -> p t k n", p=128), w=['dfc'])
    dbi = 0
    fo_i = 0
    for n in range(9):
        if n == 0:
            NT, nk, tcol = 256, 2, 0
        else:
            NT, nk, tcol = 512, 32, 256 + (n - 1) * 512
        for trig in range(2):
            if n == 0:
                rhs_of = lambda k: dfc[:, trig, k, :]
                rk = 'dfc'
            else:
                db, rk = dfb[dbi % 2], 'dfb%d' % (dbi % 2)
                dbi += 1
                kb.dma(db[:, :, :], dftx_d[trig, :, (n - 1) * 512:n * 512].rearrange("(k p) n -> p k n", p=128), w=[rk])
                rhs_of = lambda k: db[:, k, :]
            for g in range(2):
                for k in range(nk):
                    tile_i = k if n == 0 else 2 + k
                    kb.op('pe', lambda E: E.matmul(pY[g][trig][:, 0:NT], lhsT=xf[:, tile_i, g * 128:(g + 1) * 128], rhs=rhs_of(k),
                                                   start=(k == 0), stop=(k == nk - 1)), r=['xf', rk], w=['pY%d%d' % (g, trig)])
                kb.op('act', lambda E: E.copy(out=ysb[:, g, trig, 0:NT], in_=pY[g][trig][:, 0:NT]), r=['pY%d%d' % (g, trig)], w=['ysb%d%d' % (g, trig)])
        for g in range(2):
            pf, pfk = pFo[fo_i % 2], 'pFo%d' % (fo_i % 2)
            fo_i += 1
            for trig in range(2):
                kb.op('pe', lambda E: E.matmul(pf[:, 0:NT], lhsT=dch[:, trig, :], rhs=ysb[:, g, trig, 0:NT], start=(trig == 0), stop=(trig == 1)),
                      r=['dch', 'ysb%d%d' % (g, trig)], w=[pfk])
            kb.op('dve', lambda E: E.tensor_copy(out=fob[:, 0:NT], in_=pf[:, 0:NT]), r=[pfk], w=['fob'])
            kb.dma(Ov[512 + g * 128:512 + (g + 1) * 128, tcol // 128:(tcol + NT) // 128, :], fob[:, 0:NT].rearrange("p (t c) -> p t c", c=128), r=['fob'], w=['brT_d'])
    kb.barrier()
    es3.close()
    kb.es = es

    tri = kb.sb([128, 4, 128], F32, "tri")
    kb.dma(tri[:, :, :], tri_d.rearrange("m j i -> j m i"), w=['tri'])
    r16 = kb.sb([128, NTA, 16], F32, "r16")
    sp16 = kb.sb([128, NTA, 16], F32, "sp16")
    brow = kb.sb([128, 32], F32, "brow")
    la = kb.sb([128, NTA, 12], F32, "la")
    acs = kb.sb([128, NTA, 12], F32, "acs")
    eacs = kb.sb([128, NTA, 12], F32, "eacs")
    neacs = kb.sb([128, NTA, 12], F32, "neacs")
    wst = kb.sb([128, NTA, 12], F32, "wst")
    etot = kb.sb([128, NTA, 12], F32, "etot")
    beta = kb.sb([128, NTA, 4], F32, "beta")
    rowp = kb.sb([128, 3, 256], F32, "rowp")
    with nc.allow_non_contiguous_dma(reason="small strided loads"):
        kb.dma(r16[:, :, :], tm_d[:, 768:784].rearrange("(t p) c -> p t c", p=128), r=['tm_d'], w=['r16'])
    kb.dma(brow[:, :], rowp_d[0, 0:32].partition_broadcast(128), w=['brow'])
    for j in range(3):
        kb.dma(rowp[:, j, :], rowp_d[1 + j, :].partition_broadcast(128), w=['rowp'])
    es4 = ExitStack()
    kb.es = es4
    pc1 = kb.ps([128, 512], F32, "pc1")
    pc2 = kb.ps([128, 512], F32, "pc2")
    pc3 = kb.ps([128, 512], F32, "pc3")
    V_ = lambda fn, r, w: kb.op('dve', fn, r=r, w=w)
    A_ = lambda fn, r, w: kb.op('act', fn, r=r, w=w)
    G_ = lambda fn, r, w: kb.op('pool', fn, r=r, w=w)
    P_ = lambda fn, r, w: kb.op('pe', fn, r=r, w=w)
    A_(lambda E: E.activation(out=brow[:, 16:32], in_=brow[:, 16:32], func=AF.Exp), ['brow'], ['brow'])
    V_(lambda E: E.tensor_scalar(out=brow[:, 16:32], in0=brow[:, 16:32], scalar1=-1.0, scalar2=None, op0=ALU.mult), ['brow'], ['brow'])
    V_(lambda E: E.tensor_tensor(out=sp16[:, :, :], in0=r16[:, :, :], in1=brow[:, 0:16].unsqueeze(1).to_broadcast([128, NTA, 16]), op=ALU.add), ['r16', 'brow'], ['sp16'])
    A_(lambda E: E.activation(out=sp16[:, :, :], in_=sp16[:, :, :], func=AF.Exp), ['sp16'], ['sp16'])
    V_(lambda E: E.tensor_scalar(out=sp16[:, :, :], in0=sp16[:, :, :], scalar1=1.0, scalar2=None, op0=ALU.add), ['sp16'], ['sp16'])
    A_(lambda E: E.activation(out=sp16[:, :, :], in_=sp16[:, :, :], func=AF.Ln), ['sp16'], ['sp16'])
    A_(lambda E: E.activation(out=beta[:, :, :], in_=r16[:, :, 8:12], func=AF.Sigmoid), ['r16'], ['beta'])
    V_(lambda E: E.tensor_tensor(out=la[:, :, 0:8], in0=sp16[:, :, 0:8], in1=brow[:, 16:24].unsqueeze(1).to_broadcast([128, NTA, 8]), op=ALU.mult), ['sp16', 'brow'], ['la'])
    V_(lambda E: E.tensor_tensor(out=la[:, :, 8:12], in0=sp16[:, :, 12:16], in1=brow[:, 28:32].unsqueeze(1).to_broadcast([128, NTA, 4]), op=ALU.mult), ['sp16', 'brow', 'la'], ['la'])
    laf = la[:, :, :].rearrange("p t c -> p (t c)")
    P_(lambda E: E.matmul(pc1[:, 0:408], lhsT=tri[:, 0, :], rhs=laf, start=True, stop=True), ['tri', 'la'], ['pc1'])
    P_(lambda E: E.matmul(pc2[:, 0:408], lhsT=tri[:, 1, :], rhs=laf, start=True, stop=True), ['tri', 'la'], ['pc2'])
    P_(lambda E: E.matmul(pc3[:, 0:408], lhsT=ones_f[:, :], rhs=laf, start=True, stop=True), ['ones', 'la'], ['pc3'])
    p1v = pc1[:, 0:408].rearrange("p (t c) -> p t c", c=12)
    p2v = pc2[:, 0:408].rearrange("p (t c) -> p t c", c=12)
    p3v = pc3[:, 0:408].rearrange("p (t c) -> p t c", c=12)
    for (c0, c1, pv, pk) in ((0, 4, p1v, 'pc1'), (4, 8, p2v, 'pc2'), (8, 10, p1v, 'pc1'), (10, 12, p2v, 'pc2')):
        V_(lambda E: E.tensor_copy(out=acs[:, :, c0:c1], in_=pv[:, :, c0:c1]), [pk, 'acs'], ['acs'])
    A_(lambda E: E.activation(out=eacs[:, :, :], in_=acs[:, :, :], func=AF.Exp), ['acs'], ['eacs'])
    V_(lambda E: E.tensor_scalar(out=neacs[:, :, :], in0=eacs[:, :, :], scalar1=-1.0, scalar2=None, op0=ALU.mult), ['eacs'], ['neacs'])
    V_(lambda E: E.tensor_tensor(out=wst[:, :, :], in0=p3v, in1=acs[:, :, :], op=ALU.subtract), ['pc3', 'acs'], ['wst'])
    A_(lambda E: E.activation(out=wst[:, :, :], in_=wst[:, :, :], func=AF.Exp), ['wst'], ['wst'])
    A_(lambda E: E.activation(out=etot[:, :, :], in_=p3v, func=AF.Exp), ['pc3'], ['etot'])

    kb.barrier()
    es4.close()
    kb.es = es
    cwt = kb.sb([128, 10, 6], F32, "cwt")
    kb.dma(cwt[:, :, :], cw_d[:, :, :], w=['cwt'])
    PSH = {}

    def open_conv_psum():
        esc = ExitStack()
        kb.es = esc
        PSH['pTt'] = kb.ps([128, 8, 128], BF16, "pTt")
        PSH['pSS'] = kb.ps([128, 512], F32, "pSS")
        return esc
    segs = ((0, 256), (256, TA))

    def conv_chunk(ci, cin, cout):
        kb.dma(cin[:, :], fm_d[(4 + ci) * 128:(5 + ci) * 128, :], r=['fm_d'], w=['cin'])
        V_(lambda E: E.tensor_scalar(out=cout[:, :], in0=cin[:, :], scalar1=cwt[:, ci, 2:3], scalar2=cwt[:, ci, 5:6], op0=ALU.mult, op1=ALU.add), ['cin', 'cwt'], ['cout'])
        for k in (0, 1, 3, 4):
            dd = k - 2
            for (s0, s1) in segs:
                a, b = s0 + max(0, -dd), s1 - max(0, dd)
                V_(lambda E: E.scalar_tensor_tensor(out=cout[:, a:b], in0=cin[:, a + dd:b + dd], scalar=cwt[:, ci, k:k + 1], in1=cout[:, a:b],
                                                    op0=ALU.mult, op1=ALU.add), ['cin', 'cwt', 'cout'], ['cout'])
        A_(lambda E: E.activation(out=cout[:, :], in_=cout[:, :], func=AF.Silu), ['cout'], ['cout'])

    def to_tok(src_bf, skey, dst, dkey, c0):
        for t0 in range(0, NTA, 8):
            n = min(8, NTA - t0)
            for j in range(n):
                P_(lambda E: E.transpose(out=PSH['pTt'][:, j, :], in_=src_bf[:, (t0 + j) * 128:(t0 + j + 1) * 128], identity=ident_b[:, :]), [skey], ['pTt'])
            A_(lambda E: E.copy(out=dst[:, t0:t0 + n, c0:c0 + 128], in_=PSH['pTt'][:, 0:n, :]), ['pTt'], [dkey])

    def l2norm_to(cout, cin, dst, dkey, scale):
        A_(lambda E: E.activation(out=cin[:, :], in_=cout[:, :], func=AF.Square), ['cout'], ['cin'])
        for (t0, NT, v) in blocks:
            P_(lambda E: E.matmul(PSH['pSS'][:, 0:NT], lhsT=ones_f[:, :], rhs=cin[:, t0:t0 + NT], start=True, stop=True), ['ones', 'cin'], ['pSS'])
            V_(lambda E: E.tensor_scalar(out=cin[:, t0:t0 + NT], in0=PSH['pSS'][:, 0:NT], scalar1=1e-6, scalar2=None, op0=ALU.add), ['pSS', 'cin'], ['cin'])
        A_(lambda E: E.activation(out=cin[:, :], in_=cin[:, :], func=AF.Sqrt), ['cin'], ['cin'])
        V_(lambda E: E.reciprocal(out=cin[:, :], in_=cin[:, :]), ['cin'], ['cin'])
        V_(lambda E: E.scalar_tensor_tensor(out=dst, in0=cout[:, :], scalar=scale, in1=cin[:, :], op0=ALU.mult, op1=ALU.mult), ['cout', 'cin'], [dkey])

    order_f = list(range(NTA))
    order_b = [1, 0] + list(range(NTA - 1, 1, -1))

    def scan(units, p, yacc, es_):
        kb.es = es_
        lanes = []
        for ui, u in enumerate(units):
            for d in range(2):
                L = dict(u=u, d=d, id='%d_%d' % (ui, d))
                L['S'] = kb.sb([128, p], F32, "S" + L['id'])
                L['Sb'] = kb.sb([128, p], BF16, "Sb" + L['id'])
                L['labc'] = kb.sb([128, 128], F32, "labc" + L['id'])
                L['Dm'] = kb.sb([128, 128], F32, "Dm" + L['id'])
                L['PT'] = kb.sb([128, 128], BF16, "PT" + L['id'])
                L['Vb'] = kb.sb([128, p], BF16, "Vb" + L['id'])
                L['Vw'] = kb.sb([128, p], BF16, "Vw" + L['id'])
                L['yt'] = kb.sb([128, p], F32, "yt" + L['id'])
                if u['gdn']:
                    for nm in ('Es', 'N0', 'N1', 'A0', 'A1', 'P', 'r'):
                        L[nm] = kb.sb([128, 128], F32, nm + L['id'])
                V_(lambda E: E.memset(L['S'][:, :], 0.0), [], ['S' + L['id']])
                V_(lambda E: E.memset(L['Sb'][:, :], 0.0), [], ['Sb' + L['id']])
                lanes.append(L)
        pA = kb.ps([128, 512], F32, "pA")
        pKQ = kb.ps([128, 512], F32, "pKQ")
        pY = kb.ps([128, 512], F32, "pYs")
        pO = kb.ps([128, 512], F32, "pO")
        pS_ = kb.ps([128, 512], F32, "pSd")
        pG = [kb.ps([128, 512], F32, "pG%d" % i) for i in range(3)] if units[0]['gdn'] else None
        for step in range(NTA):
            for L in lanes:
                u, d, lid = L['u'], L['d'], L['id']
                t = (order_f if d == 0 else order_b)[step]
                col = u['col'](d)
                QT = u['QT'][:, t * 128:(t + 1) * 128]
                KT = u['KT'][:, t * 128:(t + 1) * 128]
                Ktok = u['Ktok'][:, t, :]
                k = lambda nm: nm + lid
                V_(lambda E: E.tensor_scalar(out=L['labc'][:, :], in0=ones_f[:, :], scalar1=la[:, t, col:col + 1], scalar2=None, op0=ALU.mult), ['ones', 'la'], [k('labc')])
                P_(lambda E: E.matmul(pA[:, 0:128], lhsT=L['labc'][:, :], rhs=tri[:, d, :], start=True, stop=True), [k('labc'), 'tri'], ['pA'])
                V_(lambda E: E.scalar_tensor_tensor(out=L['Dm'][:, :], in0=pA[:, 0:128], scalar=acs[:, t, col:col + 1], in1=tri[:, 2 + d, :],
                                                    op0=ALU.subtract, op1=ALU.add), ['pA', 'acs', 'tri'], [k('Dm')])
                A_(lambda E: E.activation(out=L['Dm'][:, :], in_=L['Dm'][:, :], func=AF.Exp), [k('Dm')], [k('Dm')])
                P_(lambda E: E.matmul(pKQ[:, 0:128], lhsT=KT, rhs=QT, start=True, stop=True), [u['KTk'], u['QTk']], ['pKQ'])
                V_(lambda E: E.tensor_tensor(out=L['PT'][:, :], in0=pKQ[:, 0:128], in1=L['Dm'][:, :], op=ALU.mult), ['pKQ', k('Dm')], [k('PT')])
                if u['gdn']:
                    bc = u['bcol'](d)
                    G_(lambda E: E.tensor_tensor(out=L['Es'][:, :], in0=L['Dm'][:, :], in1=ident_f[:, :], op=ALU.subtract), [k('Dm'), 'ident_f'], [k('Es')])
                    P_(lambda E: E.matmul(pG[0][:, 0:128], lhsT=KT, rhs=KT, start=True, stop=True), [u['KTk']], ['pG0'])
                    N, A, Nn, An = L['N0'], L['A0'], L['N1'], L['A1']
                    nk = [k('N0'), k('A0'), k('N1'), k('A1')]
                    V_(lambda E: E.scalar_tensor_tensor(out=N[:, :], in0=pG[0][:, 0:128], scalar=beta[:, t, bc:bc + 1], in1=L['Es'][:, :],
                                                        op0=ALU.mult, op1=ALU.mult), ['pG0', 'beta', k('Es')], [nk[0]])
                    P_(lambda E: E.transpose(out=pG[1][:, 0:128], in_=N[:, :], identity=ident_f[:, :]), [nk[0], 'ident_f'], ['pG1'])
                    A_(lambda E: E.copy(out=A[:, :], in_=pG[1][:, 0:128]), ['pG1'], [nk[1]])
                    G_(lambda E: E.tensor_tensor(out=L['P'][:, :], in0=ident_f[:, :], in1=N[:, :], op=ALU.subtract), [nk[0], 'ident_f'], [k('P')])
                    for it in range(1, 7):
                        P_(lambda E: E.matmul(pG[1][:, 0:128], lhsT=N[:, :], rhs=A[:, :], start=True, stop=True), [nk[0], nk[1]], ['pG1'])
                        A_(lambda E: E.copy(out=An[:, :], in_=pG[1][:, 0:128]), ['pG1'], [nk[3]])
                        if it < 6:
                            P_(lambda E: E.matmul(pG[0][:, 0:128], lhsT=A[:, :], rhs=N[:, :], start=True, stop=True), [nk[0], nk[1]], ['pG0'])
                            V_(lambda E: E.tensor_copy(out=Nn[:, :], in_=pG[0][:, 0:128]), ['pG0'], [nk[2]])
                        P_(lambda E: E.matmul(pG[2][:, 0:128], lhsT=An[:, :], rhs=L['P'][:, :], start=True, stop=True), [nk[3], k('P')], ['pG2'])
                        V_(lambda E: E.tensor_tensor(out=L['P'][:, :], in0=L['P'][:, :], in1=pG[2][:, 0:128], op=ALU.add), ['pG2', k('P')], [k('P')])
                        N, A, Nn, An = Nn, An, N, A
                        nk = [nk[2], nk[3], nk[0], nk[1]]
                    P_(lambda E: E.matmul(pO[:, 0:p], lhsT=KT, rhs=L['Sb'][:, :], start=True, stop=True), [u['KTk'], k('Sb')], ['pO'])
                    V_(lambda E: E.scalar_tensor_tensor(out=L['r'][:, :], in0=pO[:, 0:p], scalar=neacs[:, t, col:col + 1], in1=u['vtok'][:, t, :],
                                                        op0=ALU.mult, op1=ALU.add), ['pO', 'neacs', u['vk']], [k('r')])
                    P_(lambda E: E.matmul(pS_[:, 0:p], lhsT=L['P'][:, :], rhs=L['r'][:, :], start=True, stop=True), [k('P'), k('r')], ['pSd'])
                    V_(lambda E: E.tensor_scalar(out=L['Vb'][:, :], in0=pS_[:, 0:p], scalar1=beta[:, t, bc:bc + 1], scalar2=None, op0=ALU.mult), ['pSd', 'beta'], [k('Vb')])
                else:
                    u['vfn'](d, t, L['Vb'], k('Vb'))
                P_(lambda E: E.matmul(pY[:, 0:p], lhsT=L['PT'][:, :], rhs=L['Vb'][:, :], start=True, stop=True), [k('PT'), k('Vb')], ['pY'])
                P_(lambda E: E.matmul(pO[:, 0:p], lhsT=QT, rhs=L['Sb'][:, :], start=True, stop=True), [u['QTk'], k('Sb')], ['pO'])
                ys = yacc[:, t, u['y0']:u['y0'] + p]
                V_(lambda E: E.scalar_tensor_tensor(out=L['yt'][:, :], in0=pO[:, 0:p], scalar=eacs[:, t, col:col + 1], in1=ys, op0=ALU.mult, op1=ALU.add),
                   ['pO', 'eacs', 'yacc%d' % t], [k('yt')])
                V_(lambda E: E.tensor_tensor(out=ys, in0=L['yt'][:, :], in1=pY[:, 0:p], op=ALU.add), [k('yt'), 'pY'], ['yacc%d' % t])
                V_(lambda E: E.tensor_scalar(out=L['Vw'][:, :], in0=L['Vb'][:, :], scalar1=wst[:, t, col:col + 1], scalar2=None, op0=ALU.mult), [k('Vb'), 'wst'], [k('Vw')])
                P_(lambda E: E.matmul(pS_[:, 0:p], lhsT=Ktok, rhs=L['Vw'][:, :], start=True, stop=True), [u['Ktokk'], k('Vw')], ['pSd'])
                V_(lambda E: E.scalar_tensor_tensor(out=L['S'][:, :], in0=L['S'][:, :], scalar=etot[:, t, col:col + 1], in1=pS_[:, 0:p], op0=ALU.mult, op1=ALU.add),
                   ['pSd', 'etot', k('S')], [k('S')])
                A_(lambda E: E.copy(out=L['Sb'][:, :], in_=L['S'][:, :]), [k('S')], [k('Sb')])

    def post_store(gsrc, gkey, row0, t, ob, pTo):
        for j in range(2):
            P_(lambda E: E.transpose(out=pTo[:, j, :], in_=gsrc[:, j * 128:(j + 1) * 128], identity=ident_b[:, :]), [gkey], ['pTo'])
        A_(lambda E: E.copy(out=ob[:, :, :], in_=pTo[:, :, :]), ['pTo'], ['ob'])
        for j in range(2):
            kb.dma(Ov[row0 + j * 128:row0 + (j + 1) * 128, t, :], ob[:, j, :], r=['ob'], w=['brT_d'])

    es5 = ExitStack()
    kb.es = es5
    cin = kb.sb([128, TA], F32, "cin")
    cout = kb.sb([128, TA], F32, "cout")
    cbf = kb.sb([128, TA], BF16, "cbf")
    BT = kb.sb([128, TA], BF16, "BT")
    CT = kb.sb([128, TA], BF16, "CT")
    xtok = kb.sb([128, NTA, 256], BF16, "xtok")
    Btok = kb.sb([128, NTA, 128], BF16, "Btok")
    yacc = kb.sb([128, NTA, 256], F32, "yacc")
    G_(lambda E: E.memset(yacc[:, :, :], 0.0), [], ['yacc%d' % t for t in range(NTA)])
    esc = open_conv_psum()
    for ci in range(2):
        conv_chunk(ci, cin, cout)
        A_(lambda E: E.copy(out=cbf[:, :], in_=cout[:, :]), ['cout'], ['cbf'])
        to_tok(cbf, 'cbf', xtok, 'xtok', ci * 128)
    conv_chunk(2, cin, cout)
    A_(lambda E: E.copy(out=BT[:, :], in_=cout[:, :]), ['cout'], ['BT'])
    to_tok(BT, 'BT', Btok, 'Btok', 0)
    conv_chunk(3, cin, cout)
    A_(lambda E: E.copy(out=CT[:, :], in_=cout[:, :]), ['cout'], ['CT'])

    def ssd_unit(h):
        def vfn(d, t, Vb, vkey):
            V_(lambda E: E.tensor_scalar(out=Vb[:, :], in0=xtok[:, t, h * 64:(h + 1) * 64], scalar1=sp16[:, t, d * 4 + h:d * 4 + h + 1], scalar2=None, op0=ALU.mult),
               ['xtok', 'sp16'], [vkey])
        return dict(QT=CT, QTk='CT', KT=BT, KTk='BT', Ktok=Btok, Ktokk='Btok', col=lambda d: d * 4 + h, vfn=vfn, gdn=False, y0=h * 64)
    kb.barrier()
    esc.close()
    pu_l = nc.dram_tensor("peer_u_%d" % l, [16384, D], F32, kind="ExternalInput").ap()
    pv_l = nc.dram_tensor("peer_v_%d" % l, [16384, D], F32, kind="ExternalInput").ap()
    C['pu_%d' % l], C['pv_%d' % l] = pu_l, pv_l
    for i in range(16):
        kb.dma(C['pub'][i * 1024:(i + 1) * 1024, :], pu_l[i * 1024:(i + 1) * 1024, :], w=['pub%d' % i], eng='pool')
        kb.dma(C['pvb'][i * 1024:(i + 1) * 1024, :], pv_l[i * 1024:(i + 1) * 1024, :], w=['pvb%d' % i], eng='pool')
    es5s = ExitStack()
    scan([ssd_unit(h) for h in range(4)], 64, yacc, es5s)
    kb.barrier()
    es5s.close()
    kb.es = es5
    zt = kb.sb([128, 256], F32, "zt")
    g1 = kb.sb([128, 256], F32, "g1")
    g2 = kb.sb([128, 256], F32, "g2")
    gb = kb.sb([128, 256], BF16, "gbf")
    ss = kb.sb([128, 2], F32, "ss")
    ob = kb.sb([128, 2, 128], BF16, "ob")
    pTo = kb.ps([128, 2, 128], BF16, "pTo")
    for t in range(NTA):
        kb.dma(zt[:, :], tm_d[t * 128:(t + 1) * 128, 256:512], r=['tm_d'], w=['zt'])
        A_(lambda E: E.activation(out=zt[:, :], in_=zt[:, :], func=AF.Silu), ['zt'], ['zt'])
        V_(lambda E: E.tensor_tensor(out=g1[:, :], in0=xtok[:, t, :], in1=rowp[:, 0, :], op=ALU.mult), ['xtok', 'rowp'], ['g1'])
        V_(lambda E: E.tensor_tensor(out=g1[:, :], in0=g1[:, :], in1=yacc[:, t, :], op=ALU.add), ['g1', 'yacc%d' % t], ['g1'])
        V_(lambda E: E.tensor_tensor(out=g1[:, :], in0=g1[:, :], in1=zt[:, :], op=ALU.mult), ['g1', 'zt'], ['g1'])
        V_(lambda E: E.scalar_tensor_tensor(out=g2[:, :], in0=g1[:, :], scalar=1.0 / 256.0, in1=g1[:, :], op0=ALU.mult, op1=ALU.mult, accum_out=ss[:, 0:1]), ['g1'], ['g2', 'ss'])
        V_(lambda E: E.tensor_scalar(out=ss[:, 0:1], in0=ss[:, 0:1], scalar1=EPS, scalar2=None, op0=ALU.add), ['ss'], ['ss'])
        A_(lambda E: E.activation(out=ss[:, 0:1], in_=ss[:, 0:1], func=AF.Sqrt), ['ss'], ['ss'])
        V_(lambda E: E.reciprocal(out=ss[:, 0:1], in_=ss[:, 0:1]), ['ss'], ['ss'])
        V_(lambda E: E.scalar_tensor_tensor(out=gb[:, :], in0=g1[:, :], scalar=ss[:, 0:1], in1=rowp[:, 1, :], op0=ALU.mult, op1=ALU.mult), ['g1', 'ss', 'rowp'], ['gb'])
        post_store(gb, 'gb', 768, t, ob, pTo)
    kb.barrier()
    es5.close()

    es6 = ExitStack()
    kb.es = es6
    cin = kb.sb([128, TA], F32, "cin6")
    cout = kb.sb([128, TA], F32, "cout6")
    cbf = kb.sb([128, TA], BF16, "cbf6")
    qT = [kb.sb([128, TA], BF16, "qT%d" % h) for h in range(2)]
    kT = [kb.sb([128, TA], BF16, "kT%d" % h) for h in range(2)]
    ktok = [kb.sb([128, NTA, 128], BF16, "ktok%d" % h) for h in range(2)]
    vtok = [kb.sb([128, NTA, 128], BF16, "vtok%d" % h) for h in range(2)]
    oacc = kb.sb([128, NTA, 256], F32, "oacc")
    G_(lambda E: E.memset(oacc[:, :, :], 0.0), [], ['yacc%d' % t for t in range(NTA)])
    esc = open_conv_psum()
    for h in range(2):
        conv_chunk(4 + h, cin, cout)
        l2norm_to(cout, cin, qT[h][:, :], 'qT%d' % h, 128 ** -0.5)
        conv_chunk(6 + h, cin, cout)
        l2norm_to(cout, cin, kT[h][:, :], 'kT%d' % h, 1.0)
        to_tok(kT[h], 'kT%d' % h, ktok[h], 'ktok%d' % h, 0)
        conv_chunk(8 + h, cin, cout)
        A_(lambda E: E.copy(out=cbf[:, :], in_=cout[:, :]), ['cout'], ['cbf'])
        to_tok(cbf, 'cbf', vtok[h], 'vtok%d' % h, 0)

    def gdn_unit(h):
        return dict(QT=qT[h], QTk='qT%d' % h, KT=kT[h], KTk='kT%d' % h, Ktok=ktok[h], Ktokk='ktok%d' % h, col=lambda d: 8 + d * 2 + h,
                    bcol=lambda d: d * 2 + h, vtok=vtok[h], vk='vtok%d' % h, gdn=True, y0=h * 128)
    kb.barrier()
    esc.close()
    es6s = ExitStack()
    scan([gdn_unit(h) for h in range(2)], 128, oacc, es6s)
    kb.barrier()
    es6s.close()
    kb.es = es6
    zt = kb.sb([128, 256], F32, "zt6")
    g1 = kb.sb([128, 256], F32, "g16")
    g2 = kb.sb([128, 256], F32, "g26")
    gb = kb.sb([128, 256], BF16, "gbf6")
    ss = kb.sb([128, 2], F32, "ss6")
    ob = kb.sb([128, 2, 128], BF16, "ob6")
    pTo = kb.ps([128, 2, 128], BF16, "pTo6")
    for t in range(NTA):
        kb.dma(zt[:, :], tm_d[t * 128:(t + 1) * 128, 512:768], r=['tm_d'], w=['zt'])
        A_(lambda E: E.activation(out=zt[:, :], in_=zt[:, :], func=AF.Silu), ['zt'], ['zt'])
        for h in range(2):
            hs = slice(h * 128, (h + 1) * 128)
            V_(lambda E: E.scalar_tensor_tensor(out=g2[:, hs], in0=oacc[:, t, hs], scalar=1.0 / 128.0, in1=oacc[:, t, hs], op0=ALU.mult, op1=ALU.mult,
                                                accum_out=ss[:, h:h + 1]), ['yacc%d' % t], ['g2', 'ss'])
        V_(lambda E: E.tensor_scalar(out=ss[:, :], in0=ss[:, :], scalar1=EPS, scalar2=None, op0=ALU.add), ['ss'], ['ss'])
        A_(lambda E: E.activation(out=ss[:, :], in_=ss[:, :], func=AF.Sqrt), ['ss'], ['ss'])
        V_(lambda E: E.reciprocal(out=ss[:, :], in_=ss[:, :]), ['ss'], ['ss'])
        for h in range(2):
            hs = slice(h * 128, (h + 1) * 128)
            V_(lambda E: E.scalar_tensor_tensor(out=g1[:, hs], in0=oacc[:, t, hs], scalar=ss[:, h:h + 1], in1=rowp[:, 2, hs], op0=ALU.mult, op1=ALU.mult),
               ['yacc%d' % t, 'ss', 'rowp'], ['g1'])
        V_(lambda E: E.tensor_tensor(out=gb[:, :], in0=g1[:, :], in1=zt[:, :], op=ALU.mult), ['g1', 'zt'], ['gb'])
        post_store(gb, 'gb', 1024, t, ob, pTo)
    kb.barrier()
    es6.close()
    kb.barrier()
    es.close()


def _cnt(n, win):
    lo = win // 2
    hi = win - 1 - lo
    t = np.arange(n)
    return (np.minimum(t + hi, n - 1) - np.maximum(t - lo, 0) + 1).astype(np.float64)


def consts_A():
    c = consts()
    if 'dft_x' not in c:
        import ml_dtypes
        bf = ml_dtypes.bfloat16
        t = np.arange(4096, dtype=np.int64)
        m = (t[:, None] * t[None, :]) % 4096
        ang = 2.0 * np.pi * m.astype(np.float64) / 4096.0
        c['dft_x'] = np.stack([(np.cos(ang) / 64.0).astype(np.float32).astype(bf), (np.sin(ang) / 64.0).astype(np.float32).astype(bf)], 0)
        t = np.arange(256, dtype=np.int64)
        ang = 2.0 * np.pi * ((t[:, None] * t[None, :]) % 256).astype(np.float64) / 256.0
        c['dft_c'] = np.stack([(np.cos(ang) / 16.0).astype(np.float32).astype(bf), (np.sin(ang) / 16.0).astype(np.float32).astype(bf)], 0)
        t = np.arange(128, dtype=np.int64)
        ang = 2.0 * np.pi * ((t[:, None] * t[None, :]) % 128).astype(np.float64) / 128.0
        c['dft_ch'] = np.stack([np.cos(ang) / np.sqrt(128.0), -np.sin(ang) / np.sqrt(128.0)], 0).astype(np.float32)
        j = np.arange(128)[:, None]
        i = np.arange(128)[None, :]
        c['tri'] = np.stack([(j <= i).astype(np.float32), (j >= i).astype(np.float32),
                             np.where(i >= j, 0.0, NEG).astype(np.float32), np.where(i <= j, 0.0, NEG).astype(np.float32)], 0)
        c['invcnt'] = np.stack([(1.0 / (_cnt(64, w)[:, None] * _cnt(64, w)[None, :])).reshape(-1) for w in POOL_WINDOWS], 0).astype(np.float32)
        c['invcnt_c'] = np.stack([1.0 / _cnt(256, w) for w in POOL_WINDOWS], 0).astype(np.float32)
    return c


def layer_weights_A(inp, l, s):
    w_in = inp['w_in'][l]
    cat = np.concatenate
    fm_cols = cat([np.arange(0, 512), 1024 + s * 256 + np.arange(256), 1536 + s * 128 + np.arange(128), 1792 + s * 128 + np.arange(128),
                   2576 + s * 256 + np.arange(256), 3088 + s * 256 + np.arange(256), 3600 + s * 256 + np.arange(256)])
    tm_cols = cat([512 + s * 256 + np.arange(256), 2048 + s * 256 + np.arange(256), 4112 + s * 256 + np.arange(256),
                   2560 + 4 * s + np.arange(4), 2568 + 4 * s + np.arange(4), 4624 + 2 * s + np.arange(2), 4628 + 2 * s + np.arange(2),
                   4632 + 2 * s + np.arange(2), 4636 + 2 * s + np.arange(2)])
    scw, scb, gcw = inp['ssd_conv_w'][l], inp['ssd_conv_b'][l], inp['gdn_conv_w'][l]
    ch_ssd = cat([s * 256 + np.arange(256), 512 + s * 128 + np.arange(128), 768 + s * 128 + np.arange(128)])
    ch_gdn = cat([s * 256 + np.arange(256), 512 + s * 256 + np.arange(256), 1024 + s * 256 + np.arange(256)])
    cw = np.zeros((10, 128, 6), np.float32)
    cw[0:4, :, 0:5] = scw[:, ch_ssd].T.reshape(4, 128, 5)
    cw[0:4, :, 5] = scb[ch_ssd].reshape(4, 128)
    cw[4:10, :, 0:5] = gcw[:, ch_gdn].T.reshape(6, 128, 5)
    rowp = np.zeros((5, 256), np.float32)
    rowp[0, 0:4] = inp['ssd_dt_bias'][l][0, 4 * s:4 * s + 4]
    rowp[0, 4:8] = inp['ssd_dt_bias'][l][1, 4 * s:4 * s + 4]
    rowp[0, 12:14] = inp['gdn_dt_bias'][l][0, 2 * s:2 * s + 2]
    rowp[0, 14:16] = inp['gdn_dt_bias'][l][1, 2 * s:2 * s + 2]
    rowp[0, 16:20] = inp['ssd_a_log'][l][0, 4 * s:4 * s + 4]
    rowp[0, 20:24] = inp['ssd_a_log'][l][1, 4 * s:4 * s + 4]
    rowp[0, 28:30] = inp['gdn_a_log'][l][0, 2 * s:2 * s + 2]
    rowp[0, 30:32] = inp['gdn_a_log'][l][1, 2 * s:2 * s + 2]
    rowp[1] = np.repeat(inp['ssd_d'][l][4 * s:4 * s + 4], 64)
    rowp[2] = inp['ssd_norm_w'][l][s * 256:(s + 1) * 256]
    rowp[3] = np.tile(inp['gdn_norm_w'][l], 2)
    c = consts_A()
    return {
        'w_mod': np.ascontiguousarray(inp['w_mod'][l][:, :2 * D]), 'b_mod': np.ascontiguousarray(inp['b_mod'][l][:2 * D]),
        'wa_fm': np.ascontiguousarray(w_in[:, fm_cols]), 'wa_tm': np.ascontiguousarray(w_in[:, tm_cols]),
        'convw': np.ascontiguousarray(cw.transpose(1, 0, 2)), 'pool_w': inp['pool_w'][l],
        'pool_scale': np.ascontiguousarray(inp['pool_scale'][l].reshape(4, 128).T), 'invcnt': c['invcnt'], 'invcnt_c': c['invcnt_c'],
        'rowp': rowp, 'dft_x': c['dft_x'], 'dft_c': c['dft_c'], 'dft_ch': c['dft_ch'], 'tri': c['tri'], 'ident': c['ident'],
    }


RG_PAIRS = [[0, 1], [2, 3], [4, 5], [6, 7]]
DEPTH = 4
O_CH = [(i * 6400, min((i + 1) * 6400, NTA * 1280)) for i in range(7)]
X_CH = [(i * 512, min((i + 1) * 512, NTOK)) for i in range(5)]


def _gathered_row(chunks, rank, j):
    for (a, b) in chunks:
        if a <= j < b:
            return 2 * a + rank * (b - a) + (j - a)
    raise ValueError(j)


def build_fused(depth=DEPTH):
    es = ExitStack()
    nc = bass.Bass("TRN2", target_bir_lowering=False)
    kb = KB(nc, es)
    kb.es_top = es
    di = lambda n, s, d=F32: nc.dram_tensor(n, list(s), d, kind="ExternalInput").ap()
    dint = lambda n, s, d=F32: nc.dram_tensor(n, list(s), d, addr_space="Local", kind="Internal").ap()
    C = {
        'cvec': di("cvec", [2, 128, 8]), 'ident': di("ident", [128, 128]), 'iota16': di("iota16", [128, 16]),
        'idxB': di("idxB", [128, 17, 16], U32), 'invcnt': di("invcnt", [4, 4096]), 'invcnt_c': di("invcnt_c", [4, 256]),
        'dft_x': di("dft_x", [2, 4096, 4096], BF16), 'dft_c': di("dft_c", [2, 256, 256], BF16), 'dft_ch': di("dft_ch", [2, 128, 128]),
        'tri': di("tri", [4, 128, 128]),
        'pub': dint("pub_scr", [16384, D], BF16), 'pvb': dint("pvb_scr", [16384, D], BF16),
        'x1': dint("x1_scr", [NTOK, D]), 'tm': dint("tm_scr", [TA, 784]), 'fm': dint("fm_scr", [1792, TA]),
    }
    for l in range(depth):
        C['w_mod_%d' % l] = di("w_mod_%d" % l, [D, 6 * D])
        C['b_mod_%d' % l] = di("b_mod_%d" % l, [6 * D])
    xa0 = di("xa0", [TA, D])
    xmy0 = di("xmy0", [NTOK, D])
    out_d = nc.dram_tensor("out", [NTOK, D], F32, kind="ExternalOutput").ap()
    O_d = dint("O_loc", [NTA * 1280, 128], BF16)
    G_d = dint("G_all", [2 * NTA * 1280, 128], BF16)
    xg = [xa0] + [dint("xg_%d" % l, [TA, D]) for l in range(1, depth)]
    xmy = [xmy0] + [dint("xmy_%d" % l, [NTOK, D]) for l in range(1, depth)]
    for l in range(depth):
        emit_A(nc, kb, l, C, xg[l], O_d)
        kb.collective("AllGather", [O_d[a:b, :] for (a, b) in O_CH], [G_d[2 * a:2 * b, :] for (a, b) in O_CH])
        o = out_d if l == depth - 1 else xmy[l + 1]
        emit_B(nc, kb, l, C, xmy[l], G_d, o)
        if l < depth - 1:
            kb.collective("AllGather", [xmy[l + 1][a:b, :] for (a, b) in X_CH], [xg[l + 1][2 * a:2 * b, :] for (a, b) in X_CH])
    kb.finish()
    return nc, es


def idx_table(s):
    idx = np.zeros((128, 17, 16), np.uint32)
    p = np.arange(128)
    for tau in range(17):
        T = s if tau == 0 else 2 + 16 * s + (tau - 1)
        for kc in range(16):
            br, j = kc // 4, kc % 4
            if br == 0:
                rs, row0 = 0, j * 128
            else:
                rs, row0 = j // 2, 512 + (br - 1) * 256 + (j % 2) * 128
            idx[:, tau, kc] = _gathered_row(O_CH, rs, T * 1280 + row0) + p
    return idx


_PROG = {}


def kernel(**inp):
    inp = {k: np.asarray(v) for k, v in inp.items()}
    depth = inp['w_in'].shape[0]
    if 'nc' not in _PROG:
        _PROG['nc'] = build_fused(depth)
    nc, _ = _PROG['nc']
    cA = consts_A()
    shared = {'ident': cA['ident'], 'iota16': cA['iota16'], 'invcnt': cA['invcnt'], 'invcnt_c': cA['invcnt_c'],
              'dft_x': cA['dft_x'], 'dft_c': cA['dft_c'], 'dft_ch': cA['dft_ch'], 'tri': cA['tri']}
    skipA = ('w_mod', 'b_mod', 'invcnt', 'invcnt_c', 'dft_x', 'dft_c', 'dft_ch', 'tri', 'ident')
    skipB = ('ident', 'iota16')
    per_half = []
    for s in range(2):
        m = {}
        for l in range(depth):
            for k, v in layer_weights_A(inp, l, s).items():
                if k not in skipA:
                    m['%s_%d' % (k, l)] = v
        m['idxB'] = idx_table(s)
        per_half.append(m)
    common = dict(shared)
    for l in range(depth):
        for k, v in layer_weights_B(inp, l).items():
            if k not in skipB:
                common['%s_%d' % (k, l)] = v
    x, ctx = inp['x'], inp['ctx']
    in_maps = []
    for core in range(8):
        b, s = core // 2, core % 2
        halves = [np.concatenate([ctx[b, r * 128:(r + 1) * 128], x[b, r * 2048:(r + 1) * 2048]], 0) for r in range(2)]
        m = dict(common)
        m.update(per_half[s])
        m['xa0'] = np.ascontiguousarray(np.concatenate([ctx[b], x[b]], 0))
        m['xmy0'] = np.ascontiguousarray(halves[s])
        m['cvec'] = cvec_layout(inp['c_ctx'], inp['c'][b])
        in_maps.append(m)
    res = run_bass_kernel_spmd(nc, in_maps, core_ids=list(range(8))).results
    out = np.empty_like(x)
    for core in range(8):
        b, s = core // 2, core % 2
        out[b, s * 2048:(s + 1) * 2048] = np.asarray(res[core]['out'])[128:]
    return out
```
